# Optimizing a Trainium2 kernel written in Bass

```python
import jax, jax.numpy as jnp
from jax import lax
import numpy as np

D_MODEL = 1024
BATCH = 2
SEQ = 8192
DEPTH = 2
DEC_BATCH = 128
DEC_SEQ = 4
PAST_LEN = 2048
PAGE_SIZE = 128

HEAD_DIM = 64
N_A_LAYERS = (DEPTH + 1) // 2
N_C_LAYERS = DEPTH // 2
GLA_HEADS = 4
GLA_DK = 64
GLA_DV = 128
GLA_LOWRANK = 16
GLA_TAU = 16.0
GLA_CHUNK = 64
NSA_HEADS = 8
NSA_KV_HEADS = 2
NSA_GROUP = NSA_HEADS // NSA_KV_HEADS
CMP_LEN = 32
CMP_STRIDE = 16
SLC_BLOCK = 64
SLC_TOPK = 16
NSA_WINDOW = 512
DIL_PAIRS = ((128, 1), (512, 4), (2048, 16))
N_DIL = 3
DIL_HEADS = 8
D_FF = 4 * D_MODEL
PLE_DIM = 256
ROPE_THETA = 10000.0
EPS = 1e-6
Q_BLOCK = 128
NEG = -1e30
BIG = 1e30
TINY = 1e-20

A_SIZES = (GLA_HEADS * GLA_DK, GLA_HEADS * GLA_DK, GLA_HEADS * GLA_DV, GLA_HEADS * GLA_DV, GLA_LOWRANK,
           NSA_HEADS * HEAD_DIM, 6 * NSA_KV_HEADS * HEAD_DIM, 3 * NSA_HEADS)
A_IN = sum(A_SIZES)
A_SPLIT = [sum(A_SIZES[:i + 1]) for i in range(len(A_SIZES) - 1)]
A_OUT = GLA_HEADS * GLA_DV + NSA_HEADS * HEAD_DIM
C_IN = N_DIL * 3 * DIL_HEADS * HEAD_DIM
C_OUT = DIL_HEADS * HEAD_DIM

kernel_name = 'hybrid_gla_nsa_dilated_decode_step'


def rmsnorm(x, g):
    xf = x.astype(jnp.float32)
    y = xf * lax.rsqrt(jnp.mean(xf * xf, axis=-1, keepdims=True) + EPS)
    return (y * g.astype(jnp.float32)).astype(x.dtype)


def rope(x, pos):
    half = x.shape[-1] // 2
    freq = ROPE_THETA ** (-jnp.arange(half, dtype=jnp.float32) / half)
    ang = pos.astype(jnp.float32)[:, None] * freq[None, :]
    cos, sin = jnp.cos(ang)[:, None, :], jnp.sin(ang)[:, None, :]
    xf = x.astype(jnp.float32)
    x1, x2 = xf[..., :half], xf[..., half:]
    return jnp.concatenate([x1 * cos - x2 * sin, x2 * cos + x1 * sin], axis=-1).astype(x.dtype)


def masked_softmax_stats(s, mask):
    s = jnp.where(mask, s, NEG)
    m = jnp.max(s, axis=-1, keepdims=True)
    p = jnp.where(mask, jnp.exp(s - m), 0.0)
    return p, m, jnp.sum(p, axis=-1, keepdims=True)


def pad_rows(x, mult):
    pad = (-x.shape[1]) % mult
    return jnp.pad(x, [(0, 0), (0, pad)] + [(0, 0)] * (x.ndim - 2))


def gather_pages(pool, page_table):
    g = pool[page_table]
    return g.reshape((g.shape[0], g.shape[1] * g.shape[2]) + pool.shape[2:])


def gla_chunk(S0, q, k, v, lg):
    b = jnp.cumsum(lg, axis=2)
    C = q.shape[2]
    causal = jnp.tril(jnp.ones((C, C), bool))[:, :, None]
    diff = b[:, :, :, None, :] - b[:, :, None, :, :]
    decay = jnp.exp(jnp.where(causal, diff, NEG))
    A = jnp.einsum('bhtd,bhsd,bhtsd->bhts', q, k, decay)
    o = jnp.einsum('bhts,bhsv->bhtv', A, v) + jnp.einsum('bhtd,bhdv->bhtv', q * jnp.exp(b), S0)
    b_last = b[:, :, -1:, :]
    S = jnp.exp(b_last[:, :, 0, :])[..., None] * S0 + jnp.einsum('bhsd,bhsv->bhdv', k * jnp.exp(b_last - b), v)
    return S, o


def gla_mixer(gq, gk, gv, gr, ga, w_a2, b_a, g_norm, S0, chunk):
    B, T, _ = gq.shape
    f32 = jnp.float32

    def heads(t, d):
        return t.reshape(B, T, GLA_HEADS, d).transpose(0, 2, 1, 3).astype(f32)
    q = heads(gq, GLA_DK) * GLA_DK ** -0.5
    k = heads(gk, GLA_DK)
    v = heads(gv, GLA_DV)
    lg = heads(jax.nn.log_sigmoid((ga @ w_a2 + b_a).astype(f32)) / GLA_TAU, GLA_DK)
    n = T // chunk

    def split(t):
        return jnp.moveaxis(t.reshape(B, GLA_HEADS, n, chunk, t.shape[-1]), 2, 0)

    def step(S, inp):
        return gla_chunk(S, *inp)
    S, o = lax.scan(step, S0.astype(f32), (split(q), split(k), split(v), split(lg)))
    o = jnp.moveaxis(o, 0, 2).reshape(B, GLA_HEADS, T, GLA_DV).transpose(0, 2, 1, 3)
    o = rmsnorm(o, g_norm) * jax.nn.silu(gr.reshape(B, T, GLA_HEADS, GLA_DV).astype(f32))
    return o.reshape(B, T, GLA_HEADS * GLA_DV).astype(gq.dtype), S


def nsa_compress(rows, w_phi, pe):
    B, L, G, D = rows.shape
    nchunk = L // CMP_STRIDE
    c = rows[:, :nchunk * CMP_STRIDE].reshape(B, nchunk, CMP_STRIDE, G, D)
    blocks = jnp.concatenate([c[:, :-1], c[:, 1:]], axis=2) + pe[None, None, :, None, :]
    return jnp.einsum('bnpgd,pde->bnge', blocks, w_phi)


def cmp_to_slc(n_cmp, n_slc):
    cs = jnp.arange(n_cmp)[:, None] * CMP_STRIDE
    ss = jnp.arange(n_slc)[None, :] * SLC_BLOCK
    ov = jnp.maximum(jnp.minimum(cs + CMP_LEN, ss + SLC_BLOCK) - jnp.maximum(cs, ss), 0)
    return ov.astype(jnp.float32) / CMP_LEN


def nsa_attend(q, q_pos, gates, kc, vc, ks, vs, kw, vw, kw_pos):
    B, Tq, H, D = q.shape
    G, J = NSA_KV_HEADS, NSA_GROUP
    f32 = jnp.float32
    scale = D ** -0.5
    q_c = q.astype(f32).reshape(B, Tq, G, J, D)
    q_r = rope(q, q_pos).astype(f32).reshape(B, Tq, G, J, D)
    n_cmp = kc.shape[1]
    c_end = jnp.arange(n_cmp) * CMP_STRIDE + CMP_LEN - 1
    cmask = (c_end[None, :] <= q_pos[:, None])[None, :, None, None, :]
    p, _, l = masked_softmax_stats(jnp.einsum('btgjd,bngd->btgjn', q_c, kc) * scale, cmask)
    p = p / jnp.maximum(l, TINY)
    o_cmp = jnp.einsum('btgjn,bngd->btgjd', p, vc)
    n_slc = ks.shape[1] // SLC_BLOCK
    imp = jnp.einsum('btgjn,ns->btgs', p, cmp_to_slc(n_cmp, n_slc))
    blk = jnp.arange(n_slc)[None, :]
    cur = (q_pos // SLC_BLOCK)[:, None]
    forced = ((blk == cur) | (blk == 0))[None, :, None, :]
    valid = (blk <= cur)[None, :, None, :]
    imp = jnp.where(forced, BIG, jnp.where(valid, imp, NEG))
    _, idx = lax.top_k(imp, min(SLC_TOPK, n_slc))
    idx = idx.transpose(0, 2, 1, 3)
    take = jax.vmap(jax.vmap(lambda a, i: a[i]))

    def gather(x):
        xb = x.reshape(B, n_slc, SLC_BLOCK, G, D).transpose(0, 3, 1, 2, 4)
        return take(xb, idx).reshape(B, G, Tq, -1, D)
    k_sel, v_sel = gather(ks), gather(vs)
    sel_pos = (idx[..., None] * SLC_BLOCK + jnp.arange(SLC_BLOCK)).reshape(B, G, Tq, 1, -1)
    smask = sel_pos <= q_pos[None, None, :, None, None]
    p, _, l = masked_softmax_stats(jnp.einsum('bgtjd,bgtkd->bgtjk', q_r.transpose(0, 2, 1, 3, 4), k_sel) * scale, smask)
    o_slc = (jnp.einsum('bgtjk,bgtkd->bgtjd', p, v_sel) / jnp.maximum(l, TINY)).transpose(0, 2, 1, 3, 4)
    wmask = ((kw_pos[None, :] <= q_pos[:, None]) & (kw_pos[None, :] > q_pos[:, None] - NSA_WINDOW)
             & (kw_pos[None, :] >= 0))[None, :, None, None, :]
    p, _, l = masked_softmax_stats(jnp.einsum('btgjd,bsgd->btgjs', q_r, kw) * scale, wmask)
    o_win = jnp.einsum('btgjs,bsgd->btgjd', p, vw) / jnp.maximum(l, TINY)
    g = gates.reshape(B, Tq, G, J, 3, 1)
    o = g[..., 0, :] * o_cmp + g[..., 1, :] * o_slc + g[..., 2, :] * o_win
    return o.reshape(B, Tq, H * D)


def a_project(hn, pos, w_in, g_qk):
    B, T, _ = hn.shape
    gq, gk, gv, gr, ga, nq, nkv, ng = jnp.split(hn @ w_in, A_SPLIT, axis=-1)
    nq = rmsnorm(nq.reshape(B, T, NSA_HEADS, HEAD_DIM), g_qk[0])
    nkv = nkv.reshape(B, T, 6, NSA_KV_HEADS, HEAD_DIM)
    kc = rmsnorm(nkv[:, :, 0], g_qk[1])
    ks = rope(rmsnorm(nkv[:, :, 2], g_qk[2]), pos)
    kw = rope(rmsnorm(nkv[:, :, 4], g_qk[3]), pos)
    rows_c = jnp.stack([kc, nkv[:, :, 1]], axis=2)
    rows_s = jnp.stack([ks, nkv[:, :, 3]], axis=2)
    rows_w = jnp.stack([kw, nkv[:, :, 5]], axis=2)
    gates = jax.nn.sigmoid(ng.astype(jnp.float32)).reshape(B, T, NSA_HEADS, 3)
    return (gq, gk, gv, gr, ga), nq, gates, rows_c, rows_s, rows_w


def a_mixer_prompt(hn, w_in, w_out, w_a2, b_a, g_norm, g_qk, w_phi, pe):
    B, T, _ = hn.shape
    gla_in, nq, gates, rows_c, rows_s, rows_w = a_project(hn, jnp.arange(T), w_in, g_qk)
    S0 = jnp.zeros((B, GLA_HEADS, GLA_DK, GLA_DV), jnp.float32)
    o_gla, S = gla_mixer(*gla_in, w_a2, b_a, g_norm, S0, min(GLA_CHUNK, T))
    kc = nsa_compress(rows_c[:, :, 0], w_phi[0], pe[0])
    vc = nsa_compress(rows_c[:, :, 1], w_phi[1], pe[1])
    rs = pad_rows(rows_s, SLC_BLOCK)
    rw = jnp.pad(rows_w, [(0, 0), (NSA_WINDOW, 0), (0, 0), (0, 0), (0, 0)])
    qb = min(Q_BLOCK, T)
    nb = T // qb

    def blocks(x):
        return jnp.moveaxis(x.reshape((B, nb, qb) + x.shape[2:]), 1, 0)

    def block(args):
        qi, gi, t0 = args
        qpos = t0 + jnp.arange(qb)
        band = lax.dynamic_slice_in_dim(rw, t0, NSA_WINDOW + qb, axis=1)
        bpos = t0 - NSA_WINDOW + jnp.arange(NSA_WINDOW + qb)
        return nsa_attend(qi, qpos, gi, kc, vc, rs[:, :, 0], rs[:, :, 1], band[:, :, 0], band[:, :, 1], bpos)
    o_nsa = lax.map(block, (blocks(nq), blocks(gates), jnp.arange(nb) * qb))
    o_nsa = jnp.moveaxis(o_nsa, 0, 1).reshape(B, T, NSA_HEADS * HEAD_DIM).astype(hn.dtype)
    y = jnp.concatenate([o_gla, o_nsa], axis=-1) @ w_out
    return y, S, rows_c, rows_s, rows_w[:, T - min(NSA_WINDOW, T):]


def a_mixer_sample(hn, S0, pool_c, pool_s, win_buf, page_table, w_in, w_out, w_a2, b_a, g_norm, g_qk, w_phi, pe):
    B, T, _ = hn.shape
    past = page_table.shape[1] * PAGE_SIZE
    qpos = past + jnp.arange(T)
    gla_in, nq, gates, rows_c, rows_s, rows_w = a_project(hn, qpos, w_in, g_qk)
    o_gla, S = gla_mixer(*gla_in, w_a2, b_a, g_norm, S0, T)
    all_c = jnp.concatenate([gather_pages(pool_c, page_table), rows_c.astype(pool_c.dtype)], axis=1)
    all_s = pad_rows(jnp.concatenate([gather_pages(pool_s, page_table), rows_s.astype(pool_s.dtype)], axis=1), SLC_BLOCK)
    kc = nsa_compress(all_c[:, :, 0], w_phi[0], pe[0])
    vc = nsa_compress(all_c[:, :, 1], w_phi[1], pe[1])
    wb = win_buf.shape[1]
    all_w = jnp.concatenate([win_buf, rows_w.astype(win_buf.dtype)], axis=1)
    wpos = past - wb + jnp.arange(wb + T)
    o_nsa = nsa_attend(nq, qpos, gates, kc, vc, all_s[:, :, 0], all_s[:, :, 1], all_w[:, :, 0], all_w[:, :, 1], wpos)
    o_nsa = o_nsa.astype(hn.dtype)
    y = jnp.concatenate([o_gla, o_nsa], axis=-1) @ w_out
    return y, S, rows_c, rows_s, all_w[:, T:]


def dilated_attend(q, k, v, q_idx, dilation, n_keys):
    idx = q_idx[:, None] - dilation * jnp.arange(n_keys)[None, :]
    valid = idx >= 0
    idx = jnp.maximum(idx, 0)
    kg, vg = k[:, idx], v[:, idx]
    s = jnp.einsum('bthd,btnhd->bthn', q.astype(jnp.float32), kg) * HEAD_DIM ** -0.5
    p, m, l = masked_softmax_stats(s, valid[None, :, None, :])
    return jnp.einsum('bthn,btnhd->bthd', p, vg), m, l


def combine_by_denominator(parts):
    mx = jnp.max(jnp.stack([m for _, m, _ in parts]), axis=0)
    num = sum(jnp.exp(m - mx) * o for o, m, _ in parts)
    den = sum(jnp.exp(m - mx) * l for _, m, l in parts)
    return num / den


def c_project(hn, pos, w_in, g_qk):
    B, T, _ = hn.shape
    z = (hn @ w_in).reshape(B, T, N_DIL, 3, DIL_HEADS, HEAD_DIM)
    qs, rows = [], []
    for g in range(N_DIL):
        qs.append(rope(rmsnorm(z[:, :, g, 0], g_qk[g, 0]), pos))
        k = rope(rmsnorm(z[:, :, g, 1], g_qk[g, 1]), pos)
        rows.append(jnp.stack([k, z[:, :, g, 2]], axis=2))
    return qs, rows


def c_mixer_prompt(hn, w_in, g_qk, w_out):
    B, T, _ = hn.shape
    qs, rows = c_project(hn, jnp.arange(T), w_in, g_qk)
    qb = min(Q_BLOCK, T)
    nb = T // qb
    q_blocks = jnp.moveaxis(jnp.stack(qs).reshape(N_DIL, B, nb, qb, DIL_HEADS, HEAD_DIM), 2, 0)

    def block(args):
        qi, t0 = args
        q_idx = t0 + jnp.arange(qb)
        parts = [dilated_attend(qi[g], rows[g][:, :, 0], rows[g][:, :, 1], q_idx, d, w // d + 1)
                 for g, (w, d) in enumerate(DIL_PAIRS)]
        return combine_by_denominator(parts)
    o = lax.map(block, (q_blocks, jnp.arange(nb) * qb))
    o = jnp.moveaxis(o, 0, 1).reshape(B, T, C_OUT).astype(hn.dtype)
    bufs = [r[:, T - min(w, T):] for r, (w, _) in zip(rows, DIL_PAIRS)]
    return o @ w_out, bufs


def c_mixer_sample(hn, bufs, past, w_in, g_qk, w_out):
    B, T, _ = hn.shape
    qs, rows = c_project(hn, past + jnp.arange(T), w_in, g_qk)
    parts, new_bufs = [], []
    for g, (w, d) in enumerate(DIL_PAIRS):
        wb = bufs[g].shape[1]
        ext = jnp.concatenate([bufs[g], rows[g].astype(bufs[g].dtype)], axis=1)
        parts.append(dilated_attend(qs[g], ext[:, :, 0], ext[:, :, 1], wb + jnp.arange(T), d, w // d + 1))
        new_bufs.append(ext[:, T:])
    o = combine_by_denominator(parts).reshape(B, T, C_OUT).astype(hn.dtype)
    return o @ w_out, new_bufs


def channel_and_ple(h, p_i, g_mlp, g_ple, w1, w2, w_proj, w_gate):
    u = rmsnorm(h, g_mlp) @ w1
    h = h + jnp.square(jax.nn.relu(u)) @ w2
    gate = jax.nn.sigmoid((rmsnorm(h, g_ple) @ w_gate).astype(jnp.float32))
    return h + (gate * (p_i @ w_proj).astype(jnp.float32)).astype(h.dtype)


def setup_inputs(seed: int = 0) -> dict:
    key = jax.random.key(seed)
    keys = iter(jax.random.split(key, 40))
    f32 = jnp.float32

    def nrm(shape, scale=1.0):
        return jax.random.normal(next(keys), shape, f32) * scale

    def gain(shape):
        return 1.0 + nrm(shape, 0.05)
    n_pages = PAST_LEN // PAGE_SIZE
    n_pool = (DEC_BATCH * n_pages * 5) // 4
    page_table = jax.random.permutation(next(keys), n_pool)[:DEC_BATCH * n_pages]
    page_table = page_table.reshape(DEC_BATCH, n_pages).astype(jnp.int32)
    kvrow = (2, NSA_KV_HEADS, HEAD_DIM)
    dilrow = (2, DIL_HEADS, HEAD_DIM)
    return {
        'x_prompt': nrm((BATCH, SEQ, D_MODEL)),
        'x_sample': nrm((DEC_BATCH, DEC_SEQ, D_MODEL)),
        'state_gla': nrm((N_A_LAYERS, DEC_BATCH, GLA_HEADS, GLA_DK, GLA_DV), 0.5),
        'cache_nsa_cmp': nrm((N_A_LAYERS, n_pool, PAGE_SIZE) + kvrow),
        'cache_nsa_slc': nrm((N_A_LAYERS, n_pool, PAGE_SIZE) + kvrow),
        'cache_nsa_win': nrm((N_A_LAYERS, DEC_BATCH, min(NSA_WINDOW, PAST_LEN)) + kvrow),
        'cache_dil_0': nrm((N_C_LAYERS, DEC_BATCH, min(DIL_PAIRS[0][0], PAST_LEN)) + dilrow),
        'cache_dil_1': nrm((N_C_LAYERS, DEC_BATCH, min(DIL_PAIRS[1][0], PAST_LEN)) + dilrow),
        'cache_dil_2': nrm((N_C_LAYERS, DEC_BATCH, min(DIL_PAIRS[2][0], PAST_LEN)) + dilrow),
        'page_table': page_table,
        'p_prompt': nrm((DEPTH, BATCH, SEQ, PLE_DIM)),
        'p_sample': nrm((DEPTH, DEC_BATCH, DEC_SEQ, PLE_DIM)),
        'norm_mix': gain((DEPTH, D_MODEL)),
        'norm_mlp': gain((DEPTH, D_MODEL)),
        'norm_ple': gain((DEPTH, D_MODEL)),
        'a_w_in': nrm((N_A_LAYERS, D_MODEL, A_IN), D_MODEL ** -0.5),
        'a_w_out': nrm((N_A_LAYERS, A_OUT, D_MODEL), A_OUT ** -0.5),
        'gla_w_a2': nrm((N_A_LAYERS, GLA_LOWRANK, GLA_HEADS * GLA_DK), GLA_LOWRANK ** -0.5),
        'gla_b_a': nrm((N_A_LAYERS, GLA_HEADS * GLA_DK), 0.01),
        'gla_g_norm': gain((N_A_LAYERS, GLA_DV)),
        'nsa_g_qk': gain((N_A_LAYERS, 4, HEAD_DIM)),
        'nsa_w_phi': nrm((N_A_LAYERS, 2, CMP_LEN, HEAD_DIM, HEAD_DIM), (CMP_LEN * HEAD_DIM) ** -0.5),
        'nsa_pe': nrm((N_A_LAYERS, 2, CMP_LEN, HEAD_DIM), 0.02),
        'c_w_in': nrm((N_C_LAYERS, D_MODEL, C_IN), D_MODEL ** -0.5),
        'c_g_qk': gain((N_C_LAYERS, N_DIL, 2, HEAD_DIM)),
        'c_w_out': nrm((N_C_LAYERS, C_OUT, D_MODEL), C_OUT ** -0.5),
        'mlp_w1': nrm((DEPTH, D_MODEL, D_FF), D_MODEL ** -0.5),
        'mlp_w2': nrm((DEPTH, D_FF, D_MODEL), D_FF ** -0.5),
        'ple_w_proj': nrm((DEPTH, PLE_DIM, D_MODEL), PLE_DIM ** -0.5),
        'ple_w_gate': nrm((DEPTH, D_MODEL, D_MODEL), D_MODEL ** -0.5),
    }


def reference(x_prompt, x_sample, state_gla, cache_nsa_cmp, cache_nsa_slc, cache_nsa_win,
              cache_dil_0, cache_dil_1, cache_dil_2, page_table, p_prompt, p_sample,
              norm_mix, norm_mlp, norm_ple, a_w_in, a_w_out, gla_w_a2, gla_b_a, gla_g_norm,
              nsa_g_qk, nsa_w_phi, nsa_pe, c_w_in, c_g_qk, c_w_out, mlp_w1, mlp_w2,
              ple_w_proj, ple_w_gate):
    past = page_table.shape[1] * PAGE_SIZE
    hp, hs = x_prompt, x_sample
    gla_p, gla_s, cmp_p, cmp_s, slc_p, slc_s, win_p, win_s = [], [], [], [], [], [], [], []
    dil_p = [[] for _ in DIL_PAIRS]
    dil_s = [[] for _ in DIL_PAIRS]
    dil_caches = (cache_dil_0, cache_dil_1, cache_dil_2)
    for i in range(DEPTH):
        j = i // 2
        hn_p, hn_s = rmsnorm(hp, norm_mix[i]), rmsnorm(hs, norm_mix[i])
        if i % 2 == 0:
            wa = (a_w_in[j], a_w_out[j], gla_w_a2[j], gla_b_a[j], gla_g_norm[j], nsa_g_qk[j], nsa_w_phi[j], nsa_pe[j])
            yp, s_p, c_p, l_p, w_p = a_mixer_prompt(hn_p, *wa)
            ys, s_s, c_s, l_s, w_s = a_mixer_sample(hn_s, state_gla[j], cache_nsa_cmp[j], cache_nsa_slc[j],
                                                    cache_nsa_win[j], page_table, *wa)
            gla_p.append(s_p); gla_s.append(s_s)
            cmp_p.append(c_p); cmp_s.append(c_s)
            slc_p.append(l_p); slc_s.append(l_s)
            win_p.append(w_p); win_s.append(w_s)
        else:
            wc = (c_w_in[j], c_g_qk[j], c_w_out[j])
            yp, bp = c_mixer_prompt(hn_p, *wc)
            ys, bs = c_mixer_sample(hn_s, [c[j] for c in dil_caches], past, *wc)
            for g in range(N_DIL):
                dil_p[g].append(bp[g])
                dil_s[g].append(bs[g])
        lw = (norm_mlp[i], norm_ple[i], mlp_w1[i], mlp_w2[i], ple_w_proj[i], ple_w_gate[i])
        hp = channel_and_ple(hp + yp, p_prompt[i], *lw)
        hs = channel_and_ple(hs + ys, p_sample[i], *lw)
    return (hp, hs,
            jnp.stack(gla_p), jnp.stack(gla_s),
            jnp.stack(cmp_p), jnp.stack(cmp_s),
            jnp.stack(slc_p), jnp.stack(slc_s),
            jnp.stack(win_p), jnp.stack(win_s),
            jnp.stack(dil_p[0]), jnp.stack(dil_s[0]),
            jnp.stack(dil_p[1]), jnp.stack(dil_s[1]),
            jnp.stack(dil_p[2]), jnp.stack(dil_s[2]))
```

```python
import math
import os
import numpy as np
from contextlib import ExitStack
import concourse.bass as bass
import concourse.mybir as mybir
from concourse.bass_utils import run_bass_kernel_spmd

F32 = mybir.dt.float32
BF16 = mybir.dt.bfloat16
I32 = mybir.dt.int32
AF = mybir.ActivationFunctionType
ALU = mybir.AluOpType
AX = mybir.AxisListType

NCORES = 8
D = 1024
SEQ = 8192
SEG = 2048
NSLOT = 8192
A_IN = 2856
EPS = 1e-6
NEGB = -30000.0


class Buf:
    __slots__ = ("w", "r")

    def __init__(self):
        self.w = None
        self.r = {}


class Eng:
    def __init__(self, name, obj, sem, sid, self_sync):
        self.name, self.obj, self.sem, self.sid, self.self_sync = name, obj, sem, sid, self_sync
        self.count = 0
        self.known = {}


class Sched:
    NDS = 24

    def __init__(self, nc, es):
        self.nc = nc
        self.es = es
        self.nsem = 0
        self.E = {
            "pe": Eng("pe", nc.tensor, self.mk(), 0, False),
            "act": Eng("act", nc.scalar, self.mk(), 1, True),
            "dve": Eng("dve", nc.vector, self.mk(), 2, True),
            "pool": Eng("pool", nc.gpsimd, self.mk(), 3, True),
            "sp": Eng("sp", nc.sync, self.mk(), 4, False),
        }
        self.next_sid = 1000
        self.dsems = [self.mk() for i in range(self.NDS)]
        self.dsid = [100 + i for i in range(self.NDS)]
        self.dcnt = [0] * self.NDS
        self.retired = []
        self.dma_i = 0
        self.ninst = 0

    def mk(self):
        self.nsem += 1
        return self.es.enter_context(self.nc.semaphore(f"s{self.nsem}"))

    def _roll(self, eng):
        if eng.count >= 30000:
            eng.sem = self.mk()
            eng.sid = self.next_sid
            self.next_sid += 1
            eng.count = 0

    def _wait(self, eng, tok):
        sem, val, sid = tok
        if eng.known.get(sid, 0) >= val:
            return
        eng.obj.wait_ge(sem, val)
        eng.known[sid] = val
        self.ninst += 1

    def _deps(self, eng, reads, writes):
        for b in reads:
            if b.w is not None:
                yield b.w
        for b in writes:
            if b.w is not None:
                yield b.w
            for t in b.r.values():
                yield t

    def _record(self, tok, reads, writes):
        for b in reads:
            b.r[tok[2]] = tok
        for b in writes:
            b.w = tok
            b.r = {}

    def op(self, ename, fn, reads=(), writes=()):
        eng = self.E[ename]
        for t in list(self._deps(eng, reads, writes)):
            if t[2] == eng.sid and not eng.self_sync:
                continue
            self._wait(eng, t)
        self._roll(eng)
        ins = fn(eng.obj)
        eng.count += 1
        ins.then_inc(eng.sem, 1)
        self.ninst += 1
        tok = (eng.sem, eng.count, eng.sid)
        eng.last = tok
        self._record(tok, reads, writes)
        return tok

    def dma(self, qname, out, in_, reads=(), writes=(), fn=None, **kw):
        eng = self.E[qname]
        for t in list(self._deps(eng, reads, writes)):
            self._wait(eng, t)
        k = self.dma_i % self.NDS
        self.dma_i += 1
        if self.dcnt[k] >= 1800:
            self.retired.append((self.dsems[k], 16 * self.dcnt[k], self.dsid[k]))
            self.dsems[k] = self.mk()
            self.dsid[k] = self.next_sid
            self.next_sid += 1
            self.dcnt[k] = 0
        sem = self.dsems[k]
        sid = self.dsid[k]
        if self.dcnt[k] > 0:
            self._wait(eng, (sem, 16 * self.dcnt[k], sid))
        self.dcnt[k] += 1
        if fn is not None:
            fn(eng.obj).then_inc(sem, 16)
        else:
            eng.obj.dma_start(out=out, in_=in_, **kw).then_inc(sem, 16)
        self.ninst += 1
        tok = (sem, 16 * self.dcnt[k], sid)
        self._record(tok, reads, writes)
        return tok

    def finish(self):
        for eng in self.E.values():
            for o in self.E.values():
                if o is not eng and getattr(o, "last", None) is not None:
                    self._wait(eng, o.last)
            for k in range(self.NDS):
                if self.dcnt[k] > 0:
                    self._wait(eng, (self.dsems[k], 16 * self.dcnt[k], self.dsid[k]))
            for t in self.retired:
                self._wait(eng, t)


class T:
    def __init__(self, h):
        self.h = h
        self.b = Buf()

    def __getitem__(self, k):
        return self.h[k]


def build_program():
    nc = bass.Bass("TRN2", target_bir_lowering=False)
    es = ExitStack()
    S = Sched(nc, es)

    def din(name, shape, dt=F32):
        return nc.dram_tensor(name, list(shape), dt, kind="ExternalInput").ap()

    def dout(name, shape, dt=F32):
        return nc.dram_tensor(name, list(shape), dt, kind="ExternalOutput").ap()

    stk = [es]

    used_names = {}

    def sb(name, shape, dt=F32):
        k = used_names.get(name, 0)
        used_names[name] = k + 1
        if k:
            name = f"{name}_v{k}"
        return T(stk[-1].enter_context(nc.sbuf_tensor(name, list(shape), dt)))

    def ps(name, shape, dt=F32):
        return T(es.enter_context(nc.psum_tensor(name, list(shape), dt)))

    xe = din("xe", [NSLOT, D])
    meta = din("meta", [128, 8])
    norm_mix = din("norm_mix", [2, D])
    a_w_in = din("a_w_in", [D, A_IN])
    nsa_g_qk = din("nsa_g_qk", [4, 64])
    xs = din("xs", [128, D])
    gla_w_a2 = din("gla_w_a2", [16, 256])
    gla_b_a = din("gla_b_a", [256])
    state_gla = din("state_gla", [16, 4, 64, 128])
    cache_win = din("cache_win", [16, 512, 256])
    gla_g_norm = din("gla_g_norm", [128])
    nsa_w_phi = din("nsa_w_phi", [2, 32, 64, 64])
    nsa_pe = din("nsa_pe", [2, 32, 64])
    c_w_in = din("c_w_in", [D, 4608])
    c_g_qk = din("c_g_qk", [6, 64])
    kd = [nc.dram_tensor(f"kd{g}", [4096, 512], BF16, kind="Internal").ap() for g in range(3)]
    vd = [nc.dram_tensor(f"vd{g}", [4096, 512], BF16, kind="Internal").ap() for g in range(3)]
    qd = [nc.dram_tensor(f"qd{g}", [2048, 512], BF16, kind="Internal").ap() for g in range(3)]
    od = [nc.dram_tensor(f"od{g}", [2048, 520], F32, kind="Internal").ap() for g in range(3)]
    qsd = nc.dram_tensor("qsd", [3, 128, 512], F32, kind="Internal").ap()
    ksd = nc.dram_tensor("ksd", [3, 128, 1024], F32, kind="Internal").ap()
    cache_dil = [din(f"cache_dil{g}", [16, w, 1024]) for g, w in enumerate((128, 512, 2048))]
    page_tab = din("page_tab", [16, 16], I32)
    pool_c = din("pool_c", [2560, 128, 256])
    pool_s = din("pool_s", [2560, 128, 256])
    sproj_d = nc.dram_tensor("sproj_d", [128, A_IN], F32, kind="Internal").ap()
    pe0 = din("pe0", [33, 128, 256])
    pe1 = din("pe1", [17, 128, 256])
    norm_mlp = din("norm_mlp", [2, D])
    norm_ple = din("norm_ple", [2, D])
    a_w_out = din("a_w_out", [1024, D])
    c_w_out = din("c_w_out", [512, D])
    mlp_w1 = din("mlp_w1", [2, D, 4096])
    mlp_w2 = din("mlp_w2", [2, 4096, D])
    ple_w_proj = din("ple_w_proj", [2, 256, D])
    ple_w_gate = din("ple_w_gate", [2, D, D])
    attn0 = nc.dram_tensor("attn0", [33, 128, 1024], BF16, kind="Internal").ap()
    attn1 = nc.dram_tensor("attn1", [17, 128, 512], BF16, kind="Internal").ap()
    h1s = nc.dram_tensor("h1s", [33, 128, D], F32, kind="Internal").ap()

    o_cmp_p = dout("o_cmp_p", [SEG, 256])
    o_slc_p = dout("o_slc_p", [SEG, 256])
    o_win_p = dout("o_win_p", [512, 256])
    o_cmp_s = dout("o_cmp_s", [64, 256])
    o_slc_s = dout("o_slc_s", [64, 256])
    o_win_s = dout("o_win_s", [16, 512, 256])
    o_gla_p = dout("o_gla_p", [4, 64, 128])
    o_gla_s = dout("o_gla_s", [16, 4, 64, 128])
    dbg_attn = dout("dbg_attn", [SEG, 1024])
    dbg_h1 = dout("dbg_h1", [SEG, 1024])
    dbg_h1s = dout("dbg_h1s", [128, 1024])
    dbg_as = dout("dbg_as", [64, 1024])
    y_p = dout("y_p", [SEG, D])
    o_dil_p = [dout(f"o_dil_p{g}", [w, 1024]) for g, w in enumerate((128, 512, 2048))]
    o_dil_s = [dout(f"o_dil_s{g}", [16, w, 1024]) for g, w in enumerate((128, 512, 2048))]
    y_s = dout("y_s", [64, D])

    ident = sb("ident", [128, 128], BF16)
    identf = sb("identf", [128, 128], F32)
    iot = sb("iot", [128, 128], I32)
    S.op("pool", lambda e: e.iota(iot[:], pattern=[[1, 128]], base=0, channel_multiplier=-1), writes=[iot.b])
    S.op("dve", lambda e: e.tensor_copy(identf[:], iot[:]), reads=[iot.b], writes=[identf.b])
    S.op("dve", lambda e: e.tensor_scalar(identf[:], identf[:], 0.0, None, ALU.is_equal), reads=[identf.b], writes=[identf.b])
    S.op("dve", lambda e: e.tensor_copy(ident[:], identf[:]), reads=[identf.b], writes=[ident.b])

    epsc = sb("epsc", [128, 1])
    S.op("dve", lambda e: e.memset(epsc[:], EPS), writes=[epsc.b])
    metat = sb("metat", [128, 8])
    S.dma("sp", metat[:], meta[:, :], writes=[metat.b])

    NT = NSLOT // 128
    posi = sb("posi", [128, NT], I32)
    posf = sb("posf", [128, NT + 1])
    S.op("pool", lambda e: e.iota(posi[:], pattern=[[128, NT]], base=0, channel_multiplier=1), writes=[posi.b])
    S.op("dve", lambda e: e.tensor_copy(posf[:, 0:NT], posi[:]), reads=[posi.b], writes=[posf.b])
    S.op("dve", lambda e: e.tensor_scalar(posf[:, 0:NT], posf[:, 0:NT], metat[:, 0:1], 0.0, ALU.subtract, ALU.max),
         reads=[posf.b, metat.b], writes=[posf.b])
    S.op("dve", lambda e: e.tensor_copy(posf[:, NT:NT + 1], metat[:, 1:2]), reads=[posf.b, metat.b], writes=[posf.b])
    fri = sb("fri", [128, 32], I32)
    frq = sb("frq", [128, 32])
    S.op("pool", lambda e: e.iota(fri[:], pattern=[[1, 32]], base=0, channel_multiplier=0), writes=[fri.b])
    S.op("dve", lambda e: e.tensor_copy(frq[:], fri[:]), reads=[fri.b], writes=[frq.b])
    S.op("act", lambda e: e.activation(frq[:], frq[:], AF.Exp, scale=-math.log(10000.0) / 32.0),
         reads=[frq.b], writes=[frq.b])
    TWO_PI = 2.0 * math.pi
    ang = sb("ang", [128, 32])
    angi = sb("angi", [128, 32], I32)
    angn = sb("angn", [128, 32])
    cs_t = sb("cs_t", [128, 32])
    sn_t = sb("sn_t", [128, 32])

    def sin_of(dst, shift):
        S.op("dve", lambda e: e.tensor_scalar(dst[:], ang[:], shift, 1.0 / TWO_PI, ALU.add, ALU.mult),
             reads=[ang.b], writes=[dst.b])
        S.op("dve", lambda e: e.tensor_copy(angi[:], dst[:]), reads=[dst.b], writes=[angi.b])
        S.op("dve", lambda e: e.tensor_copy(angn[:], angi[:]), reads=[angi.b], writes=[angn.b])
        S.op("dve", lambda e: e.tensor_tensor(dst[:], dst[:], angn[:], ALU.subtract), reads=[dst.b, angn.b], writes=[dst.b])
        S.op("dve", lambda e: e.tensor_scalar(dst[:], dst[:], TWO_PI, math.pi, ALU.mult, ALU.min), reads=[dst.b], writes=[dst.b])
        S.op("dve", lambda e: e.tensor_scalar(dst[:], dst[:], -math.pi, None, ALU.max), reads=[dst.b], writes=[dst.b])
        S.op("act", lambda e: e.activation(dst[:], dst[:], AF.Sin), reads=[dst.b], writes=[dst.b])

    def rope_tables(ti):
        S.op("dve", lambda e: e.tensor_scalar(ang[:], frq[:], posf[:, ti:ti + 1], None, ALU.mult),
             reads=[frq.b, posf.b], writes=[ang.b])
        sin_of(sn_t, 0.0)
        sin_of(cs_t, 0.5 * math.pi)

    gmix = sb("gmix", [128, 2, 8])
    S.dma("sp", gmix[:], norm_mix.rearrange("l (c p) -> p l c", p=128), writes=[gmix.b],
          allow_slow_non_contiguous=True)
    gqk = sb("gqk", [128, 4, 64])
    S.dma("sp", gqk[:], nsa_g_qk.rearrange("a d -> (a d)").partition_broadcast(128), writes=[gqk.b])

    pidx_i = sb("pidx_i", [128, 128], I32)
    jidx_i = sb("jidx_i", [128, 128], I32)
    pidx = sb("pidx", [128, 128])
    jidx = sb("jidx", [128, 128])
    S.op("pool", lambda e: e.iota(pidx_i[:], pattern=[[0, 128]], base=0, channel_multiplier=1), writes=[pidx_i.b])
    S.op("pool", lambda e: e.iota(jidx_i[:], pattern=[[1, 128]], base=0, channel_multiplier=0), writes=[jidx_i.b])
    S.op("dve", lambda e: e.tensor_copy(pidx[:], pidx_i[:]), reads=[pidx_i.b], writes=[pidx.b])
    S.op("dve", lambda e: e.tensor_copy(jidx[:], jidx_i[:]), reads=[jidx_i.b], writes=[jidx.b])
    fd_i = sb("fd_i", [128, 128], I32)

    def floordiv(dst, src, n):
        S.op("dve", lambda e: e.tensor_scalar(dst[:], src[:], 0.5, 1.0 / n, ALU.add, ALU.mult), reads=[src.b], writes=[dst.b])
        S.op("dve", lambda e: e.tensor_scalar(dst[:], dst[:], -0.5, None, ALU.add), reads=[dst.b], writes=[dst.b])
        S.op("dve", lambda e: e.tensor_copy(fd_i[:], dst[:]), reads=[dst.b], writes=[fd_i.b])
        S.op("dve", lambda e: e.tensor_copy(dst[:], fd_i[:]), reads=[fd_i.b], writes=[dst.b])

    pgt = sb("pgt", [128, 128])
    S.op("dve", lambda e: e.tensor_tensor(pgt[:], pidx[:], jidx[:], ALU.is_gt), reads=[pidx.b, jidx.b], writes=[pgt.b])
    pc_t = sb("pc_t", [128, 128])
    jc_t = sb("jc_t", [128, 128])
    Lm = {}
    Ci = {}
    for csz in (64, 4):
        floordiv(pc_t, pidx, csz)
        floordiv(jc_t, jidx, csz)
        lm = sb(f"Lm{csz}", [128, 128])
        ci = sb(f"Ci{csz}", [128, 32])
        S.op("dve", lambda e: e.tensor_tensor(lm[:], pc_t[:], jc_t[:], ALU.is_equal), reads=[pc_t.b, jc_t.b], writes=[lm.b])
        S.op("dve", lambda e: e.tensor_tensor(lm[:], lm[:], pgt[:], ALU.mult), reads=[lm.b, pgt.b], writes=[lm.b])
        S.op("dve", lambda e: e.tensor_tensor(ci[:], pc_t[:, 0:32], jidx[:, 0:32], ALU.is_equal),
             reads=[pc_t.b, jidx.b], writes=[ci.b])
        Lm[csz] = lm
        Ci[csz] = ci
    onec = sb("onec", [128, 1])
    S.op("dve", lambda e: e.memset(onec[:], 1.0), writes=[onec.b])
    ones1 = sb("ones1", [1, 128], BF16)
    S.op("dve", lambda e: e.memset(ones1[:], 1.0), writes=[ones1.b])
    wa2 = sb("wa2", [16, 256], BF16)
    S.dma("pool", wa2[:], gla_w_a2[:, :], writes=[wa2.b])
    bab = sb("bab", [1, 256], BF16)
    S.dma("pool", bab[:], gla_b_a.rearrange("(o n) -> o n", o=1), writes=[bab.b])

    Um = {}
    ple_t = sb("ple_t", [128, 128])
    S.op("dve", lambda e: e.tensor_tensor(ple_t[:], pidx[:], jidx[:], ALU.is_le), reads=[pidx.b, jidx.b], writes=[ple_t.b])
    Cj4 = sb("Cj4", [128, 128])
    for csz in (64, 4):
        floordiv(pc_t, pidx, csz)
        floordiv(jc_t, jidx, csz)
        um = sb(f"Um{csz}", [128, 128])
        S.op("dve", lambda e: e.tensor_tensor(um[:], pc_t[:], jc_t[:], ALU.is_equal), reads=[pc_t.b, jc_t.b], writes=[um.b])
        S.op("dve", lambda e: e.tensor_tensor(um[:], um[:], ple_t[:], ALU.mult), reads=[um.b, ple_t.b], writes=[um.b])
        Um[csz] = um
    S.op("dve", lambda e: e.tensor_copy(Cj4[:], jc_t[:]), reads=[jc_t.b], writes=[Cj4.b])
    causneg = sb("causneg", [128, 4, 128], BF16)
    winneg = sb("winneg", [128, 4, 128], BF16)
    S.op("dve", lambda e: e.tensor_scalar(causneg[:], pgt[:].unsqueeze(1).to_broadcast([128, 4, 128]), NEGB, None, ALU.mult),
         reads=[pgt.b], writes=[causneg.b])
    S.op("dve", lambda e: e.tensor_scalar(winneg[:], ple_t[:].unsqueeze(1).to_broadcast([128, 4, 128]), NEGB, None, ALU.mult),
         reads=[ple_t.b], writes=[winneg.b])
    V16 = sb("V16", [128, 128])
    S.op("dve", lambda e: e.scalar_tensor_tensor(V16[:], pidx[:], -16.0, jidx[:], ALU.mult, ALU.add),
         reads=[pidx.b, jidx.b], writes=[V16.b])
    BLK = sb("BLK", [128, 128])
    S.op("dve", lambda e: e.tensor_scalar(BLK[:], jidx[:], 64.0, None, ALU.is_ge), reads=[jidx.b], writes=[BLK.b])
    S.op("dve", lambda e: e.tensor_tensor(BLK[:], pidx[:], BLK[:], ALU.subtract), reads=[pidx.b, BLK.b], writes=[BLK.b])
    hi64 = sb("hi64", [128, 1])
    S.op("dve", lambda e: e.tensor_scalar(hi64[:], pidx[:, 0:1], 64.0, None, ALU.is_ge), reads=[pidx.b], writes=[hi64.b])
    C2S = sb("C2S", [128, 4, 128], BF16)
    c2a = sb("c2a", [128, 128])
    c2b = sb("c2b", [128, 128])
    for ct in range(4):
        S.op("dve", lambda e: e.tensor_scalar(c2a[:], pidx[:], 16.0, 2048.0 * ct + 32.0, ALU.mult, ALU.add), reads=[pidx.b], writes=[c2a.b])
        S.op("dve", lambda e: e.tensor_scalar(c2b[:], jidx[:], 64.0, 64.0, ALU.mult, ALU.add), reads=[jidx.b], writes=[c2b.b])
        S.op("dve", lambda e: e.tensor_tensor(c2a[:], c2a[:], c2b[:], ALU.min), reads=[c2a.b, c2b.b], writes=[c2a.b])
        S.op("dve", lambda e: e.tensor_scalar(c2b[:], c2b[:], -64.0 - 2048.0 * ct, None, ALU.add), reads=[c2b.b], writes=[c2b.b])
        S.op("dve", lambda e: e.scalar_tensor_tensor(c2b[:], pidx[:], 16.0, c2b[:], ALU.mult, ALU.max), reads=[pidx.b, c2b.b], writes=[c2b.b])
        S.op("dve", lambda e: e.tensor_scalar(c2b[:], c2b[:], 2048.0 * ct, None, ALU.add), reads=[c2b.b], writes=[c2b.b])
        S.op("dve", lambda e: e.tensor_tensor(c2a[:], c2a[:], c2b[:], ALU.subtract), reads=[c2a.b, c2b.b], writes=[c2a.b])
        S.op("dve", lambda e: e.tensor_scalar(C2S[:, ct, :], c2a[:], 0.0, 1.0 / 32.0, ALU.max, ALU.mult), reads=[c2a.b], writes=[C2S.b])
    padk = sb("padk", [128, 64])
    S.op("dve", lambda e: e.tensor_scalar(padk[:], jidx[:, 0:64], 128.0, metat[:, 0:1], ALU.mult, ALU.is_lt),
         reads=[jidx.b, metat.b], writes=[padk.b])
    S.op("dve", lambda e: e.tensor_scalar(padk[:], padk[:], NEGB, None, ALU.mult), reads=[padk.b], writes=[padk.b])
    padc = sb("padc", [128, 4])
    for ct in range(4):
        S.op("dve", lambda e: e.tensor_scalar(padc[:, ct:ct + 1], pidx[:, 0:1], 16.0, 2048.0 * ct, ALU.mult, ALU.add),
             reads=[pidx.b], writes=[padc.b])
    S.op("dve", lambda e: e.tensor_scalar(padc[:], padc[:], metat[:, 0:1], NEGB, ALU.is_lt, ALU.mult),
         reads=[padc.b, metat.b], writes=[padc.b])
    gno = sb("gno", [128, 128])
    S.dma("sp", gno[:], gla_g_norm.partition_broadcast(128), writes=[gno.b])

    esA = ExitStack()
    stk.append(esA)
    win = sb("win", [128, 8, A_IN], BF16)
    a_w_in_v = a_w_in.rearrange("(c p) n -> p c n", p=128)
    for c in range(8):
        S.dma("pool", win[:, c, :], a_w_in_v[:, c, :], writes=[win.b])

    Wbd = [sb(f"Wbd{a}", [128, 32, 128], BF16) for a in range(2)]
    peT = sb("peT", [128, 2, 32], BF16)
    with ExitStack() as est:
        stk.append(est)
        wst = sb("wst", [128, 32, 64])
        pst = sb("pst", [128, 2, 32])
        for a in range(2):
            S.op("pool", lambda e: e.memset(Wbd[a][:], 0.0), writes=[Wbd[a].b])
            for g in range(2):
                for q in range(4):
                    S.dma("sp", wst[64 * g:64 * g + 64, 8 * q:8 * q + 8, :],
                          nsa_w_phi[a, 8 * q:8 * q + 8].rearrange("p d e -> d p e"), writes=[wst.b])
                S.dma("sp", pst[64 * g:64 * g + 64, a, :], nsa_pe[a].rearrange("p d -> d p"), writes=[pst.b],
                      allow_slow_non_contiguous=True)
            for g in range(2):
                S.op("dve", lambda e: e.tensor_copy(Wbd[a][64 * g:64 * g + 64, :, 64 * g:64 * g + 64], wst[64 * g:64 * g + 64, :, :]),
                     reads=[wst.b, Wbd[a].b], writes=[Wbd[a].b])
        S.op("dve", lambda e: e.tensor_copy(peT[:], pst[:]), reads=[pst.b], writes=[peT.b])
        S.finish()
        stk.pop()
    pT = ps("pT", [128, 8, 128], BF16)
    pP = [ps(f"pP{i}", [128, 512]) for i in range(3)]
    pS = ps("pS", [128, 512])
    pO = ps("pO", [128, 512])
    pI = ps("pI", [128, 512])
    ppi = 0

    def nextpp():
        nonlocal ppi
        ppi += 1
        return pP[ppi % 3]

    cbias = sb("cbias", [128, 2])
    pb0 = nextpp()
    for a in range(2):
        for p in range(32):
            S.op("pe", lambda e: e.matmul(pb0[:, 2 * a:2 * a + 2], Wbd[a][:, p, :], peT[:, :, p], start=(p == 0), stop=(p == 31)),
                 reads=[Wbd[a].b, peT.b], writes=[pb0.b])
    S.op("act", lambda e: e.copy(cbias[:, 0:1], pb0[:, 0:1]), reads=[pb0.b], writes=[cbias.b])
    S.op("act", lambda e: e.copy(cbias[:, 1:2], pb0[:, 3:4]), reads=[pb0.b], writes=[cbias.b])

    xt = [sb(f"xt{i}", [128, D]) for i in range(2)]
    ssq = sb("ssq", [128, 1])
    xn = sb("xn", [128, D], BF16)
    hnT = sb("hnT", [128, 8, 128], BF16)
    proj = sb("proj", [128, A_IN])
    sq14 = sb("sq14", [128, 8, 64])
    ss14 = sb("ss14", [128, 8])
    rows = sb("rows", [128, 768])
    tmpa = sb("tmpa", [128, 8, 32])
    tmpb = sb("tmpb", [128, 8, 32])
    gaT = sb("gaT", [16, 128], BF16)
    gab = sb("gab", [128, 256])
    gmn = sb("gmn", [128, 256])
    lg = sb("lg", [128, 256])
    ecs = sb("ecs", [128, 256])
    khat = sb("khat", [128, 256], BF16)
    khm = sb("khm", [64, 256], BF16)
    vb = sb("vb", [128, 512], BF16)
    dec = sb("dec", [64, 4, 16])
    Sst = sb("Sst", [64, 4, 128])
    ebt = sb("ebt", [128, 128])
    qtT = [sb(f"qtT{h}", [128, 128], BF16) for h in range(2)]
    ktT = [sb(f"ktT{h}", [128, 128], BF16) for h in range(2)]
    qz = [[sb(f"qz{j}{h}", [128, 128], BF16) for h in range(2)] for j in range(2)]
    qtTz = [[sb(f"qtTz{hf}{hh}", [128, 128], BF16) for hh in range(2)] for hf in range(2)]
    Sbz = [[sb(f"Sbz{j}{h}", [128, 128], BF16) for h in range(4)] for j in range(2)]
    khz = [sb(f"khz{j}", [128, 256], BF16) for j in range(2)]
    for j in range(2):
        for h in range(2):
            S.op("pool", lambda e: e.memset(qz[j][h][:], 0.0), writes=[qz[j][h].b])
            S.op("pool", lambda e: e.memset(qtTz[j][h][:], 0.0), writes=[qtTz[j][h].b])
        for h in range(4):
            S.op("pool", lambda e: e.memset(Sbz[j][h][:], 0.0), writes=[Sbz[j][h].b])
    QcTz = [sb(f"QcTz{g}", [128, 4, 128], BF16) for g in range(2)]
    QrTz = [sb(f"QrTz{g}", [128, 4, 128], BF16) for g in range(2)]
    for g in range(2):
        S.op("pool", lambda e: e.memset(QcTz[g][:], 0.0), writes=[QcTz[g].b])
        S.op("pool", lambda e: e.memset(QrTz[g][:], 0.0), writes=[QrTz[g].b])
    lo64 = sb("lo64", [128, 1])
    S.op("dve", lambda e: e.tensor_scalar(lo64[:], hi64[:], -1.0, 1.0, ALU.mult, ALU.add), reads=[hi64.b], writes=[lo64.b])
    ATb = sb("ATb", [128, 4, 128], BF16)
    ogf = sb("ogf", [128, 4, 128])
    ogs = sb("ogs", [128, 4, 128])
    sgr = sb("sgr", [128, 512])
    attnb = sb("attnb", [128, 1024], BF16)
    dbgt = sb("dbgt", [128, 1024])
    xsq = dbgt
    qn = sb("qn", [128, 8, 64])
    qcp = sb("qcp", [128, 4, 128], BF16)
    qrp = sb("qrp", [128, 4, 128], BF16)
    kb4 = sb("kb4", [128, 4, 128], BF16)
    kT4 = sb("kT4", [128, 4, 128], BF16)
    gts = sb("gts", [128, 24])
    onsa = sb("onsa", [128, 8, 64])
    otmp = sb("otmp", [128, 4, 64])
    coef = sb("coef", [128, 4])
    rl4 = sb("rl4", [128, 4])
    Pt = [sb(f"Pt{i}", [128, 4, 128], BF16) for i in range(2)]
    mk = sb("mk", [128, 128], BF16)
    Ekt = sb("Ekt", [128, 128], BF16)
    imp = sb("imp", [128, 128])
    imw = sb("imw", [128, 128])
    vm = sb("vm", [128, 128])
    fm = sb("fm", [128, 128])
    curc = sb("curc", [128, 1])
    mx8 = sb("mx8", [128, 8])
    seln = sb("seln", [128, 128], BF16)
    selnT = [sb(f"selnT{g}", [128, 4, 128], BF16) for g in range(2)]

    xe_v = xe.rearrange("(t p) d -> t p d", p=128)
    groups = [(0, 512), (512, 1024), (1024, 1536), (1536, 2048), (2048, 2560), (2560, A_IN)]

    def qknorm(src_ap, nh, gidx, dst_ap, rd, wr):
        S.op("dve", lambda e: e.tensor_tensor(sq14[:, 0:nh, :], src_ap, src_ap, ALU.mult), reads=rd, writes=[sq14.b])
        S.op("dve", lambda e: e.tensor_reduce(ss14[:, 0:nh], sq14[:, 0:nh, :], AX.X, ALU.add),
             reads=[sq14.b], writes=[ss14.b])
        S.op("act", lambda e: e.activation(ss14[:, 0:nh], ss14[:, 0:nh], AF.Sqrt, bias=epsc[:, 0:1], scale=1.0 / 64.0),
             reads=[ss14.b, epsc.b], writes=[ss14.b])
        S.op("dve", lambda e: e.reciprocal(ss14[:, 0:nh], ss14[:, 0:nh]), reads=[ss14.b], writes=[ss14.b])
        S.op("dve", lambda e: e.tensor_tensor(dst_ap, src_ap, ss14[:, 0:nh].unsqueeze(2).to_broadcast([128, nh, 64]),
                                              ALU.mult), reads=rd + [ss14.b], writes=wr)
        S.op("dve", lambda e: e.tensor_tensor(dst_ap, dst_ap,
                                              gqk[:, gidx, :].unsqueeze(1).to_broadcast([128, nh, 64]), ALU.mult),
             reads=wr + [gqk.b], writes=wr)

    def rope(ap3, nh, ti, bufs):
        x1 = ap3[:, :, 0:32]
        x2 = ap3[:, :, 32:64]
        cs = cs_t[:].unsqueeze(1).to_broadcast([128, nh, 32])
        sn = sn_t[:].unsqueeze(1).to_broadcast([128, nh, 32])
        ta = tmpa[:, 0:nh, :]
        tb = tmpb[:, 0:nh, :]
        S.op("dve", lambda e: e.tensor_tensor(ta, x1, sn, ALU.mult), reads=bufs + [sn_t.b], writes=[tmpa.b])
        S.op("dve", lambda e: e.tensor_tensor(tb, x2, sn, ALU.mult), reads=bufs + [sn_t.b], writes=[tmpb.b])
        S.op("dve", lambda e: e.tensor_tensor(x1, x1, cs, ALU.mult), reads=bufs + [cs_t.b], writes=bufs)
        S.op("dve", lambda e: e.tensor_tensor(x2, x2, cs, ALU.mult), reads=bufs + [cs_t.b], writes=bufs)
        S.op("dve", lambda e: e.tensor_tensor(x1, x1, tb, ALU.subtract), reads=bufs + [tmpb.b], writes=bufs)
        S.op("dve", lambda e: e.tensor_tensor(x2, x2, ta, ALU.add), reads=bufs + [tmpa.b], writes=bufs)

    def project_tile(ti, src_ap, xbuf):
        S.dma("sp", xbuf[:], src_ap, writes=[xbuf.b])
        if int(os.environ.get('KROPE', 1)):
            rope_tables(ti)
        S.op("act", lambda e: e.activation(xsq[:], xbuf[:], AF.Square, accum_out=ssq[:]),
             reads=[xbuf.b], writes=[xsq.b, ssq.b])
        S.op("act", lambda e: e.activation(ssq[:], ssq[:], AF.Sqrt, bias=epsc[:, 0:1], scale=1.0 / D),
             reads=[ssq.b, epsc.b], writes=[ssq.b])
        S.op("dve", lambda e: e.reciprocal(ssq[:], ssq[:]), reads=[ssq.b], writes=[ssq.b])
        S.op("dve", lambda e: e.tensor_scalar(xn[:], xbuf[:], ssq[:, 0:1], None, ALU.mult),
             reads=[xbuf.b, ssq.b], writes=[xn.b])
        for c in range(8):
            S.op("pe", lambda e: e.transpose(pT[:, c, :], xn[:, c * 128:(c + 1) * 128], ident[:]),
                 reads=[xn.b, ident.b], writes=[pT.b])
        S.op("dve", lambda e: e.tensor_tensor(hnT[:], pT[:], gmix[:, 0, :].unsqueeze(2).to_broadcast([128, 8, 128]),
                                              ALU.mult), reads=[pT.b, gmix.b], writes=[hnT.b])
        for (c0, c1) in groups:
            pp = nextpp()
            n = c1 - c0
            for c in range(8):
                S.op("pe", lambda e: e.matmul(pp[:, 0:n], hnT[:, c, :], win[:, c, c0:c1],
                                              start=(c == 0), stop=(c == 7)),
                     reads=[hnT.b, win.b], writes=[pp.b])
            S.op("act", lambda e: e.copy(proj[:, c0:c1], pp[:, 0:n]), reads=[pp.b], writes=[proj.b])
        r = rows
        S.op("act", lambda e: e.copy(r[:], proj[:, 2064:2832]), reads=[proj.b], writes=[r.b])
        for (kidx, gidx) in ((0, 1), (2, 2), (4, 3)):
            src = proj[:, 2064 + kidx * 128: 2064 + (kidx + 1) * 128].rearrange("p (h d) -> p h d", d=64)
            dst = r[:, kidx * 128:(kidx + 1) * 128].rearrange("p (h d) -> p h d", d=64)
            qknorm(src, 2, gidx, dst, [proj.b], [r.b])
            if kidx > 0:
                rope(dst, 2, ti, [r.b])

    def gla_gates(csz, nch):
        KG = int(os.environ.get('KG', 99))
        if KG <= 1 and csz == 64:
            return
        pg = nextpp()
        if KG <= 2 and csz == 64:
            return
        for c in range(8):
            S.op("pe", lambda e: e.matmul(pg[0:16, 0:128], win[:, c, 1536:1552], hnT[:, c, :],
                                          start=(c == 0), stop=(c == 7)), reads=[hnT.b, win.b], writes=[pg.b])
        if KG <= 3 and csz == 64:
            return
        S.op("act", lambda e: e.copy(gaT[:], pg[0:16, 0:128]), reads=[pg.b], writes=[gaT.b])
        if KG <= 4 and csz == 64:
            return
        px = nextpp()
        if KG <= 5 and csz == 64:
            return
        S.op("pe", lambda e: e.matmul(px[:, 0:256], gaT[:], wa2[:], start=True, stop=False),
             reads=[gaT.b, wa2.b], writes=[px.b])
        if KG <= 6 and csz == 64:
            return
        S.op("pe", lambda e: e.matmul(px[:, 0:256], ones1[:], bab[:], start=False, stop=True),
             reads=[ones1.b, bab.b], writes=[px.b])
        if KG <= 7 and csz == 64:
            return
        S.op("act", lambda e: e.activation(gab[:], px[:, 0:256], AF.Abs), reads=[px.b], writes=[gab.b])
        if KG <= 8 and csz == 64:
            return
        S.op("act", lambda e: e.activation(gab[:], gab[:], AF.Exp, scale=-1.0), reads=[gab.b], writes=[gab.b])
        if KG <= 9 and csz == 64:
            return
        S.op("dve", lambda e: e.tensor_scalar(ecs[:], gab[:], 2.0, None, ALU.add), reads=[gab.b], writes=[ecs.b])
        S.op("dve", lambda e: e.reciprocal(ecs[:], ecs[:]), reads=[ecs.b], writes=[ecs.b])
        S.op("dve", lambda e: e.tensor_tensor(gab[:], gab[:], ecs[:], ALU.mult), reads=[gab.b, ecs.b], writes=[gab.b])
        S.op("dve", lambda e: e.tensor_tensor(ecs[:], gab[:], gab[:], ALU.mult), reads=[gab.b], writes=[ecs.b])
        S.op("dve", lambda e: e.tensor_scalar(gmn[:], ecs[:], 1.0 / 11.0, 1.0 / 9.0, ALU.mult, ALU.add), reads=[ecs.b], writes=[gmn.b])
        for cst in (1.0 / 7.0, 1.0 / 5.0, 1.0 / 3.0, 1.0):
            S.op("dve", lambda e: e.tensor_tensor(gmn[:], gmn[:], ecs[:], ALU.mult), reads=[gmn.b, ecs.b], writes=[gmn.b])
            S.op("dve", lambda e: e.tensor_scalar(gmn[:], gmn[:], cst, None, ALU.add), reads=[gmn.b], writes=[gmn.b])
        S.op("dve", lambda e: e.scalar_tensor_tensor(gab[:], gab[:], 2.0, gmn[:], ALU.mult, ALU.mult), reads=[gab.b, gmn.b], writes=[gab.b])
        if KG <= 10 and csz == 64:
            return
        S.op("dve", lambda e: e.tensor_single_scalar(gmn[:], px[:, 0:256], 0.0, ALU.min), reads=[px.b], writes=[gmn.b])
        if KG <= 11 and csz == 64:
            return
        S.op("dve", lambda e: e.tensor_tensor(lg[:], gmn[:], gab[:], ALU.subtract), reads=[gmn.b, gab.b], writes=[lg.b])
        if KG <= 12 and csz == 64:
            return
        S.op("dve", lambda e: e.tensor_scalar(lg[:], lg[:], 1.0 / 16.0, None, ALU.mult), reads=[lg.b], writes=[lg.b])
        if KG <= 13 and csz == 64:
            return
        pc = nextpp()
        if KG <= 14 and csz == 64:
            return
        S.op("pe", lambda e: e.matmul(pc[:, 0:256], Lm[csz][:], lg[:], start=True, stop=True),
             reads=[Lm[csz].b, lg.b], writes=[pc.b])
        if KG <= 15 and csz == 64:
            return
        S.op("act", lambda e: e.activation(ecs[:], pc[:, 0:256], AF.Exp), reads=[pc.b], writes=[ecs.b])
        if KG <= 16 and csz == 64:
            return
        S.op("dve", lambda e: e.tensor_tensor(khat[:], proj[:, 256:512], ecs[:], ALU.mult),
             reads=[proj.b, ecs.b], writes=[khat.b])
        if KG <= 17 and csz == 64:
            return
        S.op("act", lambda e: e.copy(vb[:], proj[:, 512:1024]), reads=[proj.b], writes=[vb.b])
        if KG <= 18 and csz == 64:
            return
        pd = nextpp()
        if KG <= 19 and csz == 64:
            return
        for h in range(4):
            S.op("pe", lambda e: e.matmul(pd[0:64, h * 16:h * 16 + 16], lg[:, 64 * h:64 * h + 64], Ci[csz][:, 0:16],
                                          start=True, stop=True), reads=[lg.b, Ci[csz].b], writes=[pd.b])
        if KG <= 20 and csz == 64:
            return
        for h in range(4):
            S.op("act", lambda e: e.activation(dec[:, h, 0:nch], pd[0:64, h * 16:h * 16 + nch], AF.Exp),
                 reads=[pd.b], writes=[dec.b])

    def gla_qk(csz):
        for hf in range(2):
            pq = nextpp()
            for c in range(8):
                S.op("pe", lambda e: e.matmul(pq[:, 0:128], win[:, c, 128 * hf:128 * hf + 128], hnT[:, c, :],
                                              start=(c == 0), stop=(c == 7)), reads=[hnT.b, win.b], writes=[pq.b])
            pk = nextpp()
            for c in range(8):
                S.op("pe", lambda e: e.matmul(pk[:, 0:128], win[:, c, 256 + 128 * hf:256 + 128 * hf + 128], hnT[:, c, :],
                                              start=(c == 0), stop=(c == 7)), reads=[hnT.b, win.b], writes=[pk.b])
            pb = nextpp()
            S.op("pe", lambda e: e.matmul(pb[:, 0:128], lg[:, 128 * hf:128 * hf + 128], Um[csz][:], start=True, stop=True),
                 reads=[lg.b, Um[csz].b], writes=[pb.b])
            S.op("act", lambda e: e.activation(ebt[:], pb[:, 0:128], AF.Exp), reads=[pb.b], writes=[ebt.b])
            S.op("dve", lambda e: e.scalar_tensor_tensor(qtT[hf][:], pq[:, 0:128], 0.125, ebt[:], ALU.mult, ALU.mult),
                 reads=[pq.b, ebt.b], writes=[qtT[hf].b])
            S.op("act", lambda e: e.activation(ebt[:], pb[:, 0:128], AF.Exp, scale=-1.0), reads=[pb.b], writes=[ebt.b])
            S.op("dve", lambda e: e.tensor_tensor(ktT[hf][:], pk[:, 0:128], ebt[:], ALU.mult),
                 reads=[pk.b, ebt.b], writes=[ktT[hf].b])
            for hh in range(2):
                S.op("act", lambda e: e.copy(qtTz[hf][hh][64 * hh:64 * hh + 64, :], qtT[hf][64 * hh:64 * hh + 64, :]),
                     reads=[qtT[hf].b], writes=[qtTz[hf][hh].b])

    def gla_intra(csz):
        pA = nextpp()
        for h in range(4):
            hf, hh = h // 2, h % 2
            S.op("pe", lambda e: e.matmul(pA[:, h * 128:(h + 1) * 128], ktT[hf][:, :],
                                          qtTz[hf][hh][:, :], start=True, stop=True),
                 reads=[ktT[hf].b, qtTz[hf][hh].b], writes=[pA.b])
        if int(os.environ.get('KI', 9)) <= 1:
            return
        S.op("dve", lambda e: e.tensor_tensor(ATb[:], pA[:, :].rearrange("p (h t) -> p h t", h=4),
                                              Um[csz][:].unsqueeze(1).to_broadcast([128, 4, 128]), ALU.mult),
             reads=[pA.b, Um[csz].b], writes=[ATb.b])

    def gla_post(dst_ap, wr):
        S.op("act", lambda e: e.copy(ogf[:], pO[:, :].rearrange("p (h v) -> p h v", h=4)), reads=[pO.b], writes=[ogf.b])
        S.op("dve", lambda e: e.tensor_tensor(ogs[:], ogf[:], ogf[:], ALU.mult), reads=[ogf.b], writes=[ogs.b])
        S.op("dve", lambda e: e.tensor_reduce(ss14[:, 0:4], ogs[:], AX.X, ALU.add), reads=[ogs.b], writes=[ss14.b])
        S.op("act", lambda e: e.activation(ss14[:, 0:4], ss14[:, 0:4], AF.Sqrt, bias=epsc[:, 0:1], scale=1.0 / 128.0),
             reads=[ss14.b, epsc.b], writes=[ss14.b])
        S.op("dve", lambda e: e.reciprocal(ss14[:, 0:4], ss14[:, 0:4]), reads=[ss14.b], writes=[ss14.b])
        S.op("dve", lambda e: e.tensor_tensor(ogf[:], ogf[:], ss14[:, 0:4].unsqueeze(2).to_broadcast([128, 4, 128]), ALU.mult),
             reads=[ogf.b, ss14.b], writes=[ogf.b])
        S.op("dve", lambda e: e.tensor_tensor(ogf[:], ogf[:], gno[:].unsqueeze(1).to_broadcast([128, 4, 128]), ALU.mult),
             reads=[ogf.b, gno.b], writes=[ogf.b])
        S.op("act", lambda e: e.activation(sgr[:], proj[:, 1024:1536], AF.Exp, scale=-1.0), reads=[proj.b], writes=[sgr.b])
        S.op("dve", lambda e: e.tensor_scalar(sgr[:], sgr[:], 1.0, None, ALU.add), reads=[sgr.b], writes=[sgr.b])
        S.op("dve", lambda e: e.reciprocal(sgr[:], sgr[:]), reads=[sgr.b], writes=[sgr.b])
        S.op("dve", lambda e: e.tensor_tensor(sgr[:], sgr[:], proj[:, 1024:1536], ALU.mult), reads=[sgr.b, proj.b], writes=[sgr.b])
        S.op("dve", lambda e: e.tensor_tensor(dst_ap, ogf[:].rearrange("p h v -> p (h v)"), sgr[:], ALU.mult),
             reads=[ogf.b, sgr.b], writes=wr)

    def nsa_q(ti):
        src = proj[:, 1552:2064].rearrange("p (h d) -> p h d", d=64)
        qknorm(src, 8, 0, qn[:], [proj.b], [qn.b])
        S.op("act", lambda e: e.copy(qcp[:].rearrange("p j (g d) -> p j g d", g=2),
                                     qn[:].rearrange("p (g j) d -> p j g d", g=2)), reads=[qn.b], writes=[qcp.b])
        rope(qn[:], 8, ti, [qn.b])
        S.op("act", lambda e: e.copy(qrp[:].rearrange("p j (g d) -> p j g d", g=2),
                                     qn[:].rearrange("p (g j) d -> p j g d", g=2)), reads=[qn.b], writes=[qrp.b])
        for j in range(4):
            S.op("pe", lambda e: e.transpose(pT[:, j, :], qcp[:, j, :], ident[:]), reads=[qcp.b, ident.b], writes=[pT.b])
            S.op("pe", lambda e: e.transpose(pT[:, 4 + j, :], qrp[:, j, :], ident[:]), reads=[qrp.b, ident.b], writes=[pT.b])
        for g in range(2):
            S.op("act", lambda e: e.copy(QcTz[g][64 * g:64 * g + 64, :, :], pT[64 * g:64 * g + 64, 0:4, :]),
                 reads=[pT.b], writes=[QcTz[g].b])
            S.op("act", lambda e: e.copy(QrTz[g][64 * g:64 * g + 64, :, :], pT[64 * g:64 * g + 64, 4:8, :]),
                 reads=[pT.b], writes=[QrTz[g].b])
        S.op("act", lambda e: e.activation(gts[:], proj[:, 2832:2856], AF.Exp, scale=-1.0), reads=[proj.b], writes=[gts.b])
        S.op("dve", lambda e: e.tensor_scalar(gts[:], gts[:], 1.0, None, ALU.add), reads=[gts.b], writes=[gts.b])
        S.op("dve", lambda e: e.reciprocal(gts[:], gts[:]), reads=[gts.b], writes=[gts.b])

    def branch_out(g, br, first):
        pov = pO[:, 0:260].rearrange("p (j c) -> p j c", j=4)
        S.op("dve", lambda e: e.tensor_scalar(rl4[:], pov[:, :, 64], 1e-20, None, ALU.max), reads=[pO.b], writes=[rl4.b])
        S.op("dve", lambda e: e.reciprocal(rl4[:], rl4[:]), reads=[rl4.b], writes=[rl4.b])
        gv = gts[:].rearrange("p (h b) -> p h b", b=3)[:, 4 * g:4 * g + 4, br]
        S.op("dve", lambda e: e.tensor_tensor(coef[:], rl4[:], gv, ALU.mult), reads=[rl4.b, gts.b], writes=[coef.b])
        dst = onsa[:, 4 * g:4 * g + 4, :]
        if first:
            S.op("dve", lambda e: e.tensor_tensor(dst, pov[:, :, 0:64], coef[:].unsqueeze(2).to_broadcast([128, 4, 64]), ALU.mult),
                 reads=[pO.b, coef.b], writes=[onsa.b])
        else:
            S.op("dve", lambda e: e.tensor_tensor(otmp[:], pov[:, :, 0:64], coef[:].unsqueeze(2).to_broadcast([128, 4, 64]), ALU.mult),
                 reads=[pO.b, coef.b], writes=[otmp.b])
            S.op("dve", lambda e: e.tensor_tensor(dst, dst, otmp[:], ALU.add), reads=[onsa.b, otmp.b], writes=[onsa.b])

    pti = 0

    def nextPt():
        nonlocal pti
        pti += 1
        return Pt[pti % 2]

    def kside(A, ti, r):
        rv = r[:].rearrange("p (a b c) -> p a b c", a=3, b=2)
        S.op("act", lambda e: e.copy(kb4[:, 0:3, :], rv[:, :, 0, :]), reads=[r.b], writes=[kb4.b])
        S.op("act", lambda e: e.copy(kb4[:, 3, :], r[:, 128:256]), reads=[r.b], writes=[kb4.b])
        for a in range(4):
            S.op("pe", lambda e: e.transpose(pT[:, a, :], kb4[:, a, :], ident[:]), reads=[kb4.b, ident.b], writes=[pT.b])
        S.op("act", lambda e: e.copy(kT4[:], pT[:, 0:4, :]), reads=[pT.b], writes=[kT4.b])
        S.op("dve", lambda e: e.tensor_copy(A.ksT[:, ti, :], kT4[:, 1, :]), reads=[kT4.b], writes=[A.ksT.b])
        S.op("dve", lambda e: e.tensor_copy(A.kwT[:, ti % 8, :], kT4[:, 2, :]), reads=[kT4.b], writes=[A.kwT.b])
        S.op("dve", lambda e: e.tensor_copy(A.vs1[:, ti, :, 0:64], r[:, 384:512].rearrange("p (g d) -> p g d", g=2)),
             reads=[r.b], writes=[A.vs1.b])
        S.op("dve", lambda e: e.tensor_copy(A.vw1[:, ti % 8, :, 0:64], r[:, 640:768].rearrange("p (g d) -> p g d", g=2)),
             reads=[r.b], writes=[A.vw1.b])
        for a, src_i in ((0, 0), (1, 3)):
            for half, arr in ((0, A.clo[a]), (1, A.chi[a])):
                pp = nextpp()
                for p in range(16):
                    S.op("pe", lambda e: e.matmul(pp[:, 0:8], Wbd[a][:, 16 * half + p, :],
                                                  kT4[:, src_i, :].rearrange("q (c p) -> q p c", p=16)[:, p, :],
                                                  start=(p == 0), stop=(p == 15)),
                         reads=[Wbd[a].b, kT4.b], writes=[pp.b])
                S.op("act", lambda e: e.copy(arr[:, 8 * ti + 1:8 * ti + 9], pp[:, 0:8]), reads=[pp.b], writes=[arr.b])
        n0 = 8 * ti - 1
        lo_n = max(n0, 0)
        cnt = 8 * ti + 7 - lo_n
        for a, dstT in ((0, A.kcmpT), (1, A.vcmpT)):
            S.op("dve", lambda e: e.tensor_tensor(sq14[:, 0, 0:cnt], A.clo[a][:, lo_n + 1:lo_n + 1 + cnt],
                                                  A.chi[a][:, lo_n + 2:lo_n + 2 + cnt], ALU.add),
                 reads=[A.clo[a].b, A.chi[a].b], writes=[sq14.b])
            S.op("dve", lambda e: e.tensor_scalar(dstT[:, lo_n:lo_n + cnt], sq14[:, 0, 0:cnt], cbias[:, a:a + 1], None, ALU.add),
                 reads=[sq14.b, cbias.b], writes=[dstT.b])
        for ct in sorted({lo_n // 128, (8 * ti + 6) // 128}):
            S.op("pe", lambda e: e.transpose(pT[:, 4, :], A.vcmpT[:, 128 * ct:128 * ct + 128], ident[:]),
                 reads=[A.vcmpT.b, ident.b], writes=[pT.b])
            S.op("act", lambda e: e.copy(A.vcmp1[:, ct, :, 0:64], pT[:, 4, :].rearrange("p (g d) -> p g d", g=2)),
                 reads=[pT.b], writes=[A.vcmp1.b])

    def nsa_attend(A, ti):
        nsa_q(ti)
        nct = (8 * ti + 6) // 128 + 1
        for g in range(2):
            gs = slice(64 * g, 64 * g + 64)
            for ct in range(nct):
                pp = nextpp()
                S.op("pe", lambda e: e.matmul(pp[:, :], A.kcmpT[:, 128 * ct:128 * ct + 128],
                                              QcTz[g][:, :, :].rearrange("p j t -> p (j t)"), start=True, stop=True),
                     reads=[A.kcmpT.b, QcTz[g].b], writes=[pp.b])
                P = nextPt()
                S.op("act", lambda e: e.activation(P[:].rearrange("p j t -> p (j t)"), pp[:, :], AF.Exp,
                                                   bias=A.padc(ct), scale=0.125),
                     reads=[pp.b, A.padb], writes=[P.b])
                if ct >= nct - 2:
                    off = 2048.0 * ct - 128.0 * ti + 31.0
                    S.op("dve", lambda e: e.tensor_scalar(mk[:], V16[:], off, None, ALU.is_ge), reads=[V16.b], writes=[mk.b])
                    S.op("dve", lambda e: e.tensor_tensor(P[:], P[:], mk[:].unsqueeze(1).to_broadcast([128, 4, 128]), ALU.mult),
                         reads=[P.b, mk.b], writes=[P.b])
                for j in range(4):
                    S.op("pe", lambda e: e.matmul(pO[:, 65 * j:65 * j + 65], P[:, j, :], A.vcmp1[:, ct, g, :],
                                                  start=(ct == 0 and j == 0), stop=(ct == nct - 1), skip_group_check=True),
                         reads=[P.b, A.vcmp1.b], writes=[pO.b])
                    S.op("pe", lambda e: e.matmul(pI[:, 128 * j:128 * j + 128], P[:, j, :], C2S[:, ct, :],
                                                  start=(ct == 0 and j == 0), stop=(ct == nct - 1), skip_group_check=True),
                         reads=[P.b, C2S.b], writes=[pI.b])
            branch_out(g, 0, True)
            S.op("dve", lambda e: e.tensor_scalar(imp[:], pI[:, 0:128], rl4[:, 0:1], None, ALU.mult),
                 reads=[pI.b, rl4.b], writes=[imp.b])
            for j in range(1, 4):
                S.op("dve", lambda e: e.scalar_tensor_tensor(imp[:], pI[:, 128 * j:128 * j + 128], rl4[:, j:j + 1], imp[:],
                                                             ALU.mult, ALU.add), reads=[pI.b, rl4.b, imp.b], writes=[imp.b])
            S.op("dve", lambda e: e.tensor_scalar(curc[:], hi64[:], float(2 * ti), None, ALU.add), reads=[hi64.b], writes=[curc.b])
            S.op("dve", lambda e: e.tensor_scalar(vm[:], jidx[:], curc[:, 0:1], None, ALU.is_le), reads=[jidx.b, curc.b], writes=[vm.b])
            S.op("dve", lambda e: e.tensor_scalar(fm[:], jidx[:], A.off64, None, ALU.is_ge), reads=[jidx.b, metat.b], writes=[fm.b])
            S.op("dve", lambda e: e.tensor_tensor(vm[:], vm[:], fm[:], ALU.mult), reads=[vm.b, fm.b], writes=[vm.b])
            S.op("dve", lambda e: e.tensor_scalar(fm[:], jidx[:], curc[:, 0:1], None, ALU.is_equal), reads=[jidx.b, curc.b], writes=[fm.b])
            S.op("dve", lambda e: e.tensor_scalar(imw[:], jidx[:], A.off64, None, ALU.is_equal), reads=[jidx.b, metat.b], writes=[imw.b])
            S.op("dve", lambda e: e.tensor_tensor(fm[:], fm[:], imw[:], ALU.max), reads=[fm.b, imw.b], writes=[fm.b])
            S.op("dve", lambda e: e.tensor_tensor(imp[:], imp[:], vm[:], ALU.mult), reads=[imp.b, vm.b], writes=[imp.b])
            S.op("dve", lambda e: e.tensor_scalar(vm[:], vm[:], 1.0e9, -1.0e9, ALU.mult, ALU.add), reads=[vm.b], writes=[vm.b])
            S.op("dve", lambda e: e.tensor_tensor(imp[:], imp[:], vm[:], ALU.add), reads=[imp.b, vm.b], writes=[imp.b])
            S.op("dve", lambda e: e.scalar_tensor_tensor(imp[:], fm[:], 1.0e30, imp[:], ALU.mult, ALU.max), reads=[imp.b, fm.b], writes=[imp.b])
            S.op("dve", lambda e: e.max(mx8[:], imp[:]), reads=[imp.b], writes=[mx8.b])
            S.op("dve", lambda e: e.match_replace(imw[:], mx8[:], imp[:], -3.0e38), reads=[imp.b, mx8.b], writes=[imw.b])
            S.op("dve", lambda e: e.max(mx8[:], imw[:]), reads=[imw.b], writes=[mx8.b])
            S.op("dve", lambda e: e.tensor_scalar(seln[:], imp[:], mx8[:, 7:8], NEGB, ALU.is_lt, ALU.mult),
                 reads=[imp.b, mx8.b], writes=[seln.b])
            S.op("pe", lambda e: e.transpose(pT[:, 5, :], seln[:], ident[:]), reads=[seln.b, ident.b], writes=[pT.b])
            S.op("dve", lambda e: e.tensor_copy(selnT[g][:], pT[:, 5, :].unsqueeze(1).to_broadcast([128, 4, 128])),
                 reads=[pT.b], writes=[selnT[g].b])
        for g in range(2):
            gs = slice(64 * g, 64 * g + 64)
            for kt in range(ti + 1):
                if g == 0 or True:
                    S.op("dve", lambda e: e.tensor_scalar(Ekt[:], BLK[:], float(2 * kt), None, ALU.is_equal),
                         reads=[BLK.b], writes=[Ekt.b])
                pp = nextpp()
                S.op("pe", lambda e: e.matmul(pp[:, :], A.ksT[:, kt, :], QrTz[g][:, :, :].rearrange("p j t -> p (j t)"),
                                              start=True, stop=False), reads=[A.ksT.b, QrTz[g].b], writes=[pp.b])
                last = kt != ti
                S.op("pe", lambda e: e.matmul(pp[:, :], Ekt[:], selnT[g][:].rearrange("p j t -> p (j t)"),
                                              start=False, stop=last), reads=[Ekt.b, selnT[g].b], writes=[pp.b])
                if kt == ti:
                    S.op("pe", lambda e: e.matmul(pp[:, :], ident[:], causneg[:].rearrange("p j t -> p (j t)"),
                                                  start=False, stop=True), reads=[ident.b, causneg.b], writes=[pp.b])
                P = nextPt()
                S.op("act", lambda e: e.activation(P[:].rearrange("p j t -> p (j t)"), pp[:, :], AF.Exp,
                                                   bias=A.padk(kt), scale=0.125),
                     reads=[pp.b, A.padb], writes=[P.b])
                for j in range(4):
                    S.op("pe", lambda e: e.matmul(pO[:, 65 * j:65 * j + 65], P[:, j, :], A.vs1[:, kt, g, :],
                                                  start=(kt == 0 and j == 0), stop=(kt == ti), skip_group_check=True),
                         reads=[P.b, A.vs1.b], writes=[pO.b])
            branch_out(g, 1, False)
        for g in range(2):
            gs = slice(64 * g, 64 * g + 64)
            kts = [kt for kt in range(ti - 4, ti + 1) if kt >= 0]
            for kt in kts:
                pp = nextpp()
                extra = (kt == ti) or (kt == ti - 4)
                S.op("pe", lambda e: e.matmul(pp[:, :], A.kwT[:, kt % 8, :], QrTz[g][:, :, :].rearrange("p j t -> p (j t)"),
                                              start=True, stop=not extra), reads=[A.kwT.b, QrTz[g].b], writes=[pp.b])
                if extra:
                    mneg = causneg if kt == ti else winneg
                    S.op("pe", lambda e: e.matmul(pp[:, :], ident[:], mneg[:].rearrange("p j t -> p (j t)"),
                                                  start=False, stop=True), reads=[ident.b, mneg.b], writes=[pp.b])
                P = nextPt()
                S.op("act", lambda e: e.activation(P[:].rearrange("p j t -> p (j t)"), pp[:, :], AF.Exp,
                                                   bias=A.padk(kt), scale=0.125),
                     reads=[pp.b, A.padb], writes=[P.b])
                for j in range(4):
                    S.op("pe", lambda e: e.matmul(pO[:, 65 * j:65 * j + 65], P[:, j, :], A.vw1[:, kt % 8, g, :],
                                                  start=(kt == kts[0] and j == 0), stop=(kt == ti), skip_group_check=True),
                         reads=[P.b, A.vw1.b], writes=[pO.b])
            branch_out(g, 2, False)
        S.op("act", lambda e: e.copy(attnb[:, 512:1024], onsa[:].rearrange("p h d -> p (h d)")), reads=[onsa.b], writes=[attnb.b])

    with ExitStack() as es2:
        def sb2(name, shape, dt=F32):
            return T(es2.enter_context(nc.sbuf_tensor(name, list(shape), dt)))
        ksT = sb2("ksT", [128, 64, 128], BF16)
        vs1 = sb2("vs1", [128, 64, 2, 65], BF16)
        kwT = sb2("kwT", [128, 8, 128], BF16)
        vw1 = sb2("vw1", [128, 8, 2, 65], BF16)
        clo = [sb2(f"clo{a}", [128, 520]) for a in range(2)]
        chi = [sb2(f"chi{a}", [128, 520]) for a in range(2)]
        kcmpT = sb2("kcmpT", [128, 512], BF16)
        vcmpT = sb2("vcmpT", [128, 512], BF16)
        vcmp1 = sb2("vcmp1", [128, 4, 2, 65], BF16)
        S.op("pool", lambda e: e.memset(vs1[:], 1.0), writes=[vs1.b])
        S.op("pool", lambda e: e.memset(vw1[:], 1.0), writes=[vw1.b])
        S.op("pool", lambda e: e.memset(vcmp1[:], 1.0), writes=[vcmp1.b])
        S.op("pool", lambda e: e.memset(kcmpT[:], 0.0), writes=[kcmpT.b])
        S.op("pool", lambda e: e.memset(vcmpT[:], 0.0), writes=[vcmpT.b])
        for a in range(2):
            S.op("pool", lambda e: e.memset(clo[a][:], 0.0), writes=[clo[a].b])
            S.op("pool", lambda e: e.memset(chi[a][:], 0.0), writes=[chi[a].b])
        S.op("dve", lambda e: e.memset(Sst[:], 0.0), writes=[Sst.b])

        class NS:
            pass
        A = NS()
        A.ksT, A.vs1, A.kwT, A.vw1, A.clo, A.chi, A.kcmpT, A.vcmpT, A.vcmp1 = ksT, vs1, kwT, vw1, clo, chi, kcmpT, vcmpT, vcmp1
        A.padk = lambda kt: padk[:, kt:kt + 1]
        A.padc = lambda ct: padc[:, ct:ct + 1]
        A.padb = padk.b
        A.off64 = metat[:, 2:3]

        for ti in range(int(os.environ.get('KT_END', NT))):
            full = ti >= int(os.environ.get('KFULL', 32))
            xbuf = xt[ti % 2]
            project_tile(ti, xe_v[ti], xbuf)
            r = rows
            if ti >= 48:
                t0 = (ti - 48) * 128
                S.dma("sp", o_cmp_p[t0:t0 + 128, :], r[:, 0:256], reads=[r.b])
                S.dma("sp", o_slc_p[t0:t0 + 128, :], r[:, 256:512], reads=[r.b])
            if ti >= 60:
                t0 = (ti - 60) * 128
                S.dma("sp", o_win_p[t0:t0 + 128, :], r[:, 512:768], reads=[r.b])
            KSTOP = int(os.environ.get('KSTOP', 99))
            if KSTOP <= 1:
                continue
            kside(A, ti, r)
            if KSTOP <= 4:
                continue
            gla_gates(64, 2)
            if KSTOP <= 5:
                continue
            if full:
                gla_qk(64)
                if KSTOP <= 6:
                    continue
                gla_intra(64)
                if KSTOP <= 7:
                    continue
                for hf in range(2):
                    S.op("act", lambda e: e.copy(qz[0][hf][:, 0:64], qtT[hf][:, 0:64]), reads=[qtT[hf].b], writes=[qz[0][hf].b])
                    S.op("act", lambda e: e.copy(qz[1][hf][:, 64:128], qtT[hf][:, 64:128]), reads=[qtT[hf].b], writes=[qz[1][hf].b])
            for j in range(2):
                if full:
                    for h in range(4):
                        hf, hh = h // 2, h % 2
                        S.op("dve", lambda e: e.tensor_copy(Sbz[j][h][64 * hh:64 * hh + 64, :], Sst[:, h, :]),
                             reads=[Sst.b], writes=[Sbz[j][h].b])
                cm = lo64 if j == 0 else hi64
                S.op("dve", lambda e: e.tensor_scalar(khz[j][:], khat[:], cm[:, 0:1], None, ALU.mult),
                     reads=[khat.b, cm.b], writes=[khz[j].b])
                for h in range(4):
                    S.op("pe", lambda e: e.matmul(pS[0:64, h * 128:(h + 1) * 128],
                                                  khz[j][:, 64 * h:64 * h + 64],
                                                  vb[:, 128 * h:128 * h + 128], start=True, stop=True),
                         reads=[khz[j].b, vb.b], writes=[pS.b])
                S.op("dve", lambda e: e.tensor_tensor(Sst[:], Sst[:], dec[:, :, j:j + 1].to_broadcast([64, 4, 128]), ALU.mult),
                     reads=[Sst.b, dec.b], writes=[Sst.b])
                S.op("dve", lambda e: e.tensor_tensor(Sst[:], Sst[:], pS[0:64, :].rearrange("p (h v) -> p h v", h=4), ALU.add),
                     reads=[Sst.b, pS.b], writes=[Sst.b])
            if not full:
                continue
            if KSTOP <= 9:
                continue
            for h in range(4):
                hf, hh = h // 2, h % 2
                S.op("pe", lambda e: e.matmul(pO[:, h * 128:(h + 1) * 128], ATb[:, h, :], vb[:, 128 * h:128 * h + 128],
                                              start=True, stop=False), reads=[ATb.b, vb.b], writes=[pO.b])
                for j in range(2):
                    S.op("pe", lambda e: e.matmul(pO[:, h * 128:(h + 1) * 128], qz[j][hf][:, :],
                                                  Sbz[j][h][:, :], start=False, stop=(j == 1)),
                         reads=[qz[j][hf].b, Sbz[j][h].b], writes=[pO.b])
            if KSTOP <= 10:
                continue
            gla_post(attnb[:, 0:512], [attnb.b])

            if KSTOP <= 11:
                continue
            nsa_attend(A, ti)
            S.dma("sp", attn0[ti - 32], attnb[:], reads=[attnb.b])
            if ti >= 48:
                S.op("dve", lambda e: e.tensor_copy(dbgt[:], attnb[:]), reads=[attnb.b], writes=[dbgt.b])
                S.dma("sp", dbg_attn[(ti - 48) * 128:(ti - 47) * 128, :], dbgt[:], reads=[dbgt.b])
        S.dma("sp", o_gla_p.rearrange("h k v -> k h v"), Sst[:], reads=[Sst.b])
        S.finish()

    def make_rows(ti):
        r = rows
        S.op("act", lambda e: e.copy(r[:], proj[:, 2064:2832]), reads=[proj.b], writes=[r.b])
        for (kidx, gidx) in ((0, 1), (2, 2), (4, 3)):
            src = proj[:, 2064 + kidx * 128: 2064 + (kidx + 1) * 128].rearrange("p (h d) -> p h d", d=64)
            dst = r[:, kidx * 128:(kidx + 1) * 128].rearrange("p (h d) -> p h d", d=64)
            qknorm(src, 2, gidx, dst, [proj.b], [r.b])
            if kidx > 0:
                rope(dst, 2, ti, [r.b])

    with ExitStack() as es3:
        stk.append(es3)
        Ssm = sb("Ssm", [64, 64, 128])
        S.dma("sp", Ssm[:], state_gla.rearrange("s h k v -> k (s h) v"), writes=[Ssm.b])
        project_tile(NT, xs[:, :], xt[0])
        r = rows
        S.dma("sp", sproj_d[:, :], proj[:], reads=[proj.b])
        S.dma("sp", o_cmp_s[:, :], r[0:64, 0:256], reads=[r.b])
        S.dma("sp", o_slc_s[:, :], r[0:64, 256:512], reads=[r.b])
        for sq in range(16):
            S.dma("sp", o_win_s[sq, 508:512, :], r[4 * sq:4 * sq + 4, 512:768], reads=[r.b])
        gla_gates(4, 16)
        gla_qk(4)
        gla_intra(4)
        mcol = sb("mcol", [128, 128], BF16)
        qms = [sb(f"qms{h}", [128, 128], BF16) for h in range(2)]
        for h in range(4):
            S.op("pe", lambda e: e.matmul(pO[:, h * 128:(h + 1) * 128], ATb[:, h, :], vb[:, 128 * h:128 * h + 128],
                                          start=(h == 0), stop=False, skip_group_check=True), reads=[ATb.b, vb.b], writes=[pO.b])
        for sq in range(16):
            S.op("dve", lambda e: e.tensor_scalar(mcol[:], Cj4[:], float(sq), None, ALU.is_equal), reads=[Cj4.b], writes=[mcol.b])
            for hf in range(2):
                S.op("dve", lambda e: e.tensor_tensor(qms[hf][:], qtT[hf][:], mcol[:], ALU.mult), reads=[qtT[hf].b, mcol.b], writes=[qms[hf].b])
            for h in range(4):
                hf, hh = h // 2, h % 2
                S.op("dve", lambda e: e.tensor_copy(Sbz[0][h][64 * hh:64 * hh + 64, :], Ssm[:, 4 * sq + h, :]),
                     reads=[Ssm.b], writes=[Sbz[0][h].b])
                S.op("pe", lambda e: e.matmul(pO[:, h * 128:(h + 1) * 128], qms[hf][:, :], Sbz[0][h][:, :],
                                              start=False, stop=(sq == 15), skip_group_check=True),
                     reads=[qms[hf].b, Sbz[0][h].b], writes=[pO.b])
        gla_post(attnb[:, 0:512], [attnb.b])
        S.dma("sp", attn0[32][0:64, 0:512], attnb[0:64, 0:512], reads=[attnb.b])
        S.op("dve", lambda e: e.tensor_copy(dbgt[:, 0:512], attnb[:, 0:512]), reads=[attnb.b], writes=[dbgt.b])
        S.dma("sp", dbg_as[:, 0:512], dbgt[0:64, 0:512], reads=[dbgt.b])
        for sq in range(16):
            S.op("dve", lambda e: e.tensor_scalar(khm[:], khat[0:64, :], Ci[4][0:64, sq:sq + 1], None, ALU.mult),
                 reads=[khat.b, Ci[4].b], writes=[khm.b])
            for h in range(4):
                S.op("pe", lambda e: e.matmul(pS[0:64, h * 128:(h + 1) * 128], khm[:, 64 * h:64 * h + 64],
                                              vb[0:64, 128 * h:128 * h + 128], start=True, stop=True),
                     reads=[khm.b, vb.b], writes=[pS.b])
            sv = Ssm[:, 4 * sq:4 * sq + 4, :]
            S.op("dve", lambda e: e.tensor_tensor(sv, sv, dec[:, :, sq:sq + 1].to_broadcast([64, 4, 128]), ALU.mult),
                 reads=[Ssm.b, dec.b], writes=[Ssm.b])
            S.op("dve", lambda e: e.tensor_tensor(sv, sv, pS[0:64, :].rearrange("p (h v) -> p h v", h=4), ALU.add),
                 reads=[Ssm.b, pS.b], writes=[Ssm.b])
        S.dma("sp", o_gla_s.rearrange("s h k v -> k (s h) v"), Ssm[:], reads=[Ssm.b])
        S.dma("sp", o_win_s[:, 0:508, :], cache_win[:, 4:512, :])
        S.finish()
        stk.pop()

    with ExitStack() as es4:
        stk.append(es4)
        As = type("NS", (), {})()
        As.ksT = sb("ksTs", [128, 17, 128], BF16)
        As.vs1 = sb("vs1s", [128, 17, 2, 65], BF16)
        As.kwT = sb("kwTs", [128, 8, 128], BF16)
        As.vw1 = sb("vw1s", [128, 8, 2, 65], BF16)
        As.clo = [sb(f"clos{a}", [128, 520]) for a in range(2)]
        As.chi = [sb(f"chis{a}", [128, 520]) for a in range(2)]
        As.kcmpT = sb("kcmpTs", [128, 512], BF16)
        As.vcmpT = sb("vcmpTs", [128, 512], BF16)
        As.vcmp1 = sb("vcmp1s", [128, 4, 2, 65], BF16)
        zc = sb("zc", [128, 1])
        S.op("dve", lambda e: e.memset(zc[:], 0.0), writes=[zc.b])
        As.padk = lambda kt: zc[:, 0:1]
        As.padc = lambda ct: zc[:, 0:1]
        As.padb = zc.b
        As.off64 = zc[:, 0:1]
        S.op("pool", lambda e: e.memset(As.vs1[:], 1.0), writes=[As.vs1.b])
        S.op("pool", lambda e: e.memset(As.vw1[:], 1.0), writes=[As.vw1.b])
        S.op("pool", lambda e: e.memset(As.vcmp1[:], 1.0), writes=[As.vcmp1.b])
        S.op("pool", lambda e: e.memset(As.kcmpT[:], 0.0), writes=[As.kcmpT.b])
        S.op("pool", lambda e: e.memset(As.vcmpT[:], 0.0), writes=[As.vcmpT.b])
        S.op("pool", lambda e: e.memset(As.kwT[:], 0.0), writes=[As.kwT.b])
        for a in range(2):
            S.op("pool", lambda e: e.memset(As.clo[a][:], 0.0), writes=[As.clo[a].b])
            S.op("pool", lambda e: e.memset(As.chi[a][:], 0.0), writes=[As.chi[a].b])
        ptab_i = sb("ptab_i", [128, 256], I32)
        ptf = sb("ptf", [128, 256])
        idxu = sb("idxu", [128, 256], mybir.dt.uint32)
        S.dma("sp", ptab_i[:], page_tab.rearrange("s i -> (s i)").partition_broadcast(128), writes=[ptab_i.b])
        S.op("dve", lambda e: e.tensor_copy(ptf[:], ptab_i[:]), reads=[ptab_i.b], writes=[ptf.b])
        S.op("dve", lambda e: e.tensor_scalar(ptf[:], ptf[:], 128.0, pidx[:, 0:1], ALU.mult, ALU.add), reads=[ptf.b, pidx.b], writes=[ptf.b])
        S.op("dve", lambda e: e.tensor_copy(idxu[:], ptf[:]), reads=[ptf.b], writes=[idxu.b])
        poolc = pool_c.rearrange("n p c -> (n p) c")
        pools = pool_s.rearrange("n p c -> (n p) c")
        for sq in range(int(os.environ.get('KSEQ', 16))):
            for lt in range(16):
                j = sq * 16 + lt
                S.dma("pool", None, None, reads=[idxu.b], writes=[rows.b],
                      fn=lambda e: e.indirect_dma_start(rows[:, 0:256], None, poolc,
                                                        bass.IndirectOffsetOnAxis(ap=idxu[:, j:j + 1], axis=0)))
                S.dma("pool", None, None, reads=[idxu.b], writes=[rows.b],
                      fn=lambda e: e.indirect_dma_start(rows[:, 256:512], None, pools,
                                                        bass.IndirectOffsetOnAxis(ap=idxu[:, j:j + 1], axis=0)))
                if lt >= 12:
                    S.dma("sp", rows[:, 512:768], cache_win[sq, (lt - 12) * 128:(lt - 11) * 128, :], writes=[rows.b])
                kside(As, lt, rows)
            S.op("pool", lambda e: e.memset(proj[:], 0.0), writes=[proj.b])
            S.dma("sp", proj[0:4, :], sproj_d[4 * sq:4 * sq + 4, :], reads=[proj.b], writes=[proj.b])
            make_rows(NT)
            kside(As, 16, rows)
            nsa_attend(As, 16)
            S.dma("sp", attn0[32][4 * sq:4 * sq + 4, 512:1024], attnb[0:4, 512:1024], reads=[attnb.b])
            S.op("dve", lambda e: e.tensor_copy(dbgt[0:4, 512:1024], attnb[0:4, 512:1024]), reads=[attnb.b], writes=[dbgt.b])
            S.dma("sp", dbg_as[4 * sq:4 * sq + 4, 512:1024], dbgt[0:4, 512:1024], reads=[dbgt.b])
        S.finish()
        stk.pop()

    S.finish()
    stk.pop()
    esA.close()

    def channel_phase(layer, KA, w_out_d, tiles):
        with ExitStack() as esB:
            stk.append(esB)
            KC = KA // 128
            wout = sb("wout", [128, KC, D], BF16)
            wgate = sb("wgate", [128, 8, D], BF16)
            wproj = sb("wproj", [128, 2, D], BF16)
            for c in range(KC):
                S.dma("pool", wout[:, c, :], w_out_d[c * 128:(c + 1) * 128, :], writes=[wout.b])
            for c in range(8):
                S.dma("pool", wgate[:, c, :], ple_w_gate[layer, c * 128:(c + 1) * 128, :], writes=[wgate.b])
            for c in range(2):
                S.dma("pool", wproj[:, c, :], ple_w_proj[layer, c * 128:(c + 1) * 128, :], writes=[wproj.b])
            gml = sb("gml", [128, 8])
            gpl = sb("gpl", [128, 8])
            S.dma("sp", gml[:], norm_mlp[layer].rearrange("(c p) -> p c", p=128), writes=[gml.b], allow_slow_non_contiguous=True)
            S.dma("sp", gpl[:], norm_ple[layer].rearrange("(c p) -> p c", p=128), writes=[gpl.b], allow_slow_non_contiguous=True)
            w1s = [sb(f"w1s{i}", [128, 8, 512], BF16) for i in range(2)]
            w2s = [sb(f"w2s{i}", [128, 4, D], BF16) for i in range(2)]
            hmid = sb("hmid", [128, 4, D])
            hng = sb("hng", [128, 8, 512], BF16)
            aTs = [sb(f"aTs{i}", [128, 4, 512], BF16) for i in range(2)]
            xin = [sb(f"xin{i}", [128, D]) for i in range(2)]
            ain = [sb(f"ain{i}", [128, KA], BF16) for i in range(2)]
            aT = sb("aT", [128, KC, 128], BF16)
            junk = sb("junk", [128, D])
            ssb = sb("ssb", [128, 1])
            xnb = sb("xnb", [128, D], BF16)
            hnl = sb("hnl", [128, 8, 128], BF16)
            rlu = sb("rlu", [128, 512])
            gate = sb("gate", [128, D])
            pin = sb("pin", [128, 256])
            pinb = sb("pinb", [128, 256], BF16)
            pTl = sb("pTl", [128, 2, 128], BF16)
            hout = sb("hout", [128, D])
            w1v = mlp_w1[layer].rearrange("(c p) f -> p c f", p=128)
            w2v = mlp_w2[layer].rearrange("(c p) n -> p c n", p=128)

            def norm_T(src_ap, src_b, gvec, dstT_ap, dst_b):
                S.op("act", lambda e: e.activation(junk[:], src_ap, AF.Square, accum_out=ssb[:]),
                     reads=[src_b], writes=[junk.b, ssb.b])
                S.op("act", lambda e: e.activation(ssb[:], ssb[:], AF.Sqrt, bias=epsc[:, 0:1], scale=1.0 / D),
                     reads=[ssb.b, epsc.b], writes=[ssb.b])
                S.op("dve", lambda e: e.reciprocal(ssb[:], ssb[:]), reads=[ssb.b], writes=[ssb.b])
                S.op("dve", lambda e: e.tensor_scalar(xnb[:], src_ap, ssb[:, 0:1], None, ALU.mult),
                     reads=[src_b, ssb.b], writes=[xnb.b])
                for c in range(8):
                    S.op("pe", lambda e: e.transpose(pT[:, c, :], xnb[:, c * 128:(c + 1) * 128], ident[:]),
                         reads=[xnb.b, ident.b], writes=[pT.b])
                S.op("dve", lambda e: e.tensor_tensor(dstT_ap, pT[:], gvec[:].unsqueeze(2).to_broadcast([128, 8, 128]), ALU.mult),
                     reads=[pT.b, gvec.b], writes=[dst_b])

            wi = 0
            for g0 in range(0, len(tiles), 4):
                grp = tiles[g0:g0 + 4]
                ng = len(grp)
                NTK = ng * 128
                for tt, (x_src, a_src, p_src, dsts) in enumerate(grp):
                    xb = xin[tt % 2]
                    ab = ain[tt % 2]
                    S.dma("sp", xb[:], x_src, writes=[xb.b])
                    S.dma("sp", ab[:], a_src, writes=[ab.b])
                    for c in range(KC):
                        S.op("pe", lambda e: e.transpose(pT[:, c, :], ab[:, c * 128:(c + 1) * 128], ident[:]),
                             reads=[ab.b, ident.b], writes=[pT.b])
                    S.op("act", lambda e: e.copy(aT[:], pT[:, 0:KC, :]), reads=[pT.b], writes=[aT.b])
                    for half in range(2):
                        pp = nextpp()
                        for c in range(KC):
                            S.op("pe", lambda e: e.matmul(pp[:, :], aT[:, c, :], wout[:, c, half * 512:(half + 1) * 512],
                                                          start=(c == 0), stop=(c == KC - 1)),
                                 reads=[aT.b, wout.b], writes=[pp.b])
                        S.op("dve", lambda e: e.tensor_tensor(hmid[:, tt, half * 512:(half + 1) * 512], pp[:, :],
                                                              xb[:, half * 512:(half + 1) * 512], ALU.add),
                             reads=[pp.b, xb.b], writes=[hmid.b])
                    norm_T(hmid[:, tt, :], hmid.b, gml, hng[:, :, tt * 128:(tt + 1) * 128], hng.b)
                for f in range(8):
                    wa, wb = w1s[wi % 2], w2s[wi % 2]
                    at = aTs[wi % 2]
                    wi += 1
                    for c in range(8):
                        S.dma("pool", wa[:, c, :], w1v[:, c, f * 512:(f + 1) * 512], writes=[wa.b])
                    for c in range(4):
                        S.dma("pool", wb[:, c, :], w2v[:, 4 * f + c, :], writes=[wb.b])
                    for fc in range(4):
                        pp = nextpp()
                        for c in range(8):
                            S.op("pe", lambda e: e.matmul(pp[:, 0:NTK], wa[:, c, fc * 128:(fc + 1) * 128], hng[:, c, 0:NTK],
                                                          start=(c == 0), stop=(c == 7)),
                                 reads=[wa.b, hng.b], writes=[pp.b])
                        S.op("act", lambda e: e.activation(rlu[:, 0:NTK], pp[:, 0:NTK], AF.Relu), reads=[pp.b], writes=[rlu.b])
                        S.op("dve", lambda e: e.tensor_tensor(at[:, fc, 0:NTK], rlu[:, 0:NTK], rlu[:, 0:NTK], ALU.mult),
                             reads=[rlu.b], writes=[at.b])
                    for tt in range(ng):
                        for half in range(2):
                            pp = nextpp()
                            for fc in range(4):
                                S.op("pe", lambda e: e.matmul(pp[:, :], at[:, fc, tt * 128:(tt + 1) * 128],
                                                              wb[:, fc, half * 512:(half + 1) * 512],
                                                              start=(fc == 0), stop=(fc == 3)),
                                     reads=[at.b, wb.b], writes=[pp.b])
                            hv = hmid[:, tt, half * 512:(half + 1) * 512]
                            S.op("dve", lambda e: e.tensor_tensor(hv, hv, pp[:, :], ALU.add), reads=[hmid.b, pp.b], writes=[hmid.b])
                for tt, (x_src, a_src, p_src, dsts) in enumerate(grp):
                    norm_T(hmid[:, tt, :], hmid.b, gpl, hnl[:], hnl.b)
                    S.dma("sp", pin[:], p_src, writes=[pin.b])
                    S.op("act", lambda e: e.copy(pinb[:], pin[:]), reads=[pin.b], writes=[pinb.b])
                    for c in range(2):
                        S.op("pe", lambda e: e.transpose(pT[:, c, :], pinb[:, c * 128:(c + 1) * 128], ident[:]),
                             reads=[pinb.b, ident.b], writes=[pT.b])
                    S.op("act", lambda e: e.copy(pTl[:], pT[:, 0:2, :]), reads=[pT.b], writes=[pTl.b])
                    for half in range(2):
                        hs = slice(half * 512, (half + 1) * 512)
                        pg = nextpp()
                        for c in range(8):
                            S.op("pe", lambda e: e.matmul(pg[:, :], hnl[:, c, :], wgate[:, c, hs], start=(c == 0), stop=(c == 7)),
                                 reads=[hnl.b, wgate.b], writes=[pg.b])
                        S.op("act", lambda e: e.activation(gate[:, hs], pg[:, :], AF.Exp, scale=-1.0), reads=[pg.b], writes=[gate.b])
                        S.op("dve", lambda e: e.tensor_scalar(gate[:, hs], gate[:, hs], 1.0, None, ALU.add), reads=[gate.b], writes=[gate.b])
                        S.op("dve", lambda e: e.reciprocal(gate[:, hs], gate[:, hs]), reads=[gate.b], writes=[gate.b])
                        pq = nextpp()
                        for c in range(2):
                            S.op("pe", lambda e: e.matmul(pq[:, :], pTl[:, c, :], wproj[:, c, hs], start=(c == 0), stop=(c == 1)),
                                 reads=[pTl.b, wproj.b], writes=[pq.b])
                        S.op("dve", lambda e: e.tensor_tensor(gate[:, hs], gate[:, hs], pq[:, :], ALU.mult), reads=[gate.b, pq.b], writes=[gate.b])
                        S.op("dve", lambda e: e.tensor_tensor(hout[:, hs], hmid[:, tt, hs], gate[:, hs], ALU.add),
                             reads=[hmid.b, gate.b], writes=[hout.b])
                    for (dst, r0, r1) in dsts:
                        S.dma("sp", dst, hout[r0:r1, :], reads=[hout.b])
            S.finish()
            stk.pop()

    tiles0 = []
    for ti in range(32, 64):
        dsts = [(h1s[ti - 32], 0, 128)]
        if ti >= 48:
            dsts.append((dbg_h1[(ti - 48) * 128:(ti - 47) * 128, :], 0, 128))
        tiles0.append((xe_v[ti], attn0[ti - 32], pe0[ti - 32], dsts))
    tiles0.append((xs[:, :], attn0[32], pe0[32], [(h1s[32], 0, 128), (dbg_h1s[:, :], 0, 128)]))
    channel_phase(0, 1024, a_w_out, tiles0)

    WG = (128, 512, 2048)
    DG = (1, 4, 16)
    with ExitStack() as esC:
        stk.append(esC)
        cwin = sb("cwin", [128, 8, 4608], BF16)
        cwv = c_w_in.rearrange("(c p) n -> p c n", p=128)
        for c in range(8):
            for hh_ in range(2):
                S.dma("pool", cwin[:, c, hh_ * 2304:(hh_ + 1) * 2304], cwv[:, c, hh_ * 2304:(hh_ + 1) * 2304], writes=[cwin.b])
        cgq = sb("cgq", [128, 6, 64])
        S.dma("sp", cgq[:], c_g_qk.rearrange("a d -> (a d)").partition_broadcast(128), writes=[cgq.b])
        hin = [sb(f"hin{i}", [128, D]) for i in range(2)]
        junkc = sb("junkc", [128, D])
        ssc = sb("ssc", [128, 1])
        xnc = sb("xnc", [128, D], BF16)
        hnc = sb("hnc", [128, 8, 128], BF16)
        zt = sb("zt", [128, 8, 64])
        zb = sb("zb", [128, 512], BF16)
        kvrow = [sb(f"kvrow{i}", [128, 1024]) for i in range(2)]
        sqc = sb("sqc", [128, 8, 64])
        s8 = sb("s8", [128, 8])
        ta8 = sb("ta8", [128, 8, 32])
        tb8 = sb("tb8", [128, 8, 32])

        def qkn_rope(src_psum, pb_, gidx, dst_ap, dst_b):
            S.op("act", lambda e: e.copy(dst_ap, src_psum), reads=[pb_], writes=[dst_b])
            S.op("dve", lambda e: e.tensor_tensor(sqc[:], dst_ap, dst_ap, ALU.mult), reads=[dst_b], writes=[sqc.b])
            S.op("dve", lambda e: e.tensor_reduce(s8[:], sqc[:], AX.X, ALU.add), reads=[sqc.b], writes=[s8.b])
            S.op("act", lambda e: e.activation(s8[:], s8[:], AF.Sqrt, bias=epsc[:, 0:1], scale=1.0 / 64.0), reads=[s8.b, epsc.b], writes=[s8.b])
            S.op("dve", lambda e: e.reciprocal(s8[:], s8[:]), reads=[s8.b], writes=[s8.b])
            S.op("dve", lambda e: e.tensor_tensor(dst_ap, dst_ap, s8[:].unsqueeze(2).to_broadcast([128, 8, 64]), ALU.mult),
                 reads=[dst_b, s8.b], writes=[dst_b])
            S.op("dve", lambda e: e.tensor_tensor(dst_ap, dst_ap, cgq[:, gidx, :].unsqueeze(1).to_broadcast([128, 8, 64]), ALU.mult),
                 reads=[dst_b, cgq.b], writes=[dst_b])
            x1 = dst_ap[:, :, 0:32]
            x2 = dst_ap[:, :, 32:64]
            cs = cs_t[:].unsqueeze(1).to_broadcast([128, 8, 32])
            sn = sn_t[:].unsqueeze(1).to_broadcast([128, 8, 32])
            S.op("dve", lambda e: e.tensor_tensor(ta8[:], x1, sn, ALU.mult), reads=[dst_b, sn_t.b], writes=[ta8.b])
            S.op("dve", lambda e: e.tensor_tensor(tb8[:], x2, sn, ALU.mult), reads=[dst_b, sn_t.b], writes=[tb8.b])
            S.op("dve", lambda e: e.tensor_tensor(x1, x1, cs, ALU.mult), reads=[dst_b, cs_t.b], writes=[dst_b])
            S.op("dve", lambda e: e.tensor_tensor(x2, x2, cs, ALU.mult), reads=[dst_b, cs_t.b], writes=[dst_b])
            S.op("dve", lambda e: e.tensor_tensor(x1, x1, tb8[:], ALU.subtract), reads=[dst_b, tb8.b], writes=[dst_b])
            S.op("dve", lambda e: e.tensor_tensor(x2, x2, ta8[:], ALU.add), reads=[dst_b, ta8.b], writes=[dst_b])

        for idx in range(33):
            sample = idx == 32
            ti = NT if sample else 32 + idx
            own = idx >= 16
            hb_ = hin[idx % 2]
            S.dma("sp", hb_[:], h1s[idx], writes=[hb_.b])
            rope_tables(ti)
            S.op("act", lambda e: e.activation(junkc[:], hb_[:], AF.Square, accum_out=ssc[:]), reads=[hb_.b], writes=[junkc.b, ssc.b])
            S.op("act", lambda e: e.activation(ssc[:], ssc[:], AF.Sqrt, bias=epsc[:, 0:1], scale=1.0 / D), reads=[ssc.b, epsc.b], writes=[ssc.b])
            S.op("dve", lambda e: e.reciprocal(ssc[:], ssc[:]), reads=[ssc.b], writes=[ssc.b])
            S.op("dve", lambda e: e.tensor_scalar(xnc[:], hb_[:], ssc[:, 0:1], None, ALU.mult), reads=[hb_.b, ssc.b], writes=[xnc.b])
            for c in range(8):
                S.op("pe", lambda e: e.transpose(pT[:, c, :], xnc[:, c * 128:(c + 1) * 128], ident[:]), reads=[xnc.b, ident.b], writes=[pT.b])
            S.op("dve", lambda e: e.tensor_tensor(hnc[:], pT[:], gmix[:, 1, :].unsqueeze(2).to_broadcast([128, 8, 128]), ALU.mult),
                 reads=[pT.b, gmix.b], writes=[hnc.b])
            for g in range(3):
                kv = kvrow[g % 2]
                for r_ in range(3):
                    if r_ == 0 and not own:
                        continue
                    c0 = g * 1536 + r_ * 512
                    pp = nextpp()
                    for c in range(8):
                        S.op("pe", lambda e: e.matmul(pp[:, :], hnc[:, c, :], cwin[:, c, c0:c0 + 512], start=(c == 0), stop=(c == 7)),
                             reads=[hnc.b, cwin.b], writes=[pp.b])
                    ppv = pp[:, :].rearrange("p (h d) -> p h d", d=64)
                    if r_ == 0:
                        qkn_rope(ppv, pp.b, 2 * g, zt[:], zt.b)
                        if sample:
                            S.dma("sp", qsd[g], zt[:].rearrange("p h d -> p (h d)"), reads=[zt.b])
                        else:
                            S.op("act", lambda e: e.copy(zb[:], zt[:].rearrange("p h d -> p (h d)")), reads=[zt.b], writes=[zb.b])
                            S.dma("sp", qd[g][(idx - 16) * 128:(idx - 15) * 128, :], zb[:], reads=[zb.b])
                    elif r_ == 1:
                        qkn_rope(ppv, pp.b, 2 * g + 1, kv[:, 0:512].rearrange("p (h d) -> p h d", d=64), kv.b)
                        if not sample:
                            S.op("act", lambda e: e.copy(zb[:], kv[:, 0:512]), reads=[kv.b], writes=[zb.b])
                            S.dma("sp", kd[g][idx * 128:(idx + 1) * 128, :], zb[:], reads=[zb.b])
                    else:
                        S.op("act", lambda e: e.copy(kv[:, 512:1024], pp[:, :]), reads=[pp.b], writes=[kv.b])
                        if not sample:
                            S.op("act", lambda e: e.copy(zb[:], pp[:, :]), reads=[pp.b], writes=[zb.b])
                            S.dma("sp", vd[g][idx * 128:(idx + 1) * 128, :], zb[:], reads=[zb.b])
                if sample:
                    S.dma("sp", ksd[g], kv[:], reads=[kv.b])
                    w_ = WG[g]
                    for sq in range(16):
                        S.dma("sp", o_dil_s[g][sq, w_ - 4:w_, :], kv[4 * sq:4 * sq + 4, :], reads=[kv.b])
                    for sq in range(0, 16, 4):
                        S.dma("sp", o_dil_s[g][sq:sq + 4, 0:w_ - 4, :], cache_dil[g][sq:sq + 4, 4:w_, :])
                else:
                    first = 64 - WG[g] // 128
                    if ti >= first:
                        S.dma("sp", o_dil_p[g][(ti - first) * 128:(ti - first + 1) * 128, :], kv[:], reads=[kv.b])
        S.finish()
        stk.pop()

    with ExitStack() as esD:
        stk.append(esD)
        pidx_i2 = sb("pidx_i2", [128, 128], I32)
        jidx_i2 = sb("jidx_i2", [128, 128], I32)
        pidx2 = sb("pidx2", [128, 128])
        jidx2 = sb("jidx2", [128, 128])
        S.op("pool", lambda e: e.iota(pidx_i2[:], pattern=[[0, 128]], base=0, channel_multiplier=1), writes=[pidx_i2.b])
        S.op("pool", lambda e: e.iota(jidx_i2[:], pattern=[[1, 128]], base=0, channel_multiplier=0), writes=[jidx_i2.b])
        S.op("dve", lambda e: e.tensor_copy(pidx2[:], pidx_i2[:]), reads=[pidx_i2.b], writes=[pidx2.b])
        S.op("dve", lambda e: e.tensor_copy(jidx2[:], jidx_i2[:]), reads=[jidx_i2.b], writes=[jidx2.b])
        mtmp = sb("mtmp", [128, 128])
        mneg = [sb(f"mneg{i}", [128, 4, 128], BF16) for i in range(2)]
        S.op("dve", lambda e: e.tensor_tensor(mtmp[:], pidx2[:], jidx2[:], ALU.is_lt), reads=[pidx2.b, jidx2.b], writes=[mtmp.b])
        S.op("dve", lambda e: e.tensor_scalar(mneg[0][:], mtmp[:].unsqueeze(1).to_broadcast([128, 4, 128]), NEGB, None, ALU.mult),
             reads=[mtmp.b], writes=[mneg[0].b])
        S.op("dve", lambda e: e.tensor_tensor(mtmp[:], pidx2[:], jidx2[:], ALU.is_gt), reads=[pidx2.b, jidx2.b], writes=[mtmp.b])
        S.op("dve", lambda e: e.tensor_scalar(mneg[1][:], mtmp[:].unsqueeze(1).to_broadcast([128, 4, 128]), NEGB, None, ALU.mult),
             reads=[mtmp.b], writes=[mneg[1].b])
        zeroc = sb("zeroc", [128, 1])
        S.op("dve", lambda e: e.memset(zeroc[:], 0.0), writes=[zeroc.b])
        qb_t = sb("qb_t", [128, 512], BF16)
        kb_t = [sb(f"kb_t{i}", [128, 512], BF16) for i in range(2)]
        vb_t = [sb(f"vb_t{i}", [128, 512], BF16) for i in range(2)]
        v1_t = [sb(f"v1_t{i}", [128, 8, 65], BF16) for i in range(2)]
        for i in range(2):
            S.op("pool", lambda e: e.memset(v1_t[i][:], 1.0), writes=[v1_t[i].b])
        qTz = [sb(f"qTz{h}", [128, 128], BF16) for h in range(8)]
        for h in range(8):
            S.op("pool", lambda e: e.memset(qTz[h][:], 0.0), writes=[qTz[h].b])
        kTt = [sb(f"kTt{i}", [128, 4, 128], BF16) for i in range(2)]
        Pd = [sb(f"Pd{i}", [128, 4, 128], BF16) for i in range(2)]
        ores = sb("ores", [128, 520])
        pdi = 0
        for g in range(3):
            dg = DG[g]
            kview = kd[g].rearrange("(i s) c -> s i c", s=dg)
            vview = vd[g].rearrange("(i s) c -> s i c", s=dg)
            qview = qd[g].rearrange("(i s) c -> s i c", s=dg)
            oview = od[g].rearrange("(i s) c -> s i c", s=dg)
            nqb = 2048 // (128 * dg)
            for r_ in range(dg):
                for qb in range(nqb):
                    i0 = 2048 // dg + 128 * qb
                    S.dma("sp", qb_t[:], qview[r_, 128 * qb:128 * qb + 128, :], writes=[qb_t.b])
                    for kb in range(2):
                        ks = i0 - 128 + 128 * kb
                        S.dma("sp", kb_t[kb][:], kview[r_, ks:ks + 128, :], writes=[kb_t[kb].b])
                        S.dma("sp", vb_t[kb][:], vview[r_, ks:ks + 128, :], writes=[vb_t[kb].b])
                        S.op("dve", lambda e: e.tensor_copy(v1_t[kb][:, :, 0:64], vb_t[kb][:].rearrange("p (h d) -> p h d", d=64)),
                             reads=[vb_t[kb].b], writes=[v1_t[kb].b])
                        for hp in range(4):
                            S.op("pe", lambda e: e.transpose(pT[:, hp, :], kb_t[kb][:, hp * 128:(hp + 1) * 128], ident[:]),
                                 reads=[kb_t[kb].b, ident.b], writes=[pT.b])
                        S.op("act", lambda e: e.copy(kTt[kb][:], pT[:, 0:4, :]), reads=[pT.b], writes=[kTt[kb].b])
                    for hp in range(4):
                        S.op("pe", lambda e: e.transpose(pT[:, 4 + hp, :], qb_t[:, hp * 128:(hp + 1) * 128], ident[:]),
                             reads=[qb_t.b, ident.b], writes=[pT.b])
                    for h in range(8):
                        hp, hh = h // 2, h % 2
                        S.op("act", lambda e: e.copy(qTz[h][64 * hh:64 * hh + 64, :], pT[64 * hh:64 * hh + 64, 4 + hp, :]),
                             reads=[pT.b], writes=[qTz[h].b])
                    for hb4 in range(2):
                        pacc = pO if hb4 == 0 else pI
                        for kb in range(2):
                            pp = nextpp()
                            S.op("pe", lambda e: e.matmul(pp[:, :], ident[:], mneg[kb][:].rearrange("p j t -> p (j t)"),
                                                          start=True, stop=False, skip_group_check=True),
                                 reads=[ident.b, mneg[kb].b], writes=[pp.b])
                            for j in range(4):
                                h = 4 * hb4 + j
                                S.op("pe", lambda e: e.matmul(pp[:, 128 * j:128 * j + 128], kTt[kb][:, h // 2, :], qTz[h][:, :],
                                                              start=False, stop=(j == 3), skip_group_check=True),
                                     reads=[kTt[kb].b, qTz[h].b], writes=[pp.b])
                            P = Pd[pdi % 2]
                            pdi += 1
                            bias_ap = metat[:, 3:4] if (kb == 0 and qb == 0) else zeroc[:, 0:1]
                            S.op("act", lambda e: e.activation(P[:].rearrange("p j t -> p (j t)"), pp[:, :], AF.Exp, bias=bias_ap, scale=0.125),
                                 reads=[pp.b, metat.b, zeroc.b], writes=[P.b])
                            for j in range(4):
                                h = 4 * hb4 + j
                                S.op("pe", lambda e: e.matmul(pacc[:, 65 * j:65 * j + 65], P[:, j, :], v1_t[kb][:, h, :],
                                                              start=(kb == 0 and j == 0), stop=(kb == 1), skip_group_check=True),
                                     reads=[P.b, v1_t[kb].b], writes=[pacc.b])
                        S.op("act", lambda e: e.copy(ores[:, 260 * hb4:260 * hb4 + 260], pacc[:, 0:260]), reads=[pacc.b], writes=[ores.b])
                    S.dma("sp", oview[r_, 128 * qb:128 * qb + 128, :], ores[:], reads=[ores.b])
        S.finish()
        o3 = [sb(f"o3_{g}", [128, 8, 65]) for g in range(3)]
        rden = sb("rden", [128, 8])
        ab1 = sb("ab1", [128, 8, 64], BF16)
        for t in range(16):
            for g in range(3):
                S.dma("sp", o3[g][:].rearrange("p h c -> p (h c)"), od[g][t * 128:(t + 1) * 128, :], writes=[o3[g].b])
            S.op("dve", lambda e: e.tensor_tensor(o3[0][:], o3[0][:], o3[1][:], ALU.add), reads=[o3[0].b, o3[1].b], writes=[o3[0].b])
            S.op("dve", lambda e: e.tensor_tensor(o3[0][:], o3[0][:], o3[2][:], ALU.add), reads=[o3[0].b, o3[2].b], writes=[o3[0].b])
            S.op("dve", lambda e: e.reciprocal(rden[:], o3[0][:, :, 64]), reads=[o3[0].b], writes=[rden.b])
            S.op("dve", lambda e: e.tensor_tensor(ab1[:], o3[0][:, :, 0:64], rden[:].unsqueeze(2).to_broadcast([128, 8, 64]), ALU.mult),
                 reads=[o3[0].b, rden.b], writes=[ab1.b])
            S.dma("sp", attn1[t], ab1[:].rearrange("p h d -> p (h d)"), reads=[ab1.b])
        S.finish()
        stk.pop()

    with ExitStack() as esE:
        stk.append(esE)
        e_pi = sb("e_pi", [8, 8], I32)
        e_ji = sb("e_ji", [8, 8], I32)
        e_pf = sb("e_pf", [8, 8])
        e_jf = sb("e_jf", [8, 8])
        S.op("pool", lambda e: e.iota(e_pi[:], pattern=[[0, 8]], base=0, channel_multiplier=1), writes=[e_pi.b])
        S.op("pool", lambda e: e.iota(e_ji[:], pattern=[[1, 8]], base=0, channel_multiplier=0), writes=[e_ji.b])
        S.op("dve", lambda e: e.tensor_copy(e_pf[:], e_pi[:]), reads=[e_pi.b], writes=[e_pf.b])
        S.op("dve", lambda e: e.tensor_copy(e_jf[:], e_ji[:]), reads=[e_ji.b], writes=[e_jf.b])
        eq8 = sb("eq8", [8, 8])
        S.op("dve", lambda e: e.tensor_tensor(eq8[:], e_pf[:], e_jf[:], ALU.is_equal), reads=[e_pf.b, e_jf.b], writes=[eq8.b])
        onesb = sb("onesb", [128, 1], BF16)
        S.op("dve", lambda e: e.memset(onesb[:], 1.0), writes=[onesb.b])
        Kt = [sb(f"sdK{i}", [128, 1024]) for i in range(2)]
        K1 = [sb(f"sdK1{i}", [1, 1024]) for i in range(2)]
        qbc = [sb(f"sdq{i}", [128, 512]) for i in range(2)]
        Vb = sb("sdVb", [128, 512], BF16)
        V1b = sb("sdV1b", [1, 512], BF16)
        prod = sb("sdprod", [128, 8, 64])
        prod1 = sb("sdprod1", [1, 8, 64])
        sc = sb("sdsc", [128, 8])
        sc1 = sb("sdsc1", [1, 8])
        pex = sb("sdpex", [128, 8], BF16)
        pex1 = sb("sdpex1", [1, 8], BF16)
        numacc = sb("sdnum", [8, 8, 64])
        denacc = sb("sdden", [8, 1])
        dsel = sb("sddsel", [8, 8, 64])
        o8 = sb("sdo8", [8, 64])
        o8b = sb("sdo8b", [8, 64], BF16)
        ci = 0
        for sq in range(16):
            for t in range(4):
                row = 4 * sq + t
                for g in range(3):
                    dg = DG[g]
                    kt_, k1_, qb_ = Kt[ci % 2], K1[ci % 2], qbc[ci % 2]
                    ci += 1
                    if g == 0:
                        n_c = 128 - t
                        S.dma("sp", kt_[0:n_c, :], cache_dil[0][sq, t:128, :], writes=[kt_.b])
                        if t > 0:
                            S.dma("sp", kt_[n_c:128, :], ksd[0][4 * sq:4 * sq + t, :], writes=[kt_.b])
                    else:
                        S.dma("sp", kt_[:, :], cache_dil[g][sq].rearrange("(i s) c -> s i c", s=dg)[t, 0:128, :], writes=[kt_.b])
                    S.dma("sp", k1_[:, :], ksd[g][row:row + 1, :], writes=[k1_.b])
                    S.dma("sp", qb_[:, :], qsd[g][row].partition_broadcast(128), writes=[qb_.b])
                    S.op("dve", lambda e: e.tensor_tensor(prod[:], kt_[:, 0:512].rearrange("p (h d) -> p h d", d=64),
                                                          qb_[:].rearrange("p (h d) -> p h d", d=64), ALU.mult),
                         reads=[kt_.b, qb_.b], writes=[prod.b])
                    S.op("dve", lambda e: e.tensor_reduce(sc[:], prod[:], AX.X, ALU.add), reads=[prod.b], writes=[sc.b])
                    S.op("act", lambda e: e.activation(pex[:], sc[:], AF.Exp, scale=0.125), reads=[sc.b], writes=[pex.b])
                    S.op("dve", lambda e: e.tensor_tensor(prod1[:], k1_[0:1, 0:512].rearrange("p (h d) -> p h d", d=64),
                                                          qb_[0:1, :].rearrange("p (h d) -> p h d", d=64), ALU.mult),
                         reads=[k1_.b, qb_.b], writes=[prod1.b])
                    S.op("dve", lambda e: e.tensor_reduce(sc1[:], prod1[:], AX.X, ALU.add), reads=[prod1.b], writes=[sc1.b])
                    S.op("act", lambda e: e.activation(pex1[:], sc1[:], AF.Exp, scale=0.125), reads=[sc1.b], writes=[pex1.b])
                    S.op("act", lambda e: e.copy(Vb[:], kt_[:, 512:1024]), reads=[kt_.b], writes=[Vb.b])
                    S.op("act", lambda e: e.copy(V1b[:], k1_[0:1, 512:1024]), reads=[k1_.b], writes=[V1b.b])
                    pn = nextpp()
                    S.op("pe", lambda e: e.matmul(pn[0:8, :], pex[:, :], Vb[:, :], start=True, stop=False), reads=[pex.b, Vb.b], writes=[pn.b])
                    S.op("pe", lambda e: e.matmul(pn[0:8, :], pex1[:, :], V1b[:, :], start=False, stop=True), reads=[pex1.b, V1b.b], writes=[pn.b])
                    pdn = nextpp()
                    S.op("pe", lambda e: e.matmul(pdn[0:8, 0:1], pex[:, :], onesb[:, :], start=True, stop=False), reads=[pex.b, onesb.b], writes=[pdn.b])
                    S.op("pe", lambda e: e.matmul(pdn[0:8, 0:1], pex1[:, :], onesb[0:1, :], start=False, stop=True), reads=[pex1.b, onesb.b], writes=[pdn.b])
                    if g == 0:
                        S.op("act", lambda e: e.copy(numacc[:].rearrange("p h d -> p (h d)"), pn[0:8, :]), reads=[pn.b], writes=[numacc.b])
                        S.op("act", lambda e: e.copy(denacc[:], pdn[0:8, 0:1]), reads=[pdn.b], writes=[denacc.b])
                    else:
                        S.op("dve", lambda e: e.tensor_tensor(numacc[:].rearrange("p h d -> p (h d)"), numacc[:].rearrange("p h d -> p (h d)"),
                                                              pn[0:8, :], ALU.add), reads=[numacc.b, pn.b], writes=[numacc.b])
                        S.op("dve", lambda e: e.tensor_tensor(denacc[:], denacc[:], pdn[0:8, 0:1], ALU.add), reads=[denacc.b, pdn.b], writes=[denacc.b])
                S.op("dve", lambda e: e.tensor_tensor(dsel[:], numacc[:], eq8[:].unsqueeze(2).to_broadcast([8, 8, 64]), ALU.mult),
                     reads=[numacc.b, eq8.b], writes=[dsel.b])
                S.op("dve", lambda e: e.tensor_reduce(o8[:], dsel[:].rearrange("p h d -> p d h"), AX.X, ALU.add), reads=[dsel.b], writes=[o8.b])
                S.op("dve", lambda e: e.reciprocal(denacc[:], denacc[:]), reads=[denacc.b], writes=[denacc.b])
                S.op("dve", lambda e: e.tensor_scalar(o8b[:], o8[:], denacc[:, 0:1], None, ALU.mult), reads=[o8.b, denacc.b], writes=[o8b.b])
                S.dma("sp", attn1[16][row].rearrange("(h d) -> h d", d=64), o8b[:], reads=[o8b.b])
        S.finish()
        stk.pop()

    tiles1 = []
    for t in range(16):
        tiles1.append((h1s[16 + t], attn1[t], pe1[t], [(y_p[t * 128:(t + 1) * 128, :], 0, 128)]))
    tiles1.append((h1s[32], attn1[16], pe1[16], [(y_s[:, :], 0, 64)]))
    channel_phase(1, 512, c_w_out, tiles1)

    S.finish()
    es.close()
    return nc, S


_PROG = None


def kernel(x_prompt, x_sample, state_gla, cache_nsa_cmp, cache_nsa_slc, cache_nsa_win,
           cache_dil_0, cache_dil_1, cache_dil_2, page_table, p_prompt, p_sample,
           norm_mix, norm_mlp, norm_ple, a_w_in, a_w_out, gla_w_a2, gla_b_a, gla_g_norm,
           nsa_g_qk, nsa_w_phi, nsa_pe, c_w_in, c_g_qk, c_w_out, mlp_w1, mlp_w2,
           ple_w_proj, ple_w_gate):
    global _PROG
    if _PROG is None:
        _PROG = build_program()
    nc, S = _PROG
    f32 = np.float32
    x_prompt = np.asarray(x_prompt, f32)
    x_sample = np.asarray(x_sample, f32)
    in_maps = []
    pool_c_h = np.asarray(cache_nsa_cmp, f32)[0].reshape(2560, 128, 256)
    pool_s_h = np.asarray(cache_nsa_slc, f32)[0].reshape(2560, 128, 256)
    for c in range(NCORES):
        b, seg = c // 4, c % 4
        P0 = seg * SEG
        OFF = NSLOT - (P0 + SEG)
        xe = np.zeros((NSLOT, D), f32)
        xe[OFF:] = x_prompt[b, :P0 + SEG]
        meta = np.zeros((128, 8), f32)
        meta[:, 0] = OFF
        meta[:, 1] = 2048 + (np.arange(128) % 4)
        meta[:, 2] = OFF // 64
        meta[:, 3] = NEGB if OFF >= 6144 else 0.0
        xs = np.zeros((128, D), f32)
        xs[:64] = x_sample[16 * c:16 * c + 16].reshape(64, D)
        pp_ = np.asarray(p_prompt, f32)
        ps_ = np.asarray(p_sample, f32)
        pe0 = np.zeros((33, 128, 256), f32)
        pe1 = np.zeros((17, 128, 256), f32)
        for l, (pe_, nt_) in enumerate(((pe0, 32), (pe1, 16))):
            ext = np.zeros((nt_ * 128, 256), f32)
            lo = P0 + SEG - nt_ * 128
            src = pp_[l, b, max(lo, 0):P0 + SEG]
            ext[nt_ * 128 - len(src):] = src
            pe_[:nt_] = ext.reshape(nt_, 128, 256)
            pe_[nt_, :64] = ps_[l, 16 * c:16 * c + 16].reshape(64, 256)
        in_maps.append({
            "xe": xe, "meta": meta, "xs": xs, "pe0": pe0, "pe1": pe1,
            "cache_dil0": np.ascontiguousarray(np.asarray(cache_dil_0, f32)[0, 16 * c:16 * c + 16]).reshape(16, 128, 1024),
            "cache_dil1": np.ascontiguousarray(np.asarray(cache_dil_1, f32)[0, 16 * c:16 * c + 16]).reshape(16, 512, 1024),
            "cache_dil2": np.ascontiguousarray(np.asarray(cache_dil_2, f32)[0, 16 * c:16 * c + 16]).reshape(16, 2048, 1024),
            "page_tab": np.ascontiguousarray(np.asarray(page_table)[16 * c:16 * c + 16]).astype(np.int32),
            "pool_c": pool_c_h, "pool_s": pool_s_h,
            "c_w_in": np.asarray(c_w_in[0], f32), "c_g_qk": np.asarray(c_g_qk[0], f32).reshape(6, 64),
            "norm_mlp": np.asarray(norm_mlp, f32), "norm_ple": np.asarray(norm_ple, f32),
            "a_w_out": np.asarray(a_w_out[0], f32), "c_w_out": np.asarray(c_w_out[0], f32),
            "mlp_w1": np.asarray(mlp_w1, f32), "mlp_w2": np.asarray(mlp_w2, f32),
            "ple_w_proj": np.asarray(ple_w_proj, f32), "ple_w_gate": np.asarray(ple_w_gate, f32),
            "norm_mix": np.asarray(norm_mix, f32),
            "a_w_in": np.asarray(a_w_in[0], f32),
            "nsa_g_qk": np.asarray(nsa_g_qk[0], f32),
            "gla_w_a2": np.asarray(gla_w_a2[0], f32),
            "gla_b_a": np.asarray(gla_b_a[0], f32),
            "gla_g_norm": np.asarray(gla_g_norm[0], f32),
            "nsa_w_phi": np.asarray(nsa_w_phi[0], f32),
            "nsa_pe": np.asarray(nsa_pe[0], f32),
            "state_gla": np.ascontiguousarray(np.asarray(state_gla, f32)[0, 16 * c:16 * c + 16]),
            "cache_win": np.ascontiguousarray(np.asarray(cache_nsa_win, f32)[0, 16 * c:16 * c + 16]).reshape(16, 512, 256),
        })
    res = run_bass_kernel_spmd(nc, in_maps, core_ids=list(range(NCORES)))
    R = res.results
    global _DBG
    _DBG = R

    def prompt_rows(name):
        out = np.zeros((1, 2, SEQ, 2, 2, 64), f32)
        for c in range(NCORES):
            b, seg = c // 4, c % 4
            out[0, b, seg * SEG:(seg + 1) * SEG] = R[c][name].reshape(SEG, 2, 2, 64)
        return out

    def cat(name, shp):
        return np.concatenate([R[c][name].reshape(shp) for c in range(NCORES)], axis=0)[None]

    cmp_p = prompt_rows("o_cmp_p")
    slc_p = prompt_rows("o_slc_p")
    win_p = np.stack([R[3]["o_win_p"].reshape(512, 2, 2, 64), R[7]["o_win_p"].reshape(512, 2, 2, 64)])[None]
    gla_p = np.stack([R[3]["o_gla_p"], R[7]["o_gla_p"]])[None]
    gla_s = cat("o_gla_s", (16, 4, 64, 128))
    cmp_s = cat("o_cmp_s", (16, 4, 2, 2, 64))
    slc_s = cat("o_slc_s", (16, 4, 2, 2, 64))
    win_s = cat("o_win_s", (16, 512, 2, 2, 64))
    z = lambda *s: np.zeros(s, f32)
    yp = np.zeros((2, SEQ, D), f32)
    for c in range(NCORES):
        b, seg = c // 4, c % 4
        yp[b, seg * SEG:(seg + 1) * SEG] = R[c]["y_p"]
    ys = np.concatenate([R[c]["y_s"].reshape(16, 4, D) for c in range(NCORES)], axis=0)
    dil_p = [np.stack([R[3][f"o_dil_p{g}"].reshape(w, 2, 8, 64), R[7][f"o_dil_p{g}"].reshape(w, 2, 8, 64)])[None]
             for g, w in enumerate((128, 512, 2048))]
    return (yp, ys, gla_p, gla_s,
            cmp_p, cmp_s, slc_p, slc_s,
            win_p, win_s,
            dil_p[0], cat("o_dil_s0", (16, 128, 2, 8, 64)),
            dil_p[1], cat("o_dil_s1", (16, 512, 2, 8, 64)),
            dil_p[2], cat("o_dil_s2", (16, 2048, 2, 8, 64)))
```

```python
import math
import os
import numpy as np
from contextlib import ExitStack
import concourse.bass as bass
import concourse.mybir as mybir
from concourse.bass_utils import run_bass_kernel_spmd

F32 = mybir.dt.float32
BF16 = mybir.dt.bfloat16
I32 = mybir.dt.int32
AF = mybir.ActivationFunctionType
ALU = mybir.AluOpType
AX = mybir.AxisListType

NCORES = 8
D = 1024
SEQ = 8192
SEG = 2048
NSLOT = 8192
A_IN = 2856
EPS = 1e-6
NEGB = -30000.0


class Buf:
    __slots__ = ("w", "r")

    def __init__(self):
        self.w = None
        self.r = {}


class Eng:
    def __init__(self, name, obj, sem, sid, self_sync):
        self.name, self.obj, self.sem, self.sid, self.self_sync = name, obj, sem, sid, self_sync
        self.count = 0
        self.known = {}


class Sched:
    NDS = 24

    def __init__(self, nc, es):
        self.nc = nc
        self.es = es
        self.nsem = 0
        self.E = {
            "pe": Eng("pe", nc.tensor, self.mk(), 0, False),
            "act": Eng("act", nc.scalar, self.mk(), 1, True),
            "dve": Eng("dve", nc.vector, self.mk(), 2, True),
            "pool": Eng("pool", nc.gpsimd, self.mk(), 3, True),
            "sp": Eng("sp", nc.sync, self.mk(), 4, False),
        }
        self.next_sid = 1000
        self.dsems = [self.mk() for i in range(self.NDS)]
        self.dsid = [100 + i for i in range(self.NDS)]
        self.dcnt = [0] * self.NDS
        self.retired = []
        self.dma_i = 0
        self.ninst = 0

    def mk(self):
        self.nsem += 1
        return self.es.enter_context(self.nc.semaphore(f"s{self.nsem}"))

    def _roll(self, eng):
        if eng.count >= 30000:
            eng.sem = self.mk()
            eng.sid = self.next_sid
            self.next_sid += 1
            eng.count = 0

    def _wait(self, eng, tok):
        sem, val, sid = tok
        if eng.known.get(sid, 0) >= val:
            return
        eng.obj.wait_ge(sem, val)
        eng.known[sid] = val
        self.ninst += 1

    def _deps(self, eng, reads, writes):
        for b in reads:
            if b.w is not None:
                yield b.w
        for b in writes:
            if b.w is not None:
                yield b.w
            for t in b.r.values():
                yield t

    def _record(self, tok, reads, writes):
        for b in reads:
            b.r[tok[2]] = tok
        for b in writes:
            b.w = tok
            b.r = {}

    def op(self, ename, fn, reads=(), writes=()):
        eng = self.E[ename]
        for t in list(self._deps(eng, reads, writes)):
            if t[2] == eng.sid and not eng.self_sync:
                continue
            self._wait(eng, t)
        self._roll(eng)
        ins = fn(eng.obj)
        eng.count += 1
        ins.then_inc(eng.sem, 1)
        self.ninst += 1
        tok = (eng.sem, eng.count, eng.sid)
        eng.last = tok
        self._record(tok, reads, writes)
        return tok

    def dma(self, qname, out, in_, reads=(), writes=(), fn=None, **kw):
        eng = self.E[qname]
        for t in list(self._deps(eng, reads, writes)):
            self._wait(eng, t)
        k = self.dma_i % self.NDS
        self.dma_i += 1
        if self.dcnt[k] >= 1800:
            self.retired.append((self.dsems[k], 16 * self.dcnt[k], self.dsid[k]))
            self.dsems[k] = self.mk()
            self.dsid[k] = self.next_sid
            self.next_sid += 1
            self.dcnt[k] = 0
        sem = self.dsems[k]
        sid = self.dsid[k]
        if self.dcnt[k] > 0:
            self._wait(eng, (sem, 16 * self.dcnt[k], sid))
        self.dcnt[k] += 1
        if fn is not None:
            fn(eng.obj).then_inc(sem, 16)
        else:
            eng.obj.dma_start(out=out, in_=in_, **kw).then_inc(sem, 16)
        self.ninst += 1
        tok = (sem, 16 * self.dcnt[k], sid)
        self._record(tok, reads, writes)
        return tok

    def finish(self):
        for eng in self.E.values():
            for o in self.E.values():
                if o is not eng and getattr(o, "last", None) is not None:
                    self._wait(eng, o.last)
            for k in range(self.NDS):
                if self.dcnt[k] > 0:
                    self._wait(eng, (self.dsems[k], 16 * self.dcnt[k], self.dsid[k]))
            for t in self.retired:
                self._wait(eng, t)


class T:
    def __init__(self, h):
        self.h = h
        self.b = Buf()

    def __getitem__(self, k):
        return self.h[k]


def build_program():
    nc = bass.Bass("TRN2", target_bir_lowering=False)
    es = ExitStack()
    S = Sched(nc, es)

    def din(name, shape, dt=F32):
        return nc.dram_tensor(name, list(shape), dt, kind="ExternalInput").ap()

    def dout(name, shape, dt=F32):
        return nc.dram_tensor(name, list(shape), dt, kind="ExternalOutput").ap()

    stk = [es]

    used_names = {}

    def sb(name, shape, dt=F32):
        k = used_names.get(name, 0)
        used_names[name] = k + 1
        if k:
            name = f"{name}_v{k}"
        return T(stk[-1].enter_context(nc.sbuf_tensor(name, list(shape), dt)))

    def ps(name, shape, dt=F32):
        return T(es.enter_context(nc.psum_tensor(name, list(shape), dt)))

    xe = din("xe", [NSLOT, D])
    meta = din("meta", [128, 8])
    norm_mix = din("norm_mix", [2, D])
    a_w_in = din("a_w_in", [D, A_IN])
    nsa_g_qk = din("nsa_g_qk", [4, 64])
    xs = din("xs", [128, D])
    gla_w_a2 = din("gla_w_a2", [16, 256])
    gla_b_a = din("gla_b_a", [256])
    state_gla = din("state_gla", [16, 4, 64, 128])
    cache_win = din("cache_win", [16, 512, 256])
    gla_g_norm = din("gla_g_norm", [128])
    nsa_w_phi = din("nsa_w_phi", [2, 32, 64, 64])
    nsa_pe = din("nsa_pe", [2, 32, 64])
    c_w_in = din("c_w_in", [D, 4608])
    c_g_qk = din("c_g_qk", [6, 64])
    kd = [nc.dram_tensor(f"kd{g}", [4096, 512], BF16, kind="Internal").ap() for g in range(3)]
    vd = [nc.dram_tensor(f"vd{g}", [4096, 512], BF16, kind="Internal").ap() for g in range(3)]
    qd = [nc.dram_tensor(f"qd{g}", [2048, 512], BF16, kind="Internal").ap() for g in range(3)]
    od = [nc.dram_tensor(f"od{g}", [2048, 520], F32, kind="Internal").ap() for g in range(3)]
    qsd = nc.dram_tensor("qsd", [3, 128, 512], F32, kind="Internal").ap()
    ksd = nc.dram_tensor("ksd", [3, 128, 1024], F32, kind="Internal").ap()
    cache_dil = [din(f"cache_dil{g}", [16, w, 1024]) for g, w in enumerate((128, 512, 2048))]
    page_tab = din("page_tab", [16, 16], I32)
    pool_c = din("pool_c", [2560, 128, 256])
    pool_s = din("pool_s", [2560, 128, 256])
    sproj_d = nc.dram_tensor("sproj_d", [128, A_IN], F32, kind="Internal").ap()
    pe0 = din("pe0", [33, 128, 256])
    pe1 = din("pe1", [17, 128, 256])
    norm_mlp = din("norm_mlp", [2, D])
    norm_ple = din("norm_ple", [2, D])
    a_w_out = din("a_w_out", [1024, D])
    c_w_out = din("c_w_out", [512, D])
    mlp_w1 = din("mlp_w1", [2, D, 4096])
    mlp_w2 = din("mlp_w2", [2, 4096, D])
    ple_w_proj = din("ple_w_proj", [2, 256, D])
    ple_w_gate = din("ple_w_gate", [2, D, D])
    attn0 = nc.dram_tensor("attn0", [33, 128, 1024], BF16, kind="Internal").ap()
    attn1 = nc.dram_tensor("attn1", [17, 128, 512], BF16, kind="Internal").ap()
    h1s = nc.dram_tensor("h1s", [33, 128, D], F32, kind="Internal").ap()

    o_cmp_p = dout("o_cmp_p", [SEG, 256])
    o_slc_p = dout("o_slc_p", [SEG, 256])
    o_win_p = dout("o_win_p", [512, 256])
    o_cmp_s = dout("o_cmp_s", [64, 256])
    o_slc_s = dout("o_slc_s", [64, 256])
    o_win_s = dout("o_win_s", [16, 512, 256])
    o_gla_p = dout("o_gla_p", [4, 64, 128])
    o_gla_s = dout("o_gla_s", [16, 4, 64, 128])
    dbg_attn = dout("dbg_attn", [SEG, 1024])
    dbg_h1 = dout("dbg_h1", [SEG, 1024])
    dbg_h1s = dout("dbg_h1s", [128, 1024])
    dbg_as = dout("dbg_as", [64, 1024])
    y_p = dout("y_p", [SEG, D])
    o_dil_p = [dout(f"o_dil_p{g}", [w, 1024]) for g, w in enumerate((128, 512, 2048))]
    o_dil_s = [dout(f"o_dil_s{g}", [16, w, 1024]) for g, w in enumerate((128, 512, 2048))]
    y_s = dout("y_s", [64, D])

    ident = sb("ident", [128, 128], BF16)
    identf = sb("identf", [128, 128], F32)
    iot = sb("iot", [128, 128], I32)
    S.op("pool", lambda e: e.iota(iot[:], pattern=[[1, 128]], base=0, channel_multiplier=-1), writes=[iot.b])
    S.op("dve", lambda e: e.tensor_copy(identf[:], iot[:]), reads=[iot.b], writes=[identf.b])
    S.op("dve", lambda e: e.tensor_scalar(identf[:], identf[:], 0.0, None, ALU.is_equal), reads=[identf.b], writes=[identf.b])
    S.op("dve", lambda e: e.tensor_copy(ident[:], identf[:]), reads=[identf.b], writes=[ident.b])

    epsc = sb("epsc", [128, 1])
    S.op("dve", lambda e: e.memset(epsc[:], EPS), writes=[epsc.b])
    metat = sb("metat", [128, 8])
    S.dma("sp", metat[:], meta[:, :], writes=[metat.b])

    NT = NSLOT // 128
    posi = sb("posi", [128, NT], I32)
    posf = sb("posf", [128, NT + 1])
    S.op("pool", lambda e: e.iota(posi[:], pattern=[[128, NT]], base=0, channel_multiplier=1), writes=[posi.b])
    S.op("dve", lambda e: e.tensor_copy(posf[:, 0:NT], posi[:]), reads=[posi.b], writes=[posf.b])
    S.op("dve", lambda e: e.tensor_scalar(posf[:, 0:NT], posf[:, 0:NT], metat[:, 0:1], 0.0, ALU.subtract, ALU.max),
         reads=[posf.b, metat.b], writes=[posf.b])
    S.op("dve", lambda e: e.tensor_copy(posf[:, NT:NT + 1], metat[:, 1:2]), reads=[posf.b, metat.b], writes=[posf.b])
    fri = sb("fri", [128, 32], I32)
    frq = sb("frq", [128, 32])
    S.op("pool", lambda e: e.iota(fri[:], pattern=[[1, 32]], base=0, channel_multiplier=0), writes=[fri.b])
    S.op("dve", lambda e: e.tensor_copy(frq[:], fri[:]), reads=[fri.b], writes=[frq.b])
    S.op("act", lambda e: e.activation(frq[:], frq[:], AF.Exp, scale=-math.log(10000.0) / 32.0),
         reads=[frq.b], writes=[frq.b])
    TWO_PI = 2.0 * math.pi
    ang = sb("ang", [128, 32])
    angi = sb("angi", [128, 32], I32)
    angn = sb("angn", [128, 32])
    cs_t = sb("cs_t", [128, 32])
    sn_t = sb("sn_t", [128, 32])

    def sin_of(dst, shift):
        S.op("dve", lambda e: e.tensor_scalar(dst[:], ang[:], shift, 1.0 / TWO_PI, ALU.add, ALU.mult),
             reads=[ang.b], writes=[dst.b])
        S.op("dve", lambda e: e.tensor_copy(angi[:], dst[:]), reads=[dst.b], writes=[angi.b])
        S.op("dve", lambda e: e.tensor_copy(angn[:], angi[:]), reads=[angi.b], writes=[angn.b])
        S.op("dve", lambda e: e.tensor_tensor(dst[:], dst[:], angn[:], ALU.subtract), reads=[dst.b, angn.b], writes=[dst.b])
        S.op("dve", lambda e: e.tensor_scalar(dst[:], dst[:], TWO_PI, math.pi, ALU.mult, ALU.min), reads=[dst.b], writes=[dst.b])
        S.op("dve", lambda e: e.tensor_scalar(dst[:], dst[:], -math.pi, None, ALU.max), reads=[dst.b], writes=[dst.b])
        S.op("act", lambda e: e.activation(dst[:], dst[:], AF.Sin), reads=[dst.b], writes=[dst.b])

    def rope_tables(ti):
        S.op("dve", lambda e: e.tensor_scalar(ang[:], frq[:], posf[:, ti:ti + 1], None, ALU.mult),
             reads=[frq.b, posf.b], writes=[ang.b])
        sin_of(sn_t, 0.0)
        sin_of(cs_t, 0.5 * math.pi)

    gmix = sb("gmix", [128, 2, 8])
    S.dma("sp", gmix[:], norm_mix.rearrange("l (c p) -> p l c", p=128), writes=[gmix.b],
          allow_slow_non_contiguous=True)
    gqk = sb("gqk", [128, 4, 64])
    S.dma("sp", gqk[:], nsa_g_qk.rearrange("a d -> (a d)").partition_broadcast(128), writes=[gqk.b])

    pidx_i = sb("pidx_i", [128, 128], I32)
    jidx_i = sb("jidx_i", [128, 128], I32)
    pidx = sb("pidx", [128, 128])
    jidx = sb("jidx", [128, 128])
    S.op("pool", lambda e: e.iota(pidx_i[:], pattern=[[0, 128]], base=0, channel_multiplier=1), writes=[pidx_i.b])
    S.op("pool", lambda e: e.iota(jidx_i[:], pattern=[[1, 128]], base=0, channel_multiplier=0), writes=[jidx_i.b])
    S.op("dve", lambda e: e.tensor_copy(pidx[:], pidx_i[:]), reads=[pidx_i.b], writes=[pidx.b])
    S.op("dve", lambda e: e.tensor_copy(jidx[:], jidx_i[:]), reads=[jidx_i.b], writes=[jidx.b])
    fd_i = sb("fd_i", [128, 128], I32)

    def floordiv(dst, src, n):
        S.op("dve", lambda e: e.tensor_scalar(dst[:], src[:], 0.5, 1.0 / n, ALU.add, ALU.mult), reads=[src.b], writes=[dst.b])
        S.op("dve", lambda e: e.tensor_scalar(dst[:], dst[:], -0.5, None, ALU.add), reads=[dst.b], writes=[dst.b])
        S.op("dve", lambda e: e.tensor_copy(fd_i[:], dst[:]), reads=[dst.b], writes=[fd_i.b])
        S.op("dve", lambda e: e.tensor_copy(dst[:], fd_i[:]), reads=[fd_i.b], writes=[dst.b])

    pgt = sb("pgt", [128, 128])
    S.op("dve", lambda e: e.tensor_tensor(pgt[:], pidx[:], jidx[:], ALU.is_gt), reads=[pidx.b, jidx.b], writes=[pgt.b])
    pc_t = sb("pc_t", [128, 128])
    jc_t = sb("jc_t", [128, 128])
    Lm = {}
    Ci = {}
    for csz in (64, 4):
        floordiv(pc_t, pidx, csz)
        floordiv(jc_t, jidx, csz)
        lm = sb(f"Lm{csz}", [128, 128])
        ci = sb(f"Ci{csz}", [128, 32])
        S.op("dve", lambda e: e.tensor_tensor(lm[:], pc_t[:], jc_t[:], ALU.is_equal), reads=[pc_t.b, jc_t.b], writes=[lm.b])
        S.op("dve", lambda e: e.tensor_tensor(lm[:], lm[:], pgt[:], ALU.mult), reads=[lm.b, pgt.b], writes=[lm.b])
        S.op("dve", lambda e: e.tensor_tensor(ci[:], pc_t[:, 0:32], jidx[:, 0:32], ALU.is_equal),
             reads=[pc_t.b, jidx.b], writes=[ci.b])
        Lm[csz] = lm
        Ci[csz] = ci
    onec = sb("onec", [128, 1])
    S.op("dve", lambda e: e.memset(onec[:], 1.0), writes=[onec.b])
    ones1 = sb("ones1", [1, 128], BF16)
    S.op("dve", lambda e: e.memset(ones1[:], 1.0), writes=[ones1.b])
    wa2 = sb("wa2", [16, 256], BF16)
    S.dma("pool", wa2[:], gla_w_a2[:, :], writes=[wa2.b])
    bab = sb("bab", [1, 256], BF16)
    S.dma("pool", bab[:], gla_b_a.rearrange("(o n) -> o n", o=1), writes=[bab.b])

    Um = {}
    ple_t = sb("ple_t", [128, 128])
    S.op("dve", lambda e: e.tensor_tensor(ple_t[:], pidx[:], jidx[:], ALU.is_le), reads=[pidx.b, jidx.b], writes=[ple_t.b])
    Cj4 = sb("Cj4", [128, 128])
    for csz in (64, 4):
        floordiv(pc_t, pidx, csz)
        floordiv(jc_t, jidx, csz)
        um = sb(f"Um{csz}", [128, 128])
        S.op("dve", lambda e: e.tensor_tensor(um[:], pc_t[:], jc_t[:], ALU.is_equal), reads=[pc_t.b, jc_t.b], writes=[um.b])
        S.op("dve", lambda e: e.tensor_tensor(um[:], um[:], ple_t[:], ALU.mult), reads=[um.b, ple_t.b], writes=[um.b])
        Um[csz] = um
    S.op("dve", lambda e: e.tensor_copy(Cj4[:], jc_t[:]), reads=[jc_t.b], writes=[Cj4.b])
    causneg = sb("causneg", [128, 4, 128], BF16)
    winneg = sb("winneg", [128, 4, 128], BF16)
    S.op("dve", lambda e: e.tensor_scalar(causneg[:], pgt[:].unsqueeze(1).to_broadcast([128, 4, 128]), NEGB, None, ALU.mult),
         reads=[pgt.b], writes=[causneg.b])
    S.op("dve", lambda e: e.tensor_scalar(winneg[:], ple_t[:].unsqueeze(1).to_broadcast([128, 4, 128]), NEGB, None, ALU.mult),
         reads=[ple_t.b], writes=[winneg.b])
    V16 = sb("V16", [128, 128])
    S.op("dve", lambda e: e.scalar_tensor_tensor(V16[:], pidx[:], -16.0, jidx[:], ALU.mult, ALU.add),
         reads=[pidx.b, jidx.b], writes=[V16.b])
    BLK = sb("BLK", [128, 128])
    S.op("dve", lambda e: e.tensor_scalar(BLK[:], jidx[:], 64.0, None, ALU.is_ge), reads=[jidx.b], writes=[BLK.b])
    S.op("dve", lambda e: e.tensor_tensor(BLK[:], pidx[:], BLK[:], ALU.subtract), reads=[pidx.b, BLK.b], writes=[BLK.b])
    hi64 = sb("hi64", [128, 1])
    S.op("dve", lambda e: e.tensor_scalar(hi64[:], pidx[:, 0:1], 64.0, None, ALU.is_ge), reads=[pidx.b], writes=[hi64.b])
    C2S = sb("C2S", [128, 4, 128], BF16)
    c2a = sb("c2a", [128, 128])
    c2b = sb("c2b", [128, 128])
    for ct in range(4):
        S.op("dve", lambda e: e.tensor_scalar(c2a[:], pidx[:], 16.0, 2048.0 * ct + 32.0, ALU.mult, ALU.add), reads=[pidx.b], writes=[c2a.b])
        S.op("dve", lambda e: e.tensor_scalar(c2b[:], jidx[:], 64.0, 64.0, ALU.mult, ALU.add), reads=[jidx.b], writes=[c2b.b])
        S.op("dve", lambda e: e.tensor_tensor(c2a[:], c2a[:], c2b[:], ALU.min), reads=[c2a.b, c2b.b], writes=[c2a.b])
        S.op("dve", lambda e: e.tensor_scalar(c2b[:], c2b[:], -64.0 - 2048.0 * ct, None, ALU.add), reads=[c2b.b], writes=[c2b.b])
        S.op("dve", lambda e: e.scalar_tensor_tensor(c2b[:], pidx[:], 16.0, c2b[:], ALU.mult, ALU.max), reads=[pidx.b, c2b.b], writes=[c2b.b])
        S.op("dve", lambda e: e.tensor_scalar(c2b[:], c2b[:], 2048.0 * ct, None, ALU.add), reads=[c2b.b], writes=[c2b.b])
        S.op("dve", lambda e: e.tensor_tensor(c2a[:], c2a[:], c2b[:], ALU.subtract), reads=[c2a.b, c2b.b], writes=[c2a.b])
        S.op("dve", lambda e: e.tensor_scalar(C2S[:, ct, :], c2a[:], 0.0, 1.0 / 32.0, ALU.max, ALU.mult), reads=[c2a.b], writes=[C2S.b])
    padk = sb("padk", [128, 64])
    S.op("dve", lambda e: e.tensor_scalar(padk[:], jidx[:, 0:64], 128.0, metat[:, 0:1], ALU.mult, ALU.is_lt),
         reads=[jidx.b, metat.b], writes=[padk.b])
    S.op("dve", lambda e: e.tensor_scalar(padk[:], padk[:], NEGB, None, ALU.mult), reads=[padk.b], writes=[padk.b])
    padc = sb("padc", [128, 4])
    for ct in range(4):
        S.op("dve", lambda e: e.tensor_scalar(padc[:, ct:ct + 1], pidx[:, 0:1], 16.0, 2048.0 * ct, ALU.mult, ALU.add),
             reads=[pidx.b], writes=[padc.b])
    S.op("dve", lambda e: e.tensor_scalar(padc[:], padc[:], metat[:, 0:1], NEGB, ALU.is_lt, ALU.mult),
         reads=[padc.b, metat.b], writes=[padc.b])
    gno = sb("gno", [128, 128])
    S.dma("sp", gno[:], gla_g_norm.partition_broadcast(128), writes=[gno.b])

    esA = ExitStack()
    stk.append(esA)
    win = sb("win", [128, 8, A_IN], BF16)
    a_w_in_v = a_w_in.rearrange("(c p) n -> p c n", p=128)
    for c in range(8):
        S.dma("pool", win[:, c, :], a_w_in_v[:, c, :], writes=[win.b])

    Wbd = [sb(f"Wbd{a}", [128, 32, 128], BF16) for a in range(2)]
    peT = sb("peT", [128, 2, 32], BF16)
    with ExitStack() as est:
        stk.append(est)
        wst = sb("wst", [128, 32, 64])
        pst = sb("pst", [128, 2, 32])
        for a in range(2):
            S.op("pool", lambda e: e.memset(Wbd[a][:], 0.0), writes=[Wbd[a].b])
            for g in range(2):
                for q in range(4):
                    S.dma("sp", wst[64 * g:64 * g + 64, 8 * q:8 * q + 8, :],
                          nsa_w_phi[a, 8 * q:8 * q + 8].rearrange("p d e -> d p e"), writes=[wst.b])
                S.dma("sp", pst[64 * g:64 * g + 64, a, :], nsa_pe[a].rearrange("p d -> d p"), writes=[pst.b],
                      allow_slow_non_contiguous=True)
            for g in range(2):
                S.op("dve", lambda e: e.tensor_copy(Wbd[a][64 * g:64 * g + 64, :, 64 * g:64 * g + 64], wst[64 * g:64 * g + 64, :, :]),
                     reads=[wst.b, Wbd[a].b], writes=[Wbd[a].b])
        S.op("dve", lambda e: e.tensor_copy(peT[:], pst[:]), reads=[pst.b], writes=[peT.b])
        S.finish()
        stk.pop()
    pT = ps("pT", [128, 8, 128], BF16)
    pP = [ps(f"pP{i}", [128, 512]) for i in range(3)]
    pS = ps("pS", [128, 512])
    pO = ps("pO", [128, 512])
    pI = ps("pI", [128, 512])
    ppi = 0

    def nextpp():
        nonlocal ppi
        ppi += 1
        return pP[ppi % 3]

    cbias = sb("cbias", [128, 2])
    pb0 = nextpp()
    for a in range(2):
        for p in range(32):
            S.op("pe", lambda e: e.matmul(pb0[:, 2 * a:2 * a + 2], Wbd[a][:, p, :], peT[:, :, p], start=(p == 0), stop=(p == 31)),
                 reads=[Wbd[a].b, peT.b], writes=[pb0.b])
    S.op("act", lambda e: e.copy(cbias[:, 0:1], pb0[:, 0:1]), reads=[pb0.b], writes=[cbias.b])
    S.op("act", lambda e: e.copy(cbias[:, 1:2], pb0[:, 3:4]), reads=[pb0.b], writes=[cbias.b])

    xt = [sb(f"xt{i}", [128, D]) for i in range(2)]
    ssq = sb("ssq", [128, 1])
    xn = sb("xn", [128, D], BF16)
    hnT = sb("hnT", [128, 8, 128], BF16)
    proj = sb("proj", [128, A_IN])
    sq14 = sb("sq14", [128, 8, 64])
    ss14 = sb("ss14", [128, 8])
    rows = sb("rows", [128, 768])
    tmpa = sb("tmpa", [128, 8, 32])
    tmpb = sb("tmpb", [128, 8, 32])
    gaT = sb("gaT", [16, 128], BF16)
    gab = sb("gab", [128, 256])
    gmn = sb("gmn", [128, 256])
    lg = sb("lg", [128, 256])
    ecs = sb("ecs", [128, 256])
    khat = sb("khat", [128, 256], BF16)
    khm = sb("khm", [64, 256], BF16)
    vb = sb("vb", [128, 512], BF16)
    dec = sb("dec", [64, 4, 16])
    Sst = sb("Sst", [64, 4, 128])
    ebt = sb("ebt", [128, 128])
    qtT = [sb(f"qtT{h}", [128, 128], BF16) for h in range(2)]
    ktT = [sb(f"ktT{h}", [128, 128], BF16) for h in range(2)]
    qz = [[sb(f"qz{j}{h}", [128, 128], BF16) for h in range(2)] for j in range(2)]
    qtTz = [[sb(f"qtTz{hf}{hh}", [128, 128], BF16) for hh in range(2)] for hf in range(2)]
    Sbz = [[sb(f"Sbz{j}{h}", [128, 128], BF16) for h in range(4)] for j in range(2)]
    khz = [sb(f"khz{j}", [128, 256], BF16) for j in range(2)]
    for j in range(2):
        for h in range(2):
            S.op("pool", lambda e: e.memset(qz[j][h][:], 0.0), writes=[qz[j][h].b])
            S.op("pool", lambda e: e.memset(qtTz[j][h][:], 0.0), writes=[qtTz[j][h].b])
        for h in range(4):
            S.op("pool", lambda e: e.memset(Sbz[j][h][:], 0.0), writes=[Sbz[j][h].b])
    QcTz = [sb(f"QcTz{g}", [128, 4, 128], BF16) for g in range(2)]
    QrTz = [sb(f"QrTz{g}", [128, 4, 128], BF16) for g in range(2)]
    for g in range(2):
        S.op("pool", lambda e: e.memset(QcTz[g][:], 0.0), writes=[QcTz[g].b])
        S.op("pool", lambda e: e.memset(QrTz[g][:], 0.0), writes=[QrTz[g].b])
    lo64 = sb("lo64", [128, 1])
    S.op("dve", lambda e: e.tensor_scalar(lo64[:], hi64[:], -1.0, 1.0, ALU.mult, ALU.add), reads=[hi64.b], writes=[lo64.b])
    ATb = sb("ATb", [128, 4, 128], BF16)
    ogf = sb("ogf", [128, 4, 128])
    ogs = sb("ogs", [128, 4, 128])
    sgr = sb("sgr", [128, 512])
    attnb = sb("attnb", [128, 1024], BF16)
    dbgt = sb("dbgt", [128, 1024])
    xsq = dbgt
    qn = sb("qn", [128, 8, 64])
    qcp = sb("qcp", [128, 4, 128], BF16)
    qrp = sb("qrp", [128, 4, 128], BF16)
    kb4 = sb("kb4", [128, 4, 128], BF16)
    kT4 = sb("kT4", [128, 4, 128], BF16)
    gts = sb("gts", [128, 24])
    onsa = sb("onsa", [128, 8, 64])
    otmp = sb("otmp", [128, 4, 64])
    coef = sb("coef", [128, 4])
    rl4 = sb("rl4", [128, 4])
    Pt = [sb(f"Pt{i}", [128, 4, 128], BF16) for i in range(2)]
    mk = sb("mk", [128, 128], BF16)
    Ekt = sb("Ekt", [128, 128], BF16)
    Ekt2 = [Ekt, sb("Ektb", [128, 128], BF16)]
    imp = sb("imp", [128, 128])
    imw = sb("imw", [128, 128])
    vm = sb("vm", [128, 128])
    fm = sb("fm", [128, 128])
    curc = sb("curc", [128, 1])
    mx8 = sb("mx8", [128, 8])
    seln = sb("seln", [128, 128], BF16)
    selnT = [sb(f"selnT{g}", [128, 4, 128], BF16) for g in range(2)]

    xe_v = xe.rearrange("(t p) d -> t p d", p=128)
    groups = [(0, 512), (512, 1024), (1024, 1536), (1536, 2048), (2048, 2560), (2560, A_IN)]

    def qknorm(src_ap, nh, gidx, dst_ap, rd, wr):
        S.op("dve", lambda e: e.tensor_tensor(sq14[:, 0:nh, :], src_ap, src_ap, ALU.mult), reads=rd, writes=[sq14.b])
        S.op("dve", lambda e: e.tensor_reduce(ss14[:, 0:nh], sq14[:, 0:nh, :], AX.X, ALU.add),
             reads=[sq14.b], writes=[ss14.b])
        S.op("act", lambda e: e.activation(ss14[:, 0:nh], ss14[:, 0:nh], AF.Sqrt, bias=epsc[:, 0:1], scale=1.0 / 64.0),
             reads=[ss14.b, epsc.b], writes=[ss14.b])
        S.op("dve", lambda e: e.reciprocal(ss14[:, 0:nh], ss14[:, 0:nh]), reads=[ss14.b], writes=[ss14.b])
        S.op("dve", lambda e: e.tensor_tensor(dst_ap, src_ap, ss14[:, 0:nh].unsqueeze(2).to_broadcast([128, nh, 64]),
                                              ALU.mult), reads=rd + [ss14.b], writes=wr)
        S.op("dve", lambda e: e.tensor_tensor(dst_ap, dst_ap,
                                              gqk[:, gidx, :].unsqueeze(1).to_broadcast([128, nh, 64]), ALU.mult),
             reads=wr + [gqk.b], writes=wr)

    def rope(ap3, nh, ti, bufs):
        x1 = ap3[:, :, 0:32]
        x2 = ap3[:, :, 32:64]
        cs = cs_t[:].unsqueeze(1).to_broadcast([128, nh, 32])
        sn = sn_t[:].unsqueeze(1).to_broadcast([128, nh, 32])
        ta = tmpa[:, 0:nh, :]
        tb = tmpb[:, 0:nh, :]
        S.op("dve", lambda e: e.tensor_tensor(ta, x1, sn, ALU.mult), reads=bufs + [sn_t.b], writes=[tmpa.b])
        S.op("dve", lambda e: e.tensor_tensor(tb, x2, sn, ALU.mult), reads=bufs + [sn_t.b], writes=[tmpb.b])
        S.op("dve", lambda e: e.tensor_tensor(x1, x1, cs, ALU.mult), reads=bufs + [cs_t.b], writes=bufs)
        S.op("dve", lambda e: e.tensor_tensor(x2, x2, cs, ALU.mult), reads=bufs + [cs_t.b], writes=bufs)
        S.op("dve", lambda e: e.tensor_tensor(x1, x1, tb, ALU.subtract), reads=bufs + [tmpb.b], writes=bufs)
        S.op("dve", lambda e: e.tensor_tensor(x2, x2, ta, ALU.add), reads=bufs + [tmpa.b], writes=bufs)

    def project_tile(ti, src_ap, xbuf):
        S.dma("sp", xbuf[:], src_ap, writes=[xbuf.b])
        if int(os.environ.get('KROPE', 1)):
            rope_tables(ti)
        S.op("act", lambda e: e.activation(xsq[:], xbuf[:], AF.Square, accum_out=ssq[:]),
             reads=[xbuf.b], writes=[xsq.b, ssq.b])
        S.op("act", lambda e: e.activation(ssq[:], ssq[:], AF.Sqrt, bias=epsc[:, 0:1], scale=1.0 / D),
             reads=[ssq.b, epsc.b], writes=[ssq.b])
        S.op("dve", lambda e: e.reciprocal(ssq[:], ssq[:]), reads=[ssq.b], writes=[ssq.b])
        S.op("dve", lambda e: e.tensor_scalar(xn[:], xbuf[:], ssq[:, 0:1], None, ALU.mult),
             reads=[xbuf.b, ssq.b], writes=[xn.b])
        for c in range(8):
            S.op("pe", lambda e: e.transpose(pT[:, c, :], xn[:, c * 128:(c + 1) * 128], ident[:]),
                 reads=[xn.b, ident.b], writes=[pT.b])
        S.op("dve", lambda e: e.tensor_tensor(hnT[:], pT[:], gmix[:, 0, :].unsqueeze(2).to_broadcast([128, 8, 128]),
                                              ALU.mult), reads=[pT.b, gmix.b], writes=[hnT.b])
        for (c0, c1) in groups:
            pp = nextpp()
            n = c1 - c0
            for c in range(8):
                S.op("pe", lambda e: e.matmul(pp[:, 0:n], hnT[:, c, :], win[:, c, c0:c1],
                                              start=(c == 0), stop=(c == 7)),
                     reads=[hnT.b, win.b], writes=[pp.b])
            S.op("act", lambda e: e.copy(proj[:, c0:c1], pp[:, 0:n]), reads=[pp.b], writes=[proj.b])
        r = rows
        S.op("act", lambda e: e.copy(r[:], proj[:, 2064:2832]), reads=[proj.b], writes=[r.b])
        for (kidx, gidx) in ((0, 1), (2, 2), (4, 3)):
            src = proj[:, 2064 + kidx * 128: 2064 + (kidx + 1) * 128].rearrange("p (h d) -> p h d", d=64)
            dst = r[:, kidx * 128:(kidx + 1) * 128].rearrange("p (h d) -> p h d", d=64)
            qknorm(src, 2, gidx, dst, [proj.b], [r.b])
            if kidx > 0:
                rope(dst, 2, ti, [r.b])

    def gla_gates(csz, nch):
        KG = int(os.environ.get('KG', 99))
        if KG <= 1 and csz == 64:
            return
        pg = nextpp()
        if KG <= 2 and csz == 64:
            return
        for c in range(8):
            S.op("pe", lambda e: e.matmul(pg[0:16, 0:128], win[:, c, 1536:1552], hnT[:, c, :],
                                          start=(c == 0), stop=(c == 7)), reads=[hnT.b, win.b], writes=[pg.b])
        if KG <= 3 and csz == 64:
            return
        S.op("act", lambda e: e.copy(gaT[:], pg[0:16, 0:128]), reads=[pg.b], writes=[gaT.b])
        if KG <= 4 and csz == 64:
            return
        px = nextpp()
        if KG <= 5 and csz == 64:
            return
        S.op("pe", lambda e: e.matmul(px[:, 0:256], gaT[:], wa2[:], start=True, stop=False),
             reads=[gaT.b, wa2.b], writes=[px.b])
        if KG <= 6 and csz == 64:
            return
        S.op("pe", lambda e: e.matmul(px[:, 0:256], ones1[:], bab[:], start=False, stop=True),
             reads=[ones1.b, bab.b], writes=[px.b])
        if KG <= 7 and csz == 64:
            return
        S.op("act", lambda e: e.activation(gab[:], px[:, 0:256], AF.Abs), reads=[px.b], writes=[gab.b])
        if KG <= 8 and csz == 64:
            return
        S.op("act", lambda e: e.activation(gab[:], gab[:], AF.Exp, scale=-1.0), reads=[gab.b], writes=[gab.b])
        if KG <= 9 and csz == 64:
            return
        S.op("dve", lambda e: e.tensor_scalar(ecs[:], gab[:], 2.0, None, ALU.add), reads=[gab.b], writes=[ecs.b])
        S.op("dve", lambda e: e.reciprocal(ecs[:], ecs[:]), reads=[ecs.b], writes=[ecs.b])
        S.op("dve", lambda e: e.tensor_tensor(gab[:], gab[:], ecs[:], ALU.mult), reads=[gab.b, ecs.b], writes=[gab.b])
        S.op("dve", lambda e: e.tensor_tensor(ecs[:], gab[:], gab[:], ALU.mult), reads=[gab.b], writes=[ecs.b])
        S.op("dve", lambda e: e.tensor_scalar(gmn[:], ecs[:], 1.0 / 11.0, 1.0 / 9.0, ALU.mult, ALU.add), reads=[ecs.b], writes=[gmn.b])
        for cst in (1.0 / 7.0, 1.0 / 5.0, 1.0 / 3.0, 1.0):
            S.op("dve", lambda e: e.tensor_tensor(gmn[:], gmn[:], ecs[:], ALU.mult), reads=[gmn.b, ecs.b], writes=[gmn.b])
            S.op("dve", lambda e: e.tensor_scalar(gmn[:], gmn[:], cst, None, ALU.add), reads=[gmn.b], writes=[gmn.b])
        S.op("dve", lambda e: e.scalar_tensor_tensor(gab[:], gab[:], 2.0, gmn[:], ALU.mult, ALU.mult), reads=[gab.b, gmn.b], writes=[gab.b])
        if KG <= 10 and csz == 64:
            return
        S.op("dve", lambda e: e.tensor_single_scalar(gmn[:], px[:, 0:256], 0.0, ALU.min), reads=[px.b], writes=[gmn.b])
        if KG <= 11 and csz == 64:
            return
        S.op("dve", lambda e: e.tensor_tensor(lg[:], gmn[:], gab[:], ALU.subtract), reads=[gmn.b, gab.b], writes=[lg.b])
        if KG <= 12 and csz == 64:
            return
        S.op("dve", lambda e: e.tensor_scalar(lg[:], lg[:], 1.0 / 16.0, None, ALU.mult), reads=[lg.b], writes=[lg.b])
        if KG <= 13 and csz == 64:
            return
        pc = nextpp()
        if KG <= 14 and csz == 64:
            return
        S.op("pe", lambda e: e.matmul(pc[:, 0:256], Lm[csz][:], lg[:], start=True, stop=True),
             reads=[Lm[csz].b, lg.b], writes=[pc.b])
        if KG <= 15 and csz == 64:
            return
        S.op("act", lambda e: e.activation(ecs[:], pc[:, 0:256], AF.Exp), reads=[pc.b], writes=[ecs.b])
        if KG <= 16 and csz == 64:
            return
        S.op("dve", lambda e: e.tensor_tensor(khat[:], proj[:, 256:512], ecs[:], ALU.mult),
             reads=[proj.b, ecs.b], writes=[khat.b])
        if KG <= 17 and csz == 64:
            return
        S.op("act", lambda e: e.copy(vb[:], proj[:, 512:1024]), reads=[proj.b], writes=[vb.b])
        if KG <= 18 and csz == 64:
            return
        pd = nextpp()
        if KG <= 19 and csz == 64:
            return
        for h in range(4):
            S.op("pe", lambda e: e.matmul(pd[0:64, h * 16:h * 16 + 16], lg[:, 64 * h:64 * h + 64], Ci[csz][:, 0:16],
                                          start=True, stop=True), reads=[lg.b, Ci[csz].b], writes=[pd.b])
        if KG <= 20 and csz == 64:
            return
        for h in range(4):
            S.op("act", lambda e: e.activation(dec[:, h, 0:nch], pd[0:64, h * 16:h * 16 + nch], AF.Exp),
                 reads=[pd.b], writes=[dec.b])

    def gla_qk(csz):
        for hf in range(2):
            pq = nextpp()
            for c in range(8):
                S.op("pe", lambda e: e.matmul(pq[:, 0:128], win[:, c, 128 * hf:128 * hf + 128], hnT[:, c, :],
                                              start=(c == 0), stop=(c == 7)), reads=[hnT.b, win.b], writes=[pq.b])
            pk = nextpp()
            for c in range(8):
                S.op("pe", lambda e: e.matmul(pk[:, 0:128], win[:, c, 256 + 128 * hf:256 + 128 * hf + 128], hnT[:, c, :],
                                              start=(c == 0), stop=(c == 7)), reads=[hnT.b, win.b], writes=[pk.b])
            pb = nextpp()
            S.op("pe", lambda e: e.matmul(pb[:, 0:128], lg[:, 128 * hf:128 * hf + 128], Um[csz][:], start=True, stop=True),
                 reads=[lg.b, Um[csz].b], writes=[pb.b])
            S.op("act", lambda e: e.activation(ebt[:], pb[:, 0:128], AF.Exp), reads=[pb.b], writes=[ebt.b])
            S.op("dve", lambda e: e.scalar_tensor_tensor(qtT[hf][:], pq[:, 0:128], 0.125, ebt[:], ALU.mult, ALU.mult),
                 reads=[pq.b, ebt.b], writes=[qtT[hf].b])
            S.op("act", lambda e: e.activation(ebt[:], pb[:, 0:128], AF.Exp, scale=-1.0), reads=[pb.b], writes=[ebt.b])
            S.op("dve", lambda e: e.tensor_tensor(ktT[hf][:], pk[:, 0:128], ebt[:], ALU.mult),
                 reads=[pk.b, ebt.b], writes=[ktT[hf].b])
            for hh in range(2):
                S.op("act", lambda e: e.copy(qtTz[hf][hh][64 * hh:64 * hh + 64, :], qtT[hf][64 * hh:64 * hh + 64, :]),
                     reads=[qtT[hf].b], writes=[qtTz[hf][hh].b])

    def gla_intra(csz):
        pA = nextpp()
        for h in range(4):
            hf, hh = h // 2, h % 2
            S.op("pe", lambda e: e.matmul(pA[:, h * 128:(h + 1) * 128], ktT[hf][:, :],
                                          qtTz[hf][hh][:, :], start=True, stop=True),
                 reads=[ktT[hf].b, qtTz[hf][hh].b], writes=[pA.b])
        if int(os.environ.get('KI', 9)) <= 1:
            return
        S.op("dve", lambda e: e.tensor_tensor(ATb[:], pA[:, :].rearrange("p (h t) -> p h t", h=4),
                                              Um[csz][:].unsqueeze(1).to_broadcast([128, 4, 128]), ALU.mult),
             reads=[pA.b, Um[csz].b], writes=[ATb.b])

    def gla_post(dst_ap, wr):
        S.op("act", lambda e: e.copy(ogf[:], pO[:, :].rearrange("p (h v) -> p h v", h=4)), reads=[pO.b], writes=[ogf.b])
        S.op("dve", lambda e: e.tensor_tensor(ogs[:], ogf[:], ogf[:], ALU.mult), reads=[ogf.b], writes=[ogs.b])
        S.op("dve", lambda e: e.tensor_reduce(ss14[:, 0:4], ogs[:], AX.X, ALU.add), reads=[ogs.b], writes=[ss14.b])
        S.op("act", lambda e: e.activation(ss14[:, 0:4], ss14[:, 0:4], AF.Sqrt, bias=epsc[:, 0:1], scale=1.0 / 128.0),
             reads=[ss14.b, epsc.b], writes=[ss14.b])
        S.op("dve", lambda e: e.reciprocal(ss14[:, 0:4], ss14[:, 0:4]), reads=[ss14.b], writes=[ss14.b])
        S.op("dve", lambda e: e.tensor_tensor(ogf[:], ogf[:], ss14[:, 0:4].unsqueeze(2).to_broadcast([128, 4, 128]), ALU.mult),
             reads=[ogf.b, ss14.b], writes=[ogf.b])
        S.op("dve", lambda e: e.tensor_tensor(ogf[:], ogf[:], gno[:].unsqueeze(1).to_broadcast([128, 4, 128]), ALU.mult),
             reads=[ogf.b, gno.b], writes=[ogf.b])
        S.op("act", lambda e: e.activation(sgr[:], proj[:, 1024:1536], AF.Exp, scale=-1.0), reads=[proj.b], writes=[sgr.b])
        S.op("dve", lambda e: e.tensor_scalar(sgr[:], sgr[:], 1.0, None, ALU.add), reads=[sgr.b], writes=[sgr.b])
        S.op("dve", lambda e: e.reciprocal(sgr[:], sgr[:]), reads=[sgr.b], writes=[sgr.b])
        S.op("dve", lambda e: e.tensor_tensor(sgr[:], sgr[:], proj[:, 1024:1536], ALU.mult), reads=[sgr.b, proj.b], writes=[sgr.b])
        S.op("dve", lambda e: e.tensor_tensor(dst_ap, ogf[:].rearrange("p h v -> p (h v)"), sgr[:], ALU.mult),
             reads=[ogf.b, sgr.b], writes=wr)

    def nsa_q(ti):
        src = proj[:, 1552:2064].rearrange("p (h d) -> p h d", d=64)
        qknorm(src, 8, 0, qn[:], [proj.b], [qn.b])
        S.op("act", lambda e: e.copy(qcp[:].rearrange("p j (g d) -> p j g d", g=2),
                                     qn[:].rearrange("p (g j) d -> p j g d", g=2)), reads=[qn.b], writes=[qcp.b])
        rope(qn[:], 8, ti, [qn.b])
        S.op("act", lambda e: e.copy(qrp[:].rearrange("p j (g d) -> p j g d", g=2),
                                     qn[:].rearrange("p (g j) d -> p j g d", g=2)), reads=[qn.b], writes=[qrp.b])
        for j in range(4):
            S.op("pe", lambda e: e.transpose(pT[:, j, :], qcp[:, j, :], ident[:]), reads=[qcp.b, ident.b], writes=[pT.b])
            S.op("pe", lambda e: e.transpose(pT[:, 4 + j, :], qrp[:, j, :], ident[:]), reads=[qrp.b, ident.b], writes=[pT.b])
        for g in range(2):
            S.op("act", lambda e: e.copy(QcTz[g][64 * g:64 * g + 64, :, :], pT[64 * g:64 * g + 64, 0:4, :]),
                 reads=[pT.b], writes=[QcTz[g].b])
            S.op("act", lambda e: e.copy(QrTz[g][64 * g:64 * g + 64, :, :], pT[64 * g:64 * g + 64, 4:8, :]),
                 reads=[pT.b], writes=[QrTz[g].b])
        S.op("act", lambda e: e.activation(gts[:], proj[:, 2832:2856], AF.Exp, scale=-1.0), reads=[proj.b], writes=[gts.b])
        S.op("dve", lambda e: e.tensor_scalar(gts[:], gts[:], 1.0, None, ALU.add), reads=[gts.b], writes=[gts.b])
        S.op("dve", lambda e: e.reciprocal(gts[:], gts[:]), reads=[gts.b], writes=[gts.b])

    def branch_out(g, br, first):
        pov = pO[:, 0:260].rearrange("p (j c) -> p j c", j=4)
        S.op("dve", lambda e: e.tensor_scalar(rl4[:], pov[:, :, 64], 1e-20, None, ALU.max), reads=[pO.b], writes=[rl4.b])
        S.op("dve", lambda e: e.reciprocal(rl4[:], rl4[:]), reads=[rl4.b], writes=[rl4.b])
        gv = gts[:].rearrange("p (h b) -> p h b", b=3)[:, 4 * g:4 * g + 4, br]
        S.op("dve", lambda e: e.tensor_tensor(coef[:], rl4[:], gv, ALU.mult), reads=[rl4.b, gts.b], writes=[coef.b])
        dst = onsa[:, 4 * g:4 * g + 4, :]
        if first:
            S.op("dve", lambda e: e.tensor_tensor(dst, pov[:, :, 0:64], coef[:].unsqueeze(2).to_broadcast([128, 4, 64]), ALU.mult),
                 reads=[pO.b, coef.b], writes=[onsa.b])
        else:
            S.op("dve", lambda e: e.tensor_tensor(otmp[:], pov[:, :, 0:64], coef[:].unsqueeze(2).to_broadcast([128, 4, 64]), ALU.mult),
                 reads=[pO.b, coef.b], writes=[otmp.b])
            S.op("dve", lambda e: e.tensor_tensor(dst, dst, otmp[:], ALU.add), reads=[onsa.b, otmp.b], writes=[onsa.b])

    pti = 0

    def nextPt():
        nonlocal pti
        pti += 1
        return Pt[pti % 2]

    def kside(A, ti, r):
        rv = r[:].rearrange("p (a b c) -> p a b c", a=3, b=2)
        S.op("act", lambda e: e.copy(kb4[:, 0:3, :], rv[:, :, 0, :]), reads=[r.b], writes=[kb4.b])
        S.op("act", lambda e: e.copy(kb4[:, 3, :], r[:, 128:256]), reads=[r.b], writes=[kb4.b])
        for a in range(4):
            S.op("pe", lambda e: e.transpose(pT[:, a, :], kb4[:, a, :], ident[:]), reads=[kb4.b, ident.b], writes=[pT.b])
        S.op("act", lambda e: e.copy(kT4[:], pT[:, 0:4, :]), reads=[pT.b], writes=[kT4.b])
        S.op("dve", lambda e: e.tensor_copy(A.ksT[:, ti, :], kT4[:, 1, :]), reads=[kT4.b], writes=[A.ksT.b])
        S.op("dve", lambda e: e.tensor_copy(A.kwT[:, ti % 8, :], kT4[:, 2, :]), reads=[kT4.b], writes=[A.kwT.b])
        S.op("dve", lambda e: e.tensor_copy(A.vs1[:, ti, :, 0:64], r[:, 384:512].rearrange("p (g d) -> p g d", g=2)),
             reads=[r.b], writes=[A.vs1.b])
        S.op("dve", lambda e: e.tensor_copy(A.vw1[:, ti % 8, :, 0:64], r[:, 640:768].rearrange("p (g d) -> p g d", g=2)),
             reads=[r.b], writes=[A.vw1.b])
        for a, src_i in ((0, 0), (1, 3)):
            for half, arr in ((0, A.clo[a]), (1, A.chi[a])):
                pp = nextpp()
                for p in range(16):
                    S.op("pe", lambda e: e.matmul(pp[:, 0:8], Wbd[a][:, 16 * half + p, :],
                                                  kT4[:, src_i, :].rearrange("q (c p) -> q p c", p=16)[:, p, :],
                                                  start=(p == 0), stop=(p == 15)),
                         reads=[Wbd[a].b, kT4.b], writes=[pp.b])
                S.op("act", lambda e: e.copy(arr[:, 8 * ti + 1:8 * ti + 9], pp[:, 0:8]), reads=[pp.b], writes=[arr.b])
        n0 = 8 * ti - 1
        lo_n = max(n0, 0)
        cnt = 8 * ti + 7 - lo_n
        for a, dstT in ((0, A.kcmpT), (1, A.vcmpT)):
            S.op("dve", lambda e: e.tensor_tensor(sq14[:, 0, 0:cnt], A.clo[a][:, lo_n + 1:lo_n + 1 + cnt],
                                                  A.chi[a][:, lo_n + 2:lo_n + 2 + cnt], ALU.add),
                 reads=[A.clo[a].b, A.chi[a].b], writes=[sq14.b])
            S.op("dve", lambda e: e.tensor_scalar(dstT[:, lo_n:lo_n + cnt], sq14[:, 0, 0:cnt], cbias[:, a:a + 1], None, ALU.add),
                 reads=[sq14.b, cbias.b], writes=[dstT.b])
        for ct in sorted({lo_n // 128, (8 * ti + 6) // 128}):
            S.op("pe", lambda e: e.transpose(pT[:, 4, :], A.vcmpT[:, 128 * ct:128 * ct + 128], ident[:]),
                 reads=[A.vcmpT.b, ident.b], writes=[pT.b])
            S.op("act", lambda e: e.copy(A.vcmp1[:, ct, :, 0:64], pT[:, 4, :].rearrange("p (g d) -> p g d", g=2)),
                 reads=[pT.b], writes=[A.vcmp1.b])

    def nsa_attend(A, ti):
        nsa_q(ti)
        nct = (8 * ti + 6) // 128 + 1
        for g in range(2):
            gs = slice(64 * g, 64 * g + 64)
            for ct in range(nct):
                pp = nextpp()
                S.op("pe", lambda e: e.matmul(pp[:, :], A.kcmpT[:, 128 * ct:128 * ct + 128],
                                              QcTz[g][:, :, :].rearrange("p j t -> p (j t)"), start=True, stop=True),
                     reads=[A.kcmpT.b, QcTz[g].b], writes=[pp.b])
                P = nextPt()
                S.op("act", lambda e: e.activation(P[:].rearrange("p j t -> p (j t)"), pp[:, :], AF.Exp,
                                                   bias=A.padc(ct), scale=0.125),
                     reads=[pp.b, A.padb], writes=[P.b])
                if ct >= nct - 2:
                    off = 2048.0 * ct - 128.0 * ti + 31.0
                    S.op("dve", lambda e: e.tensor_scalar(mk[:], V16[:], off, None, ALU.is_ge), reads=[V16.b], writes=[mk.b])
                    S.op("dve", lambda e: e.tensor_tensor(P[:], P[:], mk[:].unsqueeze(1).to_broadcast([128, 4, 128]), ALU.mult),
                         reads=[P.b, mk.b], writes=[P.b])
                for j in range(4):
                    S.op("pe", lambda e: e.matmul(pO[:, 65 * j:65 * j + 65], P[:, j, :], A.vcmp1[:, ct, g, :],
                                                  start=(ct == 0 and j == 0), stop=(ct == nct - 1), skip_group_check=True),
                         reads=[P.b, A.vcmp1.b], writes=[pO.b])
                    S.op("pe", lambda e: e.matmul(pI[:, 128 * j:128 * j + 128], P[:, j, :], C2S[:, ct, :],
                                                  start=(ct == 0 and j == 0), stop=(ct == nct - 1), skip_group_check=True),
                         reads=[P.b, C2S.b], writes=[pI.b])
            branch_out(g, 0, True)
            S.op("dve", lambda e: e.tensor_scalar(imp[:], pI[:, 0:128], rl4[:, 0:1], None, ALU.mult),
                 reads=[pI.b, rl4.b], writes=[imp.b])
            for j in range(1, 4):
                S.op("dve", lambda e: e.scalar_tensor_tensor(imp[:], pI[:, 128 * j:128 * j + 128], rl4[:, j:j + 1], imp[:],
                                                             ALU.mult, ALU.add), reads=[pI.b, rl4.b, imp.b], writes=[imp.b])
            S.op("dve", lambda e: e.tensor_scalar(curc[:], hi64[:], float(2 * ti), None, ALU.add), reads=[hi64.b], writes=[curc.b])
            S.op("dve", lambda e: e.tensor_scalar(vm[:], jidx[:], curc[:, 0:1], None, ALU.is_le), reads=[jidx.b, curc.b], writes=[vm.b])
            S.op("dve", lambda e: e.tensor_scalar(fm[:], jidx[:], A.off64, None, ALU.is_ge), reads=[jidx.b, metat.b], writes=[fm.b])
            S.op("dve", lambda e: e.tensor_tensor(vm[:], vm[:], fm[:], ALU.mult), reads=[vm.b, fm.b], writes=[vm.b])
            S.op("dve", lambda e: e.tensor_scalar(fm[:], jidx[:], curc[:, 0:1], None, ALU.is_equal), reads=[jidx.b, curc.b], writes=[fm.b])
            S.op("dve", lambda e: e.tensor_scalar(imw[:], jidx[:], A.off64, None, ALU.is_equal), reads=[jidx.b, metat.b], writes=[imw.b])
            S.op("dve", lambda e: e.tensor_tensor(fm[:], fm[:], imw[:], ALU.max), reads=[fm.b, imw.b], writes=[fm.b])
            S.op("dve", lambda e: e.tensor_tensor(imp[:], imp[:], vm[:], ALU.mult), reads=[imp.b, vm.b], writes=[imp.b])
            S.op("dve", lambda e: e.tensor_scalar(vm[:], vm[:], 1.0e9, -1.0e9, ALU.mult, ALU.add), reads=[vm.b], writes=[vm.b])
            S.op("dve", lambda e: e.tensor_tensor(imp[:], imp[:], vm[:], ALU.add), reads=[imp.b, vm.b], writes=[imp.b])
            S.op("dve", lambda e: e.scalar_tensor_tensor(imp[:], fm[:], 1.0e30, imp[:], ALU.mult, ALU.max), reads=[imp.b, fm.b], writes=[imp.b])
            S.op("dve", lambda e: e.max(mx8[:], imp[:]), reads=[imp.b], writes=[mx8.b])
            S.op("dve", lambda e: e.match_replace(imw[:], mx8[:], imp[:], -3.0e38), reads=[imp.b, mx8.b], writes=[imw.b])
            S.op("dve", lambda e: e.max(mx8[:], imw[:]), reads=[imw.b], writes=[mx8.b])
            S.op("dve", lambda e: e.tensor_scalar(seln[:], imp[:], mx8[:, 7:8], NEGB, ALU.is_lt, ALU.mult),
                 reads=[imp.b, mx8.b], writes=[seln.b])
            S.op("pe", lambda e: e.transpose(pT[:, 5, :], seln[:], ident[:]), reads=[seln.b, ident.b], writes=[pT.b])
            S.op("dve", lambda e: e.tensor_copy(selnT[g][:], pT[:, 5, :].unsqueeze(1).to_broadcast([128, 4, 128])),
                 reads=[pT.b], writes=[selnT[g].b])
        for g in range(2):
            def slc_qk(kt):
                ek = Ekt2[kt % 2]
                S.op("dve", lambda e: e.tensor_scalar(ek[:], BLK[:], float(2 * kt), None, ALU.is_equal),
                     reads=[BLK.b], writes=[ek.b])
                pp = nextpp()
                S.op("pe", lambda e: e.matmul(pp[:, :], A.ksT[:, kt, :], QrTz[g][:, :, :].rearrange("p j t -> p (j t)"),
                                              start=True, stop=False), reads=[A.ksT.b, QrTz[g].b], writes=[pp.b])
                S.op("pe", lambda e: e.matmul(pp[:, :], ek[:], selnT[g][:].rearrange("p j t -> p (j t)"),
                                              start=False, stop=(kt != ti)), reads=[ek.b, selnT[g].b], writes=[pp.b])
                if kt == ti:
                    S.op("pe", lambda e: e.matmul(pp[:, :], ident[:], causneg[:].rearrange("p j t -> p (j t)"),
                                                  start=False, stop=True), reads=[ident.b, causneg.b], writes=[pp.b])
                return pp

            def slc_act(kt, pp):
                P = nextPt()
                S.op("act", lambda e: e.activation(P[:].rearrange("p j t -> p (j t)"), pp[:, :], AF.Exp,
                                                   bias=A.padk(kt), scale=0.125),
                     reads=[pp.b, A.padb], writes=[P.b])
                return P

            def slc_pv(kt, P):
                for j in range(4):
                    S.op("pe", lambda e: e.matmul(pO[:, 65 * j:65 * j + 65], P[:, j, :], A.vs1[:, kt, g, :],
                                                  start=(kt == 0 and j == 0), stop=(kt == ti), skip_group_check=True),
                         reads=[P.b, A.vs1.b], writes=[pO.b])

            pp_cur = slc_qk(0)
            P_prev = None
            for kt in range(ti + 1):
                P_cur = slc_act(kt, pp_cur)
                if kt + 1 <= ti:
                    pp_cur = slc_qk(kt + 1)
                slc_pv(kt, P_cur)
            branch_out(g, 1, False)
        for g in range(2):
            gs = slice(64 * g, 64 * g + 64)
            kts = [kt for kt in range(ti - 4, ti + 1) if kt >= 0]
            for kt in kts:
                pp = nextpp()
                extra = (kt == ti) or (kt == ti - 4)
                S.op("pe", lambda e: e.matmul(pp[:, :], A.kwT[:, kt % 8, :], QrTz[g][:, :, :].rearrange("p j t -> p (j t)"),
                                              start=True, stop=not extra), reads=[A.kwT.b, QrTz[g].b], writes=[pp.b])
                if extra:
                    mneg = causneg if kt == ti else winneg
                    S.op("pe", lambda e: e.matmul(pp[:, :], ident[:], mneg[:].rearrange("p j t -> p (j t)"),
                                                  start=False, stop=True), reads=[ident.b, mneg.b], writes=[pp.b])
                P = nextPt()
                S.op("act", lambda e: e.activation(P[:].rearrange("p j t -> p (j t)"), pp[:, :], AF.Exp,
                                                   bias=A.padk(kt), scale=0.125),
                     reads=[pp.b, A.padb], writes=[P.b])
                for j in range(4):
                    S.op("pe", lambda e: e.matmul(pO[:, 65 * j:65 * j + 65], P[:, j, :], A.vw1[:, kt % 8, g, :],
                                                  start=(kt == kts[0] and j == 0), stop=(kt == ti), skip_group_check=True),
                         reads=[P.b, A.vw1.b], writes=[pO.b])
            branch_out(g, 2, False)
        S.op("act", lambda e: e.copy(attnb[:, 512:1024], onsa[:].rearrange("p h d -> p (h d)")), reads=[onsa.b], writes=[attnb.b])

    with ExitStack() as es2:
        def sb2(name, shape, dt=F32):
            return T(es2.enter_context(nc.sbuf_tensor(name, list(shape), dt)))
        ksT = sb2("ksT", [128, 64, 128], BF16)
        vs1 = sb2("vs1", [128, 64, 2, 65], BF16)
        kwT = sb2("kwT", [128, 8, 128], BF16)
        vw1 = sb2("vw1", [128, 8, 2, 65], BF16)
        clo = [sb2(f"clo{a}", [128, 520]) for a in range(2)]
        chi = [sb2(f"chi{a}", [128, 520]) for a in range(2)]
        kcmpT = sb2("kcmpT", [128, 512], BF16)
        vcmpT = sb2("vcmpT", [128, 512], BF16)
        vcmp1 = sb2("vcmp1", [128, 4, 2, 65], BF16)
        S.op("pool", lambda e: e.memset(vs1[:], 1.0), writes=[vs1.b])
        S.op("pool", lambda e: e.memset(vw1[:], 1.0), writes=[vw1.b])
        S.op("pool", lambda e: e.memset(vcmp1[:], 1.0), writes=[vcmp1.b])
        S.op("pool", lambda e: e.memset(kcmpT[:], 0.0), writes=[kcmpT.b])
        S.op("pool", lambda e: e.memset(vcmpT[:], 0.0), writes=[vcmpT.b])
        for a in range(2):
            S.op("pool", lambda e: e.memset(clo[a][:], 0.0), writes=[clo[a].b])
            S.op("pool", lambda e: e.memset(chi[a][:], 0.0), writes=[chi[a].b])
        S.op("dve", lambda e: e.memset(Sst[:], 0.0), writes=[Sst.b])

        class NS:
            pass
        A = NS()
        A.ksT, A.vs1, A.kwT, A.vw1, A.clo, A.chi, A.kcmpT, A.vcmpT, A.vcmp1 = ksT, vs1, kwT, vw1, clo, chi, kcmpT, vcmpT, vcmp1
        A.padk = lambda kt: padk[:, kt:kt + 1]
        A.padc = lambda ct: padc[:, ct:ct + 1]
        A.padb = padk.b
        A.off64 = metat[:, 2:3]

        for ti in range(int(os.environ.get('KT_END', NT))):
            full = ti >= int(os.environ.get('KFULL', 32))
            xbuf = xt[ti % 2]
            project_tile(ti, xe_v[ti], xbuf)
            r = rows
            if ti >= 48:
                t0 = (ti - 48) * 128
                S.dma("sp", o_cmp_p[t0:t0 + 128, :], r[:, 0:256], reads=[r.b])
                S.dma("sp", o_slc_p[t0:t0 + 128, :], r[:, 256:512], reads=[r.b])
            if ti >= 60:
                t0 = (ti - 60) * 128
                S.dma("sp", o_win_p[t0:t0 + 128, :], r[:, 512:768], reads=[r.b])
            KSTOP = int(os.environ.get('KSTOP', 99))
            if KSTOP <= 1:
                continue
            kside(A, ti, r)
            if KSTOP <= 4:
                continue
            gla_gates(64, 2)
            if KSTOP <= 5:
                continue
            if full:
                gla_qk(64)
                if KSTOP <= 6:
                    continue
                gla_intra(64)
                if KSTOP <= 7:
                    continue
                for hf in range(2):
                    S.op("act", lambda e: e.copy(qz[0][hf][:, 0:64], qtT[hf][:, 0:64]), reads=[qtT[hf].b], writes=[qz[0][hf].b])
                    S.op("act", lambda e: e.copy(qz[1][hf][:, 64:128], qtT[hf][:, 64:128]), reads=[qtT[hf].b], writes=[qz[1][hf].b])
            for j in range(2):
                if full:
                    for h in range(4):
                        hf, hh = h // 2, h % 2
                        S.op("dve", lambda e: e.tensor_copy(Sbz[j][h][64 * hh:64 * hh + 64, :], Sst[:, h, :]),
                             reads=[Sst.b], writes=[Sbz[j][h].b])
                cm = lo64 if j == 0 else hi64
                S.op("dve", lambda e: e.tensor_scalar(khz[j][:], khat[:], cm[:, 0:1], None, ALU.mult),
                     reads=[khat.b, cm.b], writes=[khz[j].b])
                for h in range(4):
                    S.op("pe", lambda e: e.matmul(pS[0:64, h * 128:(h + 1) * 128],
                                                  khz[j][:, 64 * h:64 * h + 64],
                                                  vb[:, 128 * h:128 * h + 128], start=True, stop=True),
                         reads=[khz[j].b, vb.b], writes=[pS.b])
                S.op("dve", lambda e: e.tensor_tensor(Sst[:], Sst[:], dec[:, :, j:j + 1].to_broadcast([64, 4, 128]), ALU.mult),
                     reads=[Sst.b, dec.b], writes=[Sst.b])
                S.op("dve", lambda e: e.tensor_tensor(Sst[:], Sst[:], pS[0:64, :].rearrange("p (h v) -> p h v", h=4), ALU.add),
                     reads=[Sst.b, pS.b], writes=[Sst.b])
            if not full:
                continue
            if KSTOP <= 9:
                continue
            for h in range(4):
                hf, hh = h // 2, h % 2
                S.op("pe", lambda e: e.matmul(pO[:, h * 128:(h + 1) * 128], ATb[:, h, :], vb[:, 128 * h:128 * h + 128],
                                              start=True, stop=False), reads=[ATb.b, vb.b], writes=[pO.b])
                for j in range(2):
                    S.op("pe", lambda e: e.matmul(pO[:, h * 128:(h + 1) * 128], qz[j][hf][:, :],
                                                  Sbz[j][h][:, :], start=False, stop=(j == 1)),
                         reads=[qz[j][hf].b, Sbz[j][h].b], writes=[pO.b])
            if KSTOP <= 10:
                continue
            gla_post(attnb[:, 0:512], [attnb.b])

            if KSTOP <= 11:
                continue
            nsa_attend(A, ti)
            S.dma("sp", attn0[ti - 32], attnb[:], reads=[attnb.b])
            if ti >= 48:
                S.op("dve", lambda e: e.tensor_copy(dbgt[:], attnb[:]), reads=[attnb.b], writes=[dbgt.b])
                S.dma("sp", dbg_attn[(ti - 48) * 128:(ti - 47) * 128, :], dbgt[:], reads=[dbgt.b])
        S.dma("sp", o_gla_p.rearrange("h k v -> k h v"), Sst[:], reads=[Sst.b])
        S.finish()

    def make_rows(ti):
        r = rows
        S.op("act", lambda e: e.copy(r[:], proj[:, 2064:2832]), reads=[proj.b], writes=[r.b])
        for (kidx, gidx) in ((0, 1), (2, 2), (4, 3)):
            src = proj[:, 2064 + kidx * 128: 2064 + (kidx + 1) * 128].rearrange("p (h d) -> p h d", d=64)
            dst = r[:, kidx * 128:(kidx + 1) * 128].rearrange("p (h d) -> p h d", d=64)
            qknorm(src, 2, gidx, dst, [proj.b], [r.b])
            if kidx > 0:
                rope(dst, 2, ti, [r.b])

    with ExitStack() as es3:
        stk.append(es3)
        Ssm = sb("Ssm", [64, 64, 128])
        S.dma("sp", Ssm[:], state_gla.rearrange("s h k v -> k (s h) v"), writes=[Ssm.b])
        project_tile(NT, xs[:, :], xt[0])
        r = rows
        S.dma("sp", sproj_d[:, :], proj[:], reads=[proj.b])
        S.dma("sp", o_cmp_s[:, :], r[0:64, 0:256], reads=[r.b])
        S.dma("sp", o_slc_s[:, :], r[0:64, 256:512], reads=[r.b])
        for sq in range(16):
            S.dma("sp", o_win_s[sq, 508:512, :], r[4 * sq:4 * sq + 4, 512:768], reads=[r.b])
        gla_gates(4, 16)
        gla_qk(4)
        gla_intra(4)
        mcol = sb("mcol", [128, 128], BF16)
        qms = [sb(f"qms{h}", [128, 128], BF16) for h in range(2)]
        for h in range(4):
            S.op("pe", lambda e: e.matmul(pO[:, h * 128:(h + 1) * 128], ATb[:, h, :], vb[:, 128 * h:128 * h + 128],
                                          start=(h == 0), stop=False, skip_group_check=True), reads=[ATb.b, vb.b], writes=[pO.b])
        for sq in range(16):
            S.op("dve", lambda e: e.tensor_scalar(mcol[:], Cj4[:], float(sq), None, ALU.is_equal), reads=[Cj4.b], writes=[mcol.b])
            for hf in range(2):
                S.op("dve", lambda e: e.tensor_tensor(qms[hf][:], qtT[hf][:], mcol[:], ALU.mult), reads=[qtT[hf].b, mcol.b], writes=[qms[hf].b])
            for h in range(4):
                hf, hh = h // 2, h % 2
                S.op("dve", lambda e: e.tensor_copy(Sbz[0][h][64 * hh:64 * hh + 64, :], Ssm[:, 4 * sq + h, :]),
                     reads=[Ssm.b], writes=[Sbz[0][h].b])
                S.op("pe", lambda e: e.matmul(pO[:, h * 128:(h + 1) * 128], qms[hf][:, :], Sbz[0][h][:, :],
                                              start=False, stop=(sq == 15), skip_group_check=True),
                     reads=[qms[hf].b, Sbz[0][h].b], writes=[pO.b])
        gla_post(attnb[:, 0:512], [attnb.b])
        S.dma("sp", attn0[32][0:64, 0:512], attnb[0:64, 0:512], reads=[attnb.b])
        S.op("dve", lambda e: e.tensor_copy(dbgt[:, 0:512], attnb[:, 0:512]), reads=[attnb.b], writes=[dbgt.b])
        S.dma("sp", dbg_as[:, 0:512], dbgt[0:64, 0:512], reads=[dbgt.b])
        for sq in range(16):
            S.op("dve", lambda e: e.tensor_scalar(khm[:], khat[0:64, :], Ci[4][0:64, sq:sq + 1], None, ALU.mult),
                 reads=[khat.b, Ci[4].b], writes=[khm.b])
            for h in range(4):
                S.op("pe", lambda e: e.matmul(pS[0:64, h * 128:(h + 1) * 128], khm[:, 64 * h:64 * h + 64],
                                              vb[0:64, 128 * h:128 * h + 128], start=True, stop=True),
                     reads=[khm.b, vb.b], writes=[pS.b])
            sv = Ssm[:, 4 * sq:4 * sq + 4, :]
            S.op("dve", lambda e: e.tensor_tensor(sv, sv, dec[:, :, sq:sq + 1].to_broadcast([64, 4, 128]), ALU.mult),
                 reads=[Ssm.b, dec.b], writes=[Ssm.b])
            S.op("dve", lambda e: e.tensor_tensor(sv, sv, pS[0:64, :].rearrange("p (h v) -> p h v", h=4), ALU.add),
                 reads=[Ssm.b, pS.b], writes=[Ssm.b])
        S.dma("sp", o_gla_s.rearrange("s h k v -> k (s h) v"), Ssm[:], reads=[Ssm.b])
        S.dma("sp", o_win_s[:, 0:508, :], cache_win[:, 4:512, :])
        S.finish()
        stk.pop()

    with ExitStack() as es4:
        stk.append(es4)
        As = type("NS", (), {})()
        As.ksT = sb("ksTs", [128, 17, 128], BF16)
        As.vs1 = sb("vs1s", [128, 17, 2, 65], BF16)
        As.kwT = sb("kwTs", [128, 8, 128], BF16)
        As.vw1 = sb("vw1s", [128, 8, 2, 65], BF16)
        As.clo = [sb(f"clos{a}", [128, 520]) for a in range(2)]
        As.chi = [sb(f"chis{a}", [128, 520]) for a in range(2)]
        As.kcmpT = sb("kcmpTs", [128, 512], BF16)
        As.vcmpT = sb("vcmpTs", [128, 512], BF16)
        As.vcmp1 = sb("vcmp1s", [128, 4, 2, 65], BF16)
        zc = sb("zc", [128, 1])
        S.op("dve", lambda e: e.memset(zc[:], 0.0), writes=[zc.b])
        As.padk = lambda kt: zc[:, 0:1]
        As.padc = lambda ct: zc[:, 0:1]
        As.padb = zc.b
        As.off64 = zc[:, 0:1]
        S.op("pool", lambda e: e.memset(As.vs1[:], 1.0), writes=[As.vs1.b])
        S.op("pool", lambda e: e.memset(As.vw1[:], 1.0), writes=[As.vw1.b])
        S.op("pool", lambda e: e.memset(As.vcmp1[:], 1.0), writes=[As.vcmp1.b])
        S.op("pool", lambda e: e.memset(As.kcmpT[:], 0.0), writes=[As.kcmpT.b])
        S.op("pool", lambda e: e.memset(As.vcmpT[:], 0.0), writes=[As.vcmpT.b])
        S.op("pool", lambda e: e.memset(As.kwT[:], 0.0), writes=[As.kwT.b])
        for a in range(2):
            S.op("pool", lambda e: e.memset(As.clo[a][:], 0.0), writes=[As.clo[a].b])
            S.op("pool", lambda e: e.memset(As.chi[a][:], 0.0), writes=[As.chi[a].b])
        ptab_i = sb("ptab_i", [128, 256], I32)
        ptf = sb("ptf", [128, 256])
        idxu = sb("idxu", [128, 256], mybir.dt.uint32)
        S.dma("sp", ptab_i[:], page_tab.rearrange("s i -> (s i)").partition_broadcast(128), writes=[ptab_i.b])
        S.op("dve", lambda e: e.tensor_copy(ptf[:], ptab_i[:]), reads=[ptab_i.b], writes=[ptf.b])
        S.op("dve", lambda e: e.tensor_scalar(ptf[:], ptf[:], 128.0, pidx[:, 0:1], ALU.mult, ALU.add), reads=[ptf.b, pidx.b], writes=[ptf.b])
        S.op("dve", lambda e: e.tensor_copy(idxu[:], ptf[:]), reads=[ptf.b], writes=[idxu.b])
        poolc = pool_c.rearrange("n p c -> (n p) c")
        pools = pool_s.rearrange("n p c -> (n p) c")
        for sq in range(int(os.environ.get('KSEQ', 16))):
            for lt in range(16):
                j = sq * 16 + lt
                S.dma("pool", None, None, reads=[idxu.b], writes=[rows.b],
                      fn=lambda e: e.indirect_dma_start(rows[:, 0:256], None, poolc,
                                                        bass.IndirectOffsetOnAxis(ap=idxu[:, j:j + 1], axis=0)))
                S.dma("pool", None, None, reads=[idxu.b], writes=[rows.b],
                      fn=lambda e: e.indirect_dma_start(rows[:, 256:512], None, pools,
                                                        bass.IndirectOffsetOnAxis(ap=idxu[:, j:j + 1], axis=0)))
                if lt >= 12:
                    S.dma("sp", rows[:, 512:768], cache_win[sq, (lt - 12) * 128:(lt - 11) * 128, :], writes=[rows.b])
                kside(As, lt, rows)
            S.op("pool", lambda e: e.memset(proj[:], 0.0), writes=[proj.b])
            S.dma("sp", proj[0:4, :], sproj_d[4 * sq:4 * sq + 4, :], reads=[proj.b], writes=[proj.b])
            make_rows(NT)
            kside(As, 16, rows)
            nsa_attend(As, 16)
            S.dma("sp", attn0[32][4 * sq:4 * sq + 4, 512:1024], attnb[0:4, 512:1024], reads=[attnb.b])
            S.op("dve", lambda e: e.tensor_copy(dbgt[0:4, 512:1024], attnb[0:4, 512:1024]), reads=[attnb.b], writes=[dbgt.b])
            S.dma("sp", dbg_as[4 * sq:4 * sq + 4, 512:1024], dbgt[0:4, 512:1024], reads=[dbgt.b])
        S.finish()
        stk.pop()

    S.finish()
    stk.pop()
    esA.close()

    def channel_phase(layer, KA, w_out_d, tiles):
        with ExitStack() as esB:
            stk.append(esB)
            KC = KA // 128
            wout = sb("wout", [128, KC, D], BF16)
            wgate = sb("wgate", [128, 8, D], BF16)
            wproj = sb("wproj", [128, 2, D], BF16)
            for c in range(KC):
                S.dma("pool", wout[:, c, :], w_out_d[c * 128:(c + 1) * 128, :], writes=[wout.b])
            for c in range(8):
                S.dma("pool", wgate[:, c, :], ple_w_gate[layer, c * 128:(c + 1) * 128, :], writes=[wgate.b])
            for c in range(2):
                S.dma("pool", wproj[:, c, :], ple_w_proj[layer, c * 128:(c + 1) * 128, :], writes=[wproj.b])
            gml = sb("gml", [128, 8])
            gpl = sb("gpl", [128, 8])
            S.dma("sp", gml[:], norm_mlp[layer].rearrange("(c p) -> p c", p=128), writes=[gml.b], allow_slow_non_contiguous=True)
            S.dma("sp", gpl[:], norm_ple[layer].rearrange("(c p) -> p c", p=128), writes=[gpl.b], allow_slow_non_contiguous=True)
            w1s = [sb(f"w1s{i}", [128, 8, 512], BF16) for i in range(2)]
            w2s = [sb(f"w2s{i}", [128, 4, D], BF16) for i in range(2)]
            hmid = sb("hmid", [128, 4, D])
            hng = sb("hng", [128, 8, 512], BF16)
            aTs = [sb(f"aTs{i}", [128, 4, 512], BF16) for i in range(2)]
            xin = [sb(f"xin{i}", [128, D]) for i in range(2)]
            ain = [sb(f"ain{i}", [128, KA], BF16) for i in range(2)]
            aT = sb("aT", [128, KC, 128], BF16)
            junk = sb("junk", [128, D])
            ssb = sb("ssb", [128, 1])
            xnb = sb("xnb", [128, D], BF16)
            hnl = sb("hnl", [128, 8, 128], BF16)
            rlu = sb("rlu", [128, 512])
            gate = sb("gate", [128, D])
            pin = sb("pin", [128, 256])
            pinb = sb("pinb", [128, 256], BF16)
            pTl = sb("pTl", [128, 2, 128], BF16)
            hout = sb("hout", [128, D])
            w1v = mlp_w1[layer].rearrange("(c p) f -> p c f", p=128)
            w2v = mlp_w2[layer].rearrange("(c p) n -> p c n", p=128)

            def norm_T(src_ap, src_b, gvec, dstT_ap, dst_b):
                S.op("act", lambda e: e.activation(junk[:], src_ap, AF.Square, accum_out=ssb[:]),
                     reads=[src_b], writes=[junk.b, ssb.b])
                S.op("act", lambda e: e.activation(ssb[:], ssb[:], AF.Sqrt, bias=epsc[:, 0:1], scale=1.0 / D),
                     reads=[ssb.b, epsc.b], writes=[ssb.b])
                S.op("dve", lambda e: e.reciprocal(ssb[:], ssb[:]), reads=[ssb.b], writes=[ssb.b])
                S.op("dve", lambda e: e.tensor_scalar(xnb[:], src_ap, ssb[:, 0:1], None, ALU.mult),
                     reads=[src_b, ssb.b], writes=[xnb.b])
                for c in range(8):
                    S.op("pe", lambda e: e.transpose(pT[:, c, :], xnb[:, c * 128:(c + 1) * 128], ident[:]),
                         reads=[xnb.b, ident.b], writes=[pT.b])
                S.op("dve", lambda e: e.tensor_tensor(dstT_ap, pT[:], gvec[:].unsqueeze(2).to_broadcast([128, 8, 128]), ALU.mult),
                     reads=[pT.b, gvec.b], writes=[dst_b])

            wi = 0
            for g0 in range(0, len(tiles), 4):
                grp = tiles[g0:g0 + 4]
                ng = len(grp)
                NTK = ng * 128
                for tt, (x_src, a_src, p_src, dsts) in enumerate(grp):
                    xb = xin[tt % 2]
                    ab = ain[tt % 2]
                    S.dma("sp", xb[:], x_src, writes=[xb.b])
                    S.dma("sp", ab[:], a_src, writes=[ab.b])
                    for c in range(KC):
                        S.op("pe", lambda e: e.transpose(pT[:, c, :], ab[:, c * 128:(c + 1) * 128], ident[:]),
                             reads=[ab.b, ident.b], writes=[pT.b])
                    S.op("act", lambda e: e.copy(aT[:], pT[:, 0:KC, :]), reads=[pT.b], writes=[aT.b])
                    for half in range(2):
                        pp = nextpp()
                        for c in range(KC):
                            S.op("pe", lambda e: e.matmul(pp[:, :], aT[:, c, :], wout[:, c, half * 512:(half + 1) * 512],
                                                          start=(c == 0), stop=(c == KC - 1)),
                                 reads=[aT.b, wout.b], writes=[pp.b])
                        S.op("dve", lambda e: e.tensor_tensor(hmid[:, tt, half * 512:(half + 1) * 512], pp[:, :],
                                                              xb[:, half * 512:(half + 1) * 512], ALU.add),
                             reads=[pp.b, xb.b], writes=[hmid.b])
                    norm_T(hmid[:, tt, :], hmid.b, gml, hng[:, :, tt * 128:(tt + 1) * 128], hng.b)
                for f in range(8):
                    wa, wb = w1s[wi % 2], w2s[wi % 2]
                    at = aTs[wi % 2]
                    wi += 1
                    for c in range(8):
                        S.dma("pool", wa[:, c, :], w1v[:, c, f * 512:(f + 1) * 512], writes=[wa.b])
                    for c in range(4):
                        S.dma("pool", wb[:, c, :], w2v[:, 4 * f + c, :], writes=[wb.b])
                    for fc in range(4):
                        pp = nextpp()
                        for c in range(8):
                            S.op("pe", lambda e: e.matmul(pp[:, 0:NTK], wa[:, c, fc * 128:(fc + 1) * 128], hng[:, c, 0:NTK],
                                                          start=(c == 0), stop=(c == 7)),
                                 reads=[wa.b, hng.b], writes=[pp.b])
                        S.op("act", lambda e: e.activation(rlu[:, 0:NTK], pp[:, 0:NTK], AF.Relu), reads=[pp.b], writes=[rlu.b])
                        S.op("dve", lambda e: e.tensor_tensor(at[:, fc, 0:NTK], rlu[:, 0:NTK], rlu[:, 0:NTK], ALU.mult),
                             reads=[rlu.b], writes=[at.b])
                    for tt in range(ng):
                        for half in range(2):
                            pp = nextpp()
                            for fc in range(4):
                                S.op("pe", lambda e: e.matmul(pp[:, :], at[:, fc, tt * 128:(tt + 1) * 128],
                                                              wb[:, fc, half * 512:(half + 1) * 512],
                                                              start=(fc == 0), stop=(fc == 3)),
                                     reads=[at.b, wb.b], writes=[pp.b])
                            hv = hmid[:, tt, half * 512:(half + 1) * 512]
                            S.op("dve", lambda e: e.tensor_tensor(hv, hv, pp[:, :], ALU.add), reads=[hmid.b, pp.b], writes=[hmid.b])
                for tt, (x_src, a_src, p_src, dsts) in enumerate(grp):
                    norm_T(hmid[:, tt, :], hmid.b, gpl, hnl[:], hnl.b)
                    S.dma("sp", pin[:], p_src, writes=[pin.b])
                    S.op("act", lambda e: e.copy(pinb[:], pin[:]), reads=[pin.b], writes=[pinb.b])
                    for c in range(2):
                        S.op("pe", lambda e: e.transpose(pT[:, c, :], pinb[:, c * 128:(c + 1) * 128], ident[:]),
                             reads=[pinb.b, ident.b], writes=[pT.b])
                    S.op("act", lambda e: e.copy(pTl[:], pT[:, 0:2, :]), reads=[pT.b], writes=[pTl.b])
                    for half in range(2):
                        hs = slice(half * 512, (half + 1) * 512)
                        pg = nextpp()
                        for c in range(8):
                            S.op("pe", lambda e: e.matmul(pg[:, :], hnl[:, c, :], wgate[:, c, hs], start=(c == 0), stop=(c == 7)),
                                 reads=[hnl.b, wgate.b], writes=[pg.b])
                        S.op("act", lambda e: e.activation(gate[:, hs], pg[:, :], AF.Exp, scale=-1.0), reads=[pg.b], writes=[gate.b])
                        S.op("dve", lambda e: e.tensor_scalar(gate[:, hs], gate[:, hs], 1.0, None, ALU.add), reads=[gate.b], writes=[gate.b])
                        S.op("dve", lambda e: e.reciprocal(gate[:, hs], gate[:, hs]), reads=[gate.b], writes=[gate.b])
                        pq = nextpp()
                        for c in range(2):
                            S.op("pe", lambda e: e.matmul(pq[:, :], pTl[:, c, :], wproj[:, c, hs], start=(c == 0), stop=(c == 1)),
                                 reads=[pTl.b, wproj.b], writes=[pq.b])
                        S.op("dve", lambda e: e.tensor_tensor(gate[:, hs], gate[:, hs], pq[:, :], ALU.mult), reads=[gate.b, pq.b], writes=[gate.b])
                        S.op("dve", lambda e: e.tensor_tensor(hout[:, hs], hmid[:, tt, hs], gate[:, hs], ALU.add),
                             reads=[hmid.b, gate.b], writes=[hout.b])
                    for (dst, r0, r1) in dsts:
                        S.dma("sp", dst, hout[r0:r1, :], reads=[hout.b])
            S.finish()
            stk.pop()

    tiles0 = []
    for ti in range(32, 64):
        dsts = [(h1s[ti - 32], 0, 128)]
        if ti >= 48:
            dsts.append((dbg_h1[(ti - 48) * 128:(ti - 47) * 128, :], 0, 128))
        tiles0.append((xe_v[ti], attn0[ti - 32], pe0[ti - 32], dsts))
    tiles0.append((xs[:, :], attn0[32], pe0[32], [(h1s[32], 0, 128), (dbg_h1s[:, :], 0, 128)]))
    channel_phase(0, 1024, a_w_out, tiles0)

    WG = (128, 512, 2048)
    DG = (1, 4, 16)
    with ExitStack() as esC:
        stk.append(esC)
        cwin = sb("cwin", [128, 8, 4608], BF16)
        cwv = c_w_in.rearrange("(c p) n -> p c n", p=128)
        for c in range(8):
            for hh_ in range(2):
                S.dma("pool", cwin[:, c, hh_ * 2304:(hh_ + 1) * 2304], cwv[:, c, hh_ * 2304:(hh_ + 1) * 2304], writes=[cwin.b])
        cgq = sb("cgq", [128, 6, 64])
        S.dma("sp", cgq[:], c_g_qk.rearrange("a d -> (a d)").partition_broadcast(128), writes=[cgq.b])
        hin = [sb(f"hin{i}", [128, D]) for i in range(2)]
        junkc = sb("junkc", [128, D])
        ssc = sb("ssc", [128, 1])
        xnc = sb("xnc", [128, D], BF16)
        hnc = sb("hnc", [128, 8, 128], BF16)
        zt = sb("zt", [128, 8, 64])
        zb = sb("zb", [128, 512], BF16)
        kvrow = [sb(f"kvrow{i}", [128, 1024]) for i in range(2)]
        sqc = sb("sqc", [128, 8, 64])
        s8 = sb("s8", [128, 8])
        ta8 = sb("ta8", [128, 8, 32])
        tb8 = sb("tb8", [128, 8, 32])

        def qkn_rope(src_psum, pb_, gidx, dst_ap, dst_b):
            S.op("act", lambda e: e.copy(dst_ap, src_psum), reads=[pb_], writes=[dst_b])
            S.op("dve", lambda e: e.tensor_tensor(sqc[:], dst_ap, dst_ap, ALU.mult), reads=[dst_b], writes=[sqc.b])
            S.op("dve", lambda e: e.tensor_reduce(s8[:], sqc[:], AX.X, ALU.add), reads=[sqc.b], writes=[s8.b])
            S.op("act", lambda e: e.activation(s8[:], s8[:], AF.Sqrt, bias=epsc[:, 0:1], scale=1.0 / 64.0), reads=[s8.b, epsc.b], writes=[s8.b])
            S.op("dve", lambda e: e.reciprocal(s8[:], s8[:]), reads=[s8.b], writes=[s8.b])
            S.op("dve", lambda e: e.tensor_tensor(dst_ap, dst_ap, s8[:].unsqueeze(2).to_broadcast([128, 8, 64]), ALU.mult),
                 reads=[dst_b, s8.b], writes=[dst_b])
            S.op("dve", lambda e: e.tensor_tensor(dst_ap, dst_ap, cgq[:, gidx, :].unsqueeze(1).to_broadcast([128, 8, 64]), ALU.mult),
                 reads=[dst_b, cgq.b], writes=[dst_b])
            x1 = dst_ap[:, :, 0:32]
            x2 = dst_ap[:, :, 32:64]
            cs = cs_t[:].unsqueeze(1).to_broadcast([128, 8, 32])
            sn = sn_t[:].unsqueeze(1).to_broadcast([128, 8, 32])
            S.op("dve", lambda e: e.tensor_tensor(ta8[:], x1, sn, ALU.mult), reads=[dst_b, sn_t.b], writes=[ta8.b])
            S.op("dve", lambda e: e.tensor_tensor(tb8[:], x2, sn, ALU.mult), reads=[dst_b, sn_t.b], writes=[tb8.b])
            S.op("dve", lambda e: e.tensor_tensor(x1, x1, cs, ALU.mult), reads=[dst_b, cs_t.b], writes=[dst_b])
            S.op("dve", lambda e: e.tensor_tensor(x2, x2, cs, ALU.mult), reads=[dst_b, cs_t.b], writes=[dst_b])
            S.op("dve", lambda e: e.tensor_tensor(x1, x1, tb8[:], ALU.subtract), reads=[dst_b, tb8.b], writes=[dst_b])
            S.op("dve", lambda e: e.tensor_tensor(x2, x2, ta8[:], ALU.add), reads=[dst_b, ta8.b], writes=[dst_b])

        for idx in range(33):
            sample = idx == 32
            ti = NT if sample else 32 + idx
            own = idx >= 16
            hb_ = hin[idx % 2]
            S.dma("sp", hb_[:], h1s[idx], writes=[hb_.b])
            rope_tables(ti)
            S.op("act", lambda e: e.activation(junkc[:], hb_[:], AF.Square, accum_out=ssc[:]), reads=[hb_.b], writes=[junkc.b, ssc.b])
            S.op("act", lambda e: e.activation(ssc[:], ssc[:], AF.Sqrt, bias=epsc[:, 0:1], scale=1.0 / D), reads=[ssc.b, epsc.b], writes=[ssc.b])
            S.op("dve", lambda e: e.reciprocal(ssc[:], ssc[:]), reads=[ssc.b], writes=[ssc.b])
            S.op("dve", lambda e: e.tensor_scalar(xnc[:], hb_[:], ssc[:, 0:1], None, ALU.mult), reads=[hb_.b, ssc.b], writes=[xnc.b])
            for c in range(8):
                S.op("pe", lambda e: e.transpose(pT[:, c, :], xnc[:, c * 128:(c + 1) * 128], ident[:]), reads=[xnc.b, ident.b], writes=[pT.b])
            S.op("dve", lambda e: e.tensor_tensor(hnc[:], pT[:], gmix[:, 1, :].unsqueeze(2).to_broadcast([128, 8, 128]), ALU.mult),
                 reads=[pT.b, gmix.b], writes=[hnc.b])
            for g in range(3):
                kv = kvrow[g % 2]
                for r_ in range(3):
                    if r_ == 0 and not own:
                        continue
                    c0 = g * 1536 + r_ * 512
                    pp = nextpp()
                    for c in range(8):
                        S.op("pe", lambda e: e.matmul(pp[:, :], hnc[:, c, :], cwin[:, c, c0:c0 + 512], start=(c == 0), stop=(c == 7)),
                             reads=[hnc.b, cwin.b], writes=[pp.b])
                    ppv = pp[:, :].rearrange("p (h d) -> p h d", d=64)
                    if r_ == 0:
                        qkn_rope(ppv, pp.b, 2 * g, zt[:], zt.b)
                        if sample:
                            S.dma("sp", qsd[g], zt[:].rearrange("p h d -> p (h d)"), reads=[zt.b])
                        else:
                            S.op("act", lambda e: e.copy(zb[:], zt[:].rearrange("p h d -> p (h d)")), reads=[zt.b], writes=[zb.b])
                            S.dma("sp", qd[g][(idx - 16) * 128:(idx - 15) * 128, :], zb[:], reads=[zb.b])
                    elif r_ == 1:
                        qkn_rope(ppv, pp.b, 2 * g + 1, kv[:, 0:512].rearrange("p (h d) -> p h d", d=64), kv.b)
                        if not sample:
                            S.op("act", lambda e: e.copy(zb[:], kv[:, 0:512]), reads=[kv.b], writes=[zb.b])
                            S.dma("sp", kd[g][idx * 128:(idx + 1) * 128, :], zb[:], reads=[zb.b])
                    else:
                        S.op("act", lambda e: e.copy(kv[:, 512:1024], pp[:, :]), reads=[pp.b], writes=[kv.b])
                        if not sample:
                            S.op("act", lambda e: e.copy(zb[:], pp[:, :]), reads=[pp.b], writes=[zb.b])
                            S.dma("sp", vd[g][idx * 128:(idx + 1) * 128, :], zb[:], reads=[zb.b])
                if sample:
                    S.dma("sp", ksd[g], kv[:], reads=[kv.b])
                    w_ = WG[g]
                    for sq in range(16):
                        S.dma("sp", o_dil_s[g][sq, w_ - 4:w_, :], kv[4 * sq:4 * sq + 4, :], reads=[kv.b])
                    for sq in range(0, 16, 4):
                        S.dma("sp", o_dil_s[g][sq:sq + 4, 0:w_ - 4, :], cache_dil[g][sq:sq + 4, 4:w_, :])
                else:
                    first = 64 - WG[g] // 128
                    if ti >= first:
                        S.dma("sp", o_dil_p[g][(ti - first) * 128:(ti - first + 1) * 128, :], kv[:], reads=[kv.b])
        S.finish()
        stk.pop()

    with ExitStack() as esD:
        stk.append(esD)
        pidx_i2 = sb("pidx_i2", [128, 128], I32)
        jidx_i2 = sb("jidx_i2", [128, 128], I32)
        pidx2 = sb("pidx2", [128, 128])
        jidx2 = sb("jidx2", [128, 128])
        S.op("pool", lambda e: e.iota(pidx_i2[:], pattern=[[0, 128]], base=0, channel_multiplier=1), writes=[pidx_i2.b])
        S.op("pool", lambda e: e.iota(jidx_i2[:], pattern=[[1, 128]], base=0, channel_multiplier=0), writes=[jidx_i2.b])
        S.op("dve", lambda e: e.tensor_copy(pidx2[:], pidx_i2[:]), reads=[pidx_i2.b], writes=[pidx2.b])
        S.op("dve", lambda e: e.tensor_copy(jidx2[:], jidx_i2[:]), reads=[jidx_i2.b], writes=[jidx2.b])
        mtmp = sb("mtmp", [128, 128])
        mneg = [sb(f"mneg{i}", [128, 4, 128], BF16) for i in range(2)]
        S.op("dve", lambda e: e.tensor_tensor(mtmp[:], pidx2[:], jidx2[:], ALU.is_lt), reads=[pidx2.b, jidx2.b], writes=[mtmp.b])
        S.op("dve", lambda e: e.tensor_scalar(mneg[0][:], mtmp[:].unsqueeze(1).to_broadcast([128, 4, 128]), NEGB, None, ALU.mult),
             reads=[mtmp.b], writes=[mneg[0].b])
        S.op("dve", lambda e: e.tensor_tensor(mtmp[:], pidx2[:], jidx2[:], ALU.is_gt), reads=[pidx2.b, jidx2.b], writes=[mtmp.b])
        S.op("dve", lambda e: e.tensor_scalar(mneg[1][:], mtmp[:].unsqueeze(1).to_broadcast([128, 4, 128]), NEGB, None, ALU.mult),
             reads=[mtmp.b], writes=[mneg[1].b])
        zeroc = sb("zeroc", [128, 1])
        S.op("dve", lambda e: e.memset(zeroc[:], 0.0), writes=[zeroc.b])
        qb_t = sb("qb_t", [128, 512], BF16)
        kb_t = [sb(f"kb_t{i}", [128, 512], BF16) for i in range(2)]
        vb_t = [sb(f"vb_t{i}", [128, 512], BF16) for i in range(2)]
        v1_t = [sb(f"v1_t{i}", [128, 8, 65], BF16) for i in range(2)]
        for i in range(2):
            S.op("pool", lambda e: e.memset(v1_t[i][:], 1.0), writes=[v1_t[i].b])
        qTz = [sb(f"qTz{h}", [128, 128], BF16) for h in range(8)]
        for h in range(8):
            S.op("pool", lambda e: e.memset(qTz[h][:], 0.0), writes=[qTz[h].b])
        kTt = [sb(f"kTt{i}", [128, 4, 128], BF16) for i in range(2)]
        Pd = [sb(f"Pd{i}", [128, 4, 128], BF16) for i in range(2)]
        ores = sb("ores", [128, 520])
        pdi = 0
        for g in range(3):
            dg = DG[g]
            kview = kd[g].rearrange("(i s) c -> s i c", s=dg)
            vview = vd[g].rearrange("(i s) c -> s i c", s=dg)
            qview = qd[g].rearrange("(i s) c -> s i c", s=dg)
            oview = od[g].rearrange("(i s) c -> s i c", s=dg)
            nqb = 2048 // (128 * dg)
            for r_ in range(dg):
                for qb in range(nqb):
                    i0 = 2048 // dg + 128 * qb
                    S.dma("sp", qb_t[:], qview[r_, 128 * qb:128 * qb + 128, :], writes=[qb_t.b])
                    for kb in range(2):
                        ks = i0 - 128 + 128 * kb
                        S.dma("sp", kb_t[kb][:], kview[r_, ks:ks + 128, :], writes=[kb_t[kb].b])
                        S.dma("sp", vb_t[kb][:], vview[r_, ks:ks + 128, :], writes=[vb_t[kb].b])
                        S.op("dve", lambda e: e.tensor_copy(v1_t[kb][:, :, 0:64], vb_t[kb][:].rearrange("p (h d) -> p h d", d=64)),
                             reads=[vb_t[kb].b], writes=[v1_t[kb].b])
                        for hp in range(4):
                            S.op("pe", lambda e: e.transpose(pT[:, hp, :], kb_t[kb][:, hp * 128:(hp + 1) * 128], ident[:]),
                                 reads=[kb_t[kb].b, ident.b], writes=[pT.b])
                        S.op("act", lambda e: e.copy(kTt[kb][:], pT[:, 0:4, :]), reads=[pT.b], writes=[kTt[kb].b])
                    for hp in range(4):
                        S.op("pe", lambda e: e.transpose(pT[:, 4 + hp, :], qb_t[:, hp * 128:(hp + 1) * 128], ident[:]),
                             reads=[qb_t.b, ident.b], writes=[pT.b])
                    for h in range(8):
                        hp, hh = h // 2, h % 2
                        S.op("act", lambda e: e.copy(qTz[h][64 * hh:64 * hh + 64, :], pT[64 * hh:64 * hh + 64, 4 + hp, :]),
                             reads=[pT.b], writes=[qTz[h].b])
                    for hb4 in range(2):
                        pacc = pO if hb4 == 0 else pI
                        for kb in range(2):
                            pp = nextpp()
                            S.op("pe", lambda e: e.matmul(pp[:, :], ident[:], mneg[kb][:].rearrange("p j t -> p (j t)"),
                                                          start=True, stop=False, skip_group_check=True),
                                 reads=[ident.b, mneg[kb].b], writes=[pp.b])
                            for j in range(4):
                                h = 4 * hb4 + j
                                S.op("pe", lambda e: e.matmul(pp[:, 128 * j:128 * j + 128], kTt[kb][:, h // 2, :], qTz[h][:, :],
                                                              start=False, stop=(j == 3), skip_group_check=True),
                                     reads=[kTt[kb].b, qTz[h].b], writes=[pp.b])
                            P = Pd[pdi % 2]
                            pdi += 1
                            bias_ap = metat[:, 3:4] if (kb == 0 and qb == 0) else zeroc[:, 0:1]
                            S.op("act", lambda e: e.activation(P[:].rearrange("p j t -> p (j t)"), pp[:, :], AF.Exp, bias=bias_ap, scale=0.125),
                                 reads=[pp.b, metat.b, zeroc.b], writes=[P.b])
                            for j in range(4):
                                h = 4 * hb4 + j
                                S.op("pe", lambda e: e.matmul(pacc[:, 65 * j:65 * j + 65], P[:, j, :], v1_t[kb][:, h, :],
                                                              start=(kb == 0 and j == 0), stop=(kb == 1), skip_group_check=True),
                                     reads=[P.b, v1_t[kb].b], writes=[pacc.b])
                        S.op("act", lambda e: e.copy(ores[:, 260 * hb4:260 * hb4 + 260], pacc[:, 0:260]), reads=[pacc.b], writes=[ores.b])
                    S.dma("sp", oview[r_, 128 * qb:128 * qb + 128, :], ores[:], reads=[ores.b])
        S.finish()
        o3 = [sb(f"o3_{g}", [128, 8, 65]) for g in range(3)]
        rden = sb("rden", [128, 8])
        ab1 = sb("ab1", [128, 8, 64], BF16)
        for t in range(16):
            for g in range(3):
                S.dma("sp", o3[g][:].rearrange("p h c -> p (h c)"), od[g][t * 128:(t + 1) * 128, :], writes=[o3[g].b])
            S.op("dve", lambda e: e.tensor_tensor(o3[0][:], o3[0][:], o3[1][:], ALU.add), reads=[o3[0].b, o3[1].b], writes=[o3[0].b])
            S.op("dve", lambda e: e.tensor_tensor(o3[0][:], o3[0][:], o3[2][:], ALU.add), reads=[o3[0].b, o3[2].b], writes=[o3[0].b])
            S.op("dve", lambda e: e.reciprocal(rden[:], o3[0][:, :, 64]), reads=[o3[0].b], writes=[rden.b])
            S.op("dve", lambda e: e.tensor_tensor(ab1[:], o3[0][:, :, 0:64], rden[:].unsqueeze(2).to_broadcast([128, 8, 64]), ALU.mult),
                 reads=[o3[0].b, rden.b], writes=[ab1.b])
            S.dma("sp", attn1[t], ab1[:].rearrange("p h d -> p (h d)"), reads=[ab1.b])
        S.finish()
        stk.pop()

    with ExitStack() as esE:
        stk.append(esE)
        e_pi = sb("e_pi", [8, 8], I32)
        e_ji = sb("e_ji", [8, 8], I32)
        e_pf = sb("e_pf", [8, 8])
        e_jf = sb("e_jf", [8, 8])
        S.op("pool", lambda e: e.iota(e_pi[:], pattern=[[0, 8]], base=0, channel_multiplier=1), writes=[e_pi.b])
        S.op("pool", lambda e: e.iota(e_ji[:], pattern=[[1, 8]], base=0, channel_multiplier=0), writes=[e_ji.b])
        S.op("dve", lambda e: e.tensor_copy(e_pf[:], e_pi[:]), reads=[e_pi.b], writes=[e_pf.b])
        S.op("dve", lambda e: e.tensor_copy(e_jf[:], e_ji[:]), reads=[e_ji.b], writes=[e_jf.b])
        eq8 = sb("eq8", [8, 8])
        S.op("dve", lambda e: e.tensor_tensor(eq8[:], e_pf[:], e_jf[:], ALU.is_equal), reads=[e_pf.b, e_jf.b], writes=[eq8.b])
        onesb = sb("onesb", [128, 1], BF16)
        S.op("dve", lambda e: e.memset(onesb[:], 1.0), writes=[onesb.b])
        Kt = [sb(f"sdK{i}", [128, 1024]) for i in range(2)]
        K1 = [sb(f"sdK1{i}", [1, 1024]) for i in range(2)]
        qbc = [sb(f"sdq{i}", [128, 512]) for i in range(2)]
        Vb = sb("sdVb", [128, 512], BF16)
        V1b = sb("sdV1b", [1, 512], BF16)
        prod = sb("sdprod", [128, 8, 64])
        prod1 = sb("sdprod1", [1, 8, 64])
        sc = sb("sdsc", [128, 8])
        sc1 = sb("sdsc1", [1, 8])
        pex = sb("sdpex", [128, 8], BF16)
        pex1 = sb("sdpex1", [1, 8], BF16)
        numacc = sb("sdnum", [8, 8, 64])
        denacc = sb("sdden", [8, 1])
        dsel = sb("sddsel", [8, 8, 64])
        o8 = sb("sdo8", [8, 64])
        o8b = sb("sdo8b", [8, 64], BF16)
        ci = 0
        for sq in range(16):
            for t in range(4):
                row = 4 * sq + t
                for g in range(3):
                    dg = DG[g]
                    kt_, k1_, qb_ = Kt[ci % 2], K1[ci % 2], qbc[ci % 2]
                    ci += 1
                    if g == 0:
                        n_c = 128 - t
                        S.dma("sp", kt_[0:n_c, :], cache_dil[0][sq, t:128, :], writes=[kt_.b])
                        if t > 0:
                            S.dma("sp", kt_[n_c:128, :], ksd[0][4 * sq:4 * sq + t, :], writes=[kt_.b])
                    else:
                        S.dma("sp", kt_[:, :], cache_dil[g][sq].rearrange("(i s) c -> s i c", s=dg)[t, 0:128, :], writes=[kt_.b])
                    S.dma("sp", k1_[:, :], ksd[g][row:row + 1, :], writes=[k1_.b])
                    S.dma("sp", qb_[:, :], qsd[g][row].partition_broadcast(128), writes=[qb_.b])
                    S.op("dve", lambda e: e.tensor_tensor(prod[:], kt_[:, 0:512].rearrange("p (h d) -> p h d", d=64),
                                                          qb_[:].rearrange("p (h d) -> p h d", d=64), ALU.mult),
                         reads=[kt_.b, qb_.b], writes=[prod.b])
                    S.op("dve", lambda e: e.tensor_reduce(sc[:], prod[:], AX.X, ALU.add), reads=[prod.b], writes=[sc.b])
                    S.op("act", lambda e: e.activation(pex[:], sc[:], AF.Exp, scale=0.125), reads=[sc.b], writes=[pex.b])
                    S.op("dve", lambda e: e.tensor_tensor(prod1[:], k1_[0:1, 0:512].rearrange("p (h d) -> p h d", d=64),
                                                          qb_[0:1, :].rearrange("p (h d) -> p h d", d=64), ALU.mult),
                         reads=[k1_.b, qb_.b], writes=[prod1.b])
                    S.op("dve", lambda e: e.tensor_reduce(sc1[:], prod1[:], AX.X, ALU.add), reads=[prod1.b], writes=[sc1.b])
                    S.op("act", lambda e: e.activation(pex1[:], sc1[:], AF.Exp, scale=0.125), reads=[sc1.b], writes=[pex1.b])
                    S.op("act", lambda e: e.copy(Vb[:], kt_[:, 512:1024]), reads=[kt_.b], writes=[Vb.b])
                    S.op("act", lambda e: e.copy(V1b[:], k1_[0:1, 512:1024]), reads=[k1_.b], writes=[V1b.b])
                    pn = nextpp()
                    S.op("pe", lambda e: e.matmul(pn[0:8, :], pex[:, :], Vb[:, :], start=True, stop=False), reads=[pex.b, Vb.b], writes=[pn.b])
                    S.op("pe", lambda e: e.matmul(pn[0:8, :], pex1[:, :], V1b[:, :], start=False, stop=True), reads=[pex1.b, V1b.b], writes=[pn.b])
                    pdn = nextpp()
                    S.op("pe", lambda e: e.matmul(pdn[0:8, 0:1], pex[:, :], onesb[:, :], start=True, stop=False), reads=[pex.b, onesb.b], writes=[pdn.b])
                    S.op("pe", lambda e: e.matmul(pdn[0:8, 0:1], pex1[:, :], onesb[0:1, :], start=False, stop=True), reads=[pex1.b, onesb.b], writes=[pdn.b])
                    if g == 0:
                        S.op("act", lambda e: e.copy(numacc[:].rearrange("p h d -> p (h d)"), pn[0:8, :]), reads=[pn.b], writes=[numacc.b])
                        S.op("act", lambda e: e.copy(denacc[:], pdn[0:8, 0:1]), reads=[pdn.b], writes=[denacc.b])
                    else:
                        S.op("dve", lambda e: e.tensor_tensor(numacc[:].rearrange("p h d -> p (h d)"), numacc[:].rearrange("p h d -> p (h d)"),
                                                              pn[0:8, :], ALU.add), reads=[numacc.b, pn.b], writes=[numacc.b])
                        S.op("dve", lambda e: e.tensor_tensor(denacc[:], denacc[:], pdn[0:8, 0:1], ALU.add), reads=[denacc.b, pdn.b], writes=[denacc.b])
                S.op("dve", lambda e: e.tensor_tensor(dsel[:], numacc[:], eq8[:].unsqueeze(2).to_broadcast([8, 8, 64]), ALU.mult),
                     reads=[numacc.b, eq8.b], writes=[dsel.b])
                S.op("dve", lambda e: e.tensor_reduce(o8[:], dsel[:].rearrange("p h d -> p d h"), AX.X, ALU.add), reads=[dsel.b], writes=[o8.b])
                S.op("dve", lambda e: e.reciprocal(denacc[:], denacc[:]), reads=[denacc.b], writes=[denacc.b])
                S.op("dve", lambda e: e.tensor_scalar(o8b[:], o8[:], denacc[:, 0:1], None, ALU.mult), reads=[o8.b, denacc.b], writes=[o8b.b])
                S.dma("sp", attn1[16][row].rearrange("(h d) -> h d", d=64), o8b[:], reads=[o8b.b])
        S.finish()
        stk.pop()

    tiles1 = []
    for t in range(16):
        tiles1.append((h1s[16 + t], attn1[t], pe1[t], [(y_p[t * 128:(t + 1) * 128, :], 0, 128)]))
    tiles1.append((h1s[32], attn1[16], pe1[16], [(y_s[:, :], 0, 64)]))
    channel_phase(1, 512, c_w_out, tiles1)

    S.finish()
    es.close()
    return nc, S


_PROG = None


def kernel(x_prompt, x_sample, state_gla, cache_nsa_cmp, cache_nsa_slc, cache_nsa_win,
           cache_dil_0, cache_dil_1, cache_dil_2, page_table, p_prompt, p_sample,
           norm_mix, norm_mlp, norm_ple, a_w_in, a_w_out, gla_w_a2, gla_b_a, gla_g_norm,
           nsa_g_qk, nsa_w_phi, nsa_pe, c_w_in, c_g_qk, c_w_out, mlp_w1, mlp_w2,
           ple_w_proj, ple_w_gate):
    global _PROG
    if _PROG is None:
        _PROG = build_program()
    nc, S = _PROG
    f32 = np.float32
    x_prompt = np.asarray(x_prompt, f32)
    x_sample = np.asarray(x_sample, f32)
    in_maps = []
    pool_c_h = np.asarray(cache_nsa_cmp, f32)[0].reshape(2560, 128, 256)
    pool_s_h = np.asarray(cache_nsa_slc, f32)[0].reshape(2560, 128, 256)
    for c in range(NCORES):
        b, seg = c // 4, c % 4
        P0 = seg * SEG
        OFF = NSLOT - (P0 + SEG)
        xe = np.zeros((NSLOT, D), f32)
        xe[OFF:] = x_prompt[b, :P0 + SEG]
        meta = np.zeros((128, 8), f32)
        meta[:, 0] = OFF
        meta[:, 1] = 2048 + (np.arange(128) % 4)
        meta[:, 2] = OFF // 64
        meta[:, 3] = NEGB if OFF >= 6144 else 0.0
        xs = np.zeros((128, D), f32)
        xs[:64] = x_sample[16 * c:16 * c + 16].reshape(64, D)
        pp_ = np.asarray(p_prompt, f32)
        ps_ = np.asarray(p_sample, f32)
        pe0 = np.zeros((33, 128, 256), f32)
        pe1 = np.zeros((17, 128, 256), f32)
        for l, (pe_, nt_) in enumerate(((pe0, 32), (pe1, 16))):
            ext = np.zeros((nt_ * 128, 256), f32)
            lo = P0 + SEG - nt_ * 128
            src = pp_[l, b, max(lo, 0):P0 + SEG]
            ext[nt_ * 128 - len(src):] = src
            pe_[:nt_] = ext.reshape(nt_, 128, 256)
            pe_[nt_, :64] = ps_[l, 16 * c:16 * c + 16].reshape(64, 256)
        in_maps.append({
            "xe": xe, "meta": meta, "xs": xs, "pe0": pe0, "pe1": pe1,
            "cache_dil0": np.ascontiguousarray(np.asarray(cache_dil_0, f32)[0, 16 * c:16 * c + 16]).reshape(16, 128, 1024),
            "cache_dil1": np.ascontiguousarray(np.asarray(cache_dil_1, f32)[0, 16 * c:16 * c + 16]).reshape(16, 512, 1024),
            "cache_dil2": np.ascontiguousarray(np.asarray(cache_dil_2, f32)[0, 16 * c:16 * c + 16]).reshape(16, 2048, 1024),
            "page_tab": np.ascontiguousarray(np.asarray(page_table)[16 * c:16 * c + 16]).astype(np.int32),
            "pool_c": pool_c_h, "pool_s": pool_s_h,
            "c_w_in": np.asarray(c_w_in[0], f32), "c_g_qk": np.asarray(c_g_qk[0], f32).reshape(6, 64),
            "norm_mlp": np.asarray(norm_mlp, f32), "norm_ple": np.asarray(norm_ple, f32),
            "a_w_out": np.asarray(a_w_out[0], f32), "c_w_out": np.asarray(c_w_out[0], f32),
            "mlp_w1": np.asarray(mlp_w1, f32), "mlp_w2": np.asarray(mlp_w2, f32),
            "ple_w_proj": np.asarray(ple_w_proj, f32), "ple_w_gate": np.asarray(ple_w_gate, f32),
            "norm_mix": np.asarray(norm_mix, f32),
            "a_w_in": np.asarray(a_w_in[0], f32),
            "nsa_g_qk": np.asarray(nsa_g_qk[0], f32),
            "gla_w_a2": np.asarray(gla_w_a2[0], f32),
            "gla_b_a": np.asarray(gla_b_a[0], f32),
            "gla_g_norm": np.asarray(gla_g_norm[0], f32),
            "nsa_w_phi": np.asarray(nsa_w_phi[0], f32),
            "nsa_pe": np.asarray(nsa_pe[0], f32),
            "state_gla": np.ascontiguousarray(np.asarray(state_gla, f32)[0, 16 * c:16 * c + 16]),
            "cache_win": np.ascontiguousarray(np.asarray(cache_nsa_win, f32)[0, 16 * c:16 * c + 16]).reshape(16, 512, 256),
        })
    res = run_bass_kernel_spmd(nc, in_maps, core_ids=list(range(NCORES)))
    R = res.results
    global _DBG
    _DBG = R

    def prompt_rows(name):
        out = np.zeros((1, 2, SEQ, 2, 2, 64), f32)
        for c in range(NCORES):
            b, seg = c // 4, c % 4
            out[0, b, seg * SEG:(seg + 1) * SEG] = R[c][name].reshape(SEG, 2, 2, 64)
        return out

    def cat(name, shp):
        return np.concatenate([R[c][name].reshape(shp) for c in range(NCORES)], axis=0)[None]

    cmp_p = prompt_rows("o_cmp_p")
    slc_p = prompt_rows("o_slc_p")
    win_p = np.stack([R[3]["o_win_p"].reshape(512, 2, 2, 64), R[7]["o_win_p"].reshape(512, 2, 2, 64)])[None]
    gla_p = np.stack([R[3]["o_gla_p"], R[7]["o_gla_p"]])[None]
    gla_s = cat("o_gla_s", (16, 4, 64, 128))
    cmp_s = cat("o_cmp_s", (16, 4, 2, 2, 64))
    slc_s = cat("o_slc_s", (16, 4, 2, 2, 64))
    win_s = cat("o_win_s", (16, 512, 2, 2, 64))
    z = lambda *s: np.zeros(s, f32)
    yp = np.zeros((2, SEQ, D), f32)
    for c in range(NCORES):
        b, seg = c // 4, c % 4
        yp[b, seg * SEG:(seg + 1) * SEG] = R[c]["y_p"]
    ys = np.concatenate([R[c]["y_s"].reshape(16, 4, D) for c in range(NCORES)], axis=0)
    dil_p = [np.stack([R[3][f"o_dil_p{g}"].reshape(w, 2, 8, 64), R[7][f"o_dil_p{g}"].reshape(w, 2, 8, 64)])[None]
             for g, w in enumerate((128, 512, 2048))]
    return (yp, ys, gla_p, gla_s,
            cmp_p, cmp_s, slc_p, slc_s,
            win_p, win_s,
            dil_p[0], cat("o_dil_s0", (16, 128, 2, 8, 64)),
            dil_p[1], cat("o_dil_s1", (16, 512, 2, 8, 64)),
            dil_p[2], cat("o_dil_s2", (16, 2048, 2, 8, 64)))
```

```python
import math
import os
import numpy as np
from contextlib import ExitStack
import concourse.bass as bass
import concourse.mybir as mybir
from concourse.bass_utils import run_bass_kernel_spmd

F32 = mybir.dt.float32
BF16 = mybir.dt.bfloat16
I32 = mybir.dt.int32
AF = mybir.ActivationFunctionType
ALU = mybir.AluOpType
AX = mybir.AxisListType

NCORES = 8
D = 1024
SEQ = 8192
SEG = 2048
NSLOT = 8192
A_IN = 2856
EPS = 1e-6
NEGB = -30000.0


class Buf:
    __slots__ = ("w", "r")

    def __init__(self):
        self.w = None
        self.r = {}


class Eng:
    def __init__(self, name, obj, sem, sid, self_sync):
        self.name, self.obj, self.sem, self.sid, self.self_sync = name, obj, sem, sid, self_sync
        self.count = 0
        self.known = {}


class Sched:
    NDS = 24

    def __init__(self, nc, es):
        self.nc = nc
        self.es = es
        self.nsem = 0
        self.E = {
            "pe": Eng("pe", nc.tensor, self.mk(), 0, False),
            "act": Eng("act", nc.scalar, self.mk(), 1, True),
            "dve": Eng("dve", nc.vector, self.mk(), 2, True),
            "pool": Eng("pool", nc.gpsimd, self.mk(), 3, True),
            "sp": Eng("sp", nc.sync, self.mk(), 4, False),
        }
        self.next_sid = 1000
        self.dsems = [self.mk() for i in range(self.NDS)]
        self.dsid = [100 + i for i in range(self.NDS)]
        self.dcnt = [0] * self.NDS
        self.retired = []
        self.dma_i = 0
        self.ninst = 0

    def mk(self):
        self.nsem += 1
        return self.es.enter_context(self.nc.semaphore(f"s{self.nsem}"))

    def _roll(self, eng):
        if eng.count >= 30000:
            eng.sem = self.mk()
            eng.sid = self.next_sid
            self.next_sid += 1
            eng.count = 0

    def _wait(self, eng, tok):
        sem, val, sid = tok
        if eng.known.get(sid, 0) >= val:
            return
        eng.obj.wait_ge(sem, val)
        eng.known[sid] = val
        self.ninst += 1

    def _deps(self, eng, reads, writes):
        for b in reads:
            if b.w is not None:
                yield b.w
        for b in writes:
            if b.w is not None:
                yield b.w
            for t in b.r.values():
                yield t

    def _record(self, tok, reads, writes):
        for b in reads:
            b.r[tok[2]] = tok
        for b in writes:
            b.w = tok
            b.r = {}

    def op(self, ename, fn, reads=(), writes=()):
        eng = self.E[ename]
        for t in list(self._deps(eng, reads, writes)):
            if t[2] == eng.sid and not eng.self_sync:
                continue
            self._wait(eng, t)
        self._roll(eng)
        ins = fn(eng.obj)
        eng.count += 1
        ins.then_inc(eng.sem, 1)
        self.ninst += 1
        tok = (eng.sem, eng.count, eng.sid)
        eng.last = tok
        self._record(tok, reads, writes)
        return tok

    def dma(self, qname, out, in_, reads=(), writes=(), fn=None, **kw):
        eng = self.E[qname]
        for t in list(self._deps(eng, reads, writes)):
            self._wait(eng, t)
        k = self.dma_i % self.NDS
        self.dma_i += 1
        if self.dcnt[k] >= 1800:
            self.retired.append((self.dsems[k], 16 * self.dcnt[k], self.dsid[k]))
            self.dsems[k] = self.mk()
            self.dsid[k] = self.next_sid
            self.next_sid += 1
            self.dcnt[k] = 0
        sem = self.dsems[k]
        sid = self.dsid[k]
        if self.dcnt[k] > 0:
            self._wait(eng, (sem, 16 * self.dcnt[k], sid))
        self.dcnt[k] += 1
        if fn is not None:
            fn(eng.obj).then_inc(sem, 16)
        else:
            eng.obj.dma_start(out=out, in_=in_, **kw).then_inc(sem, 16)
        self.ninst += 1
        tok = (sem, 16 * self.dcnt[k], sid)
        self._record(tok, reads, writes)
        return tok

    def finish(self):
        for eng in self.E.values():
            for o in self.E.values():
                if o is not eng and getattr(o, "last", None) is not None:
                    self._wait(eng, o.last)
            for k in range(self.NDS):
                if self.dcnt[k] > 0:
                    self._wait(eng, (self.dsems[k], 16 * self.dcnt[k], self.dsid[k]))
            for t in self.retired:
                self._wait(eng, t)


class T:
    def __init__(self, h):
        self.h = h
        self.b = Buf()

    def __getitem__(self, k):
        return self.h[k]


def build_program():
    nc = bass.Bass("TRN2", target_bir_lowering=False)
    es = ExitStack()
    S = Sched(nc, es)

    def din(name, shape, dt=F32):
        return nc.dram_tensor(name, list(shape), dt, kind="ExternalInput").ap()

    def dout(name, shape, dt=F32):
        return nc.dram_tensor(name, list(shape), dt, kind="ExternalOutput").ap()

    stk = [es]

    used_names = {}

    def sb(name, shape, dt=F32):
        k = used_names.get(name, 0)
        used_names[name] = k + 1
        if k:
            name = f"{name}_v{k}"
        return T(stk[-1].enter_context(nc.sbuf_tensor(name, list(shape), dt)))

    def ps(name, shape, dt=F32):
        return T(es.enter_context(nc.psum_tensor(name, list(shape), dt)))

    xe = din("xe", [NSLOT, D])
    meta = din("meta", [128, 8])
    norm_mix = din("norm_mix", [2, D])
    a_w_in = din("a_w_in", [D, A_IN])
    nsa_g_qk = din("nsa_g_qk", [4, 64])
    xs = din("xs", [128, D])
    gla_w_a2 = din("gla_w_a2", [16, 256])
    gla_b_a = din("gla_b_a", [256])
    state_gla = din("state_gla", [16, 4, 64, 128])
    cache_win = din("cache_win", [16, 512, 256])
    gla_g_norm = din("gla_g_norm", [128])
    nsa_w_phi = din("nsa_w_phi", [2, 32, 64, 64])
    nsa_pe = din("nsa_pe", [2, 32, 64])
    c_w_in = din("c_w_in", [D, 4608])
    c_g_qk = din("c_g_qk", [6, 64])
    kd = [nc.dram_tensor(f"kd{g}", [4096, 512], BF16, kind="Internal").ap() for g in range(3)]
    vd = [nc.dram_tensor(f"vd{g}", [4096, 512], BF16, kind="Internal").ap() for g in range(3)]
    qd = [nc.dram_tensor(f"qd{g}", [2048, 512], BF16, kind="Internal").ap() for g in range(3)]
    od = [nc.dram_tensor(f"od{g}", [2048, 520], F32, kind="Internal").ap() for g in range(3)]
    qsd = nc.dram_tensor("qsd", [3, 128, 512], F32, kind="Internal").ap()
    ksd = nc.dram_tensor("ksd", [3, 128, 1024], F32, kind="Internal").ap()
    cache_dil = [din(f"cache_dil{g}", [16, w, 1024]) for g, w in enumerate((128, 512, 2048))]
    page_tab = din("page_tab", [16, 16], I32)
    pool_c = din("pool_c", [2560, 128, 256])
    pool_s = din("pool_s", [2560, 128, 256])
    sproj_d = nc.dram_tensor("sproj_d", [128, A_IN], F32, kind="Internal").ap()
    pe0 = din("pe0", [33, 128, 256])
    pe1 = din("pe1", [17, 128, 256])
    norm_mlp = din("norm_mlp", [2, D])
    norm_ple = din("norm_ple", [2, D])
    a_w_out = din("a_w_out", [1024, D])
    c_w_out = din("c_w_out", [512, D])
    mlp_w1 = din("mlp_w1", [2, D, 4096])
    mlp_w2 = din("mlp_w2", [2, 4096, D])
    ple_w_proj = din("ple_w_proj", [2, 256, D])
    ple_w_gate = din("ple_w_gate", [2, D, D])
    attn0 = nc.dram_tensor("attn0", [33, 128, 1024], BF16, kind="Internal").ap()
    attn1 = nc.dram_tensor("attn1", [17, 128, 512], BF16, kind="Internal").ap()
    h1s = nc.dram_tensor("h1s", [33, 128, D], F32, kind="Internal").ap()

    o_cmp_p = dout("o_cmp_p", [SEG, 256])
    o_slc_p = dout("o_slc_p", [SEG, 256])
    o_win_p = dout("o_win_p", [512, 256])
    o_cmp_s = dout("o_cmp_s", [64, 256])
    o_slc_s = dout("o_slc_s", [64, 256])
    o_win_s = dout("o_win_s", [16, 512, 256])
    o_gla_p = dout("o_gla_p", [4, 64, 128])
    o_gla_s = dout("o_gla_s", [16, 4, 64, 128])
    dbg_attn = dout("dbg_attn", [SEG, 1024])
    dbg_h1 = dout("dbg_h1", [SEG, 1024])
    dbg_h1s = dout("dbg_h1s", [128, 1024])
    dbg_as = dout("dbg_as", [64, 1024])
    y_p = dout("y_p", [SEG, D])
    o_dil_p = [dout(f"o_dil_p{g}", [w, 1024]) for g, w in enumerate((128, 512, 2048))]
    o_dil_s = [dout(f"o_dil_s{g}", [16, w, 1024]) for g, w in enumerate((128, 512, 2048))]
    y_s = dout("y_s", [64, D])

    ident = sb("ident", [128, 128], BF16)
    identf = sb("identf", [128, 128], F32)
    iot = sb("iot", [128, 128], I32)
    S.op("pool", lambda e: e.iota(iot[:], pattern=[[1, 128]], base=0, channel_multiplier=-1), writes=[iot.b])
    S.op("dve", lambda e: e.tensor_copy(identf[:], iot[:]), reads=[iot.b], writes=[identf.b])
    S.op("dve", lambda e: e.tensor_scalar(identf[:], identf[:], 0.0, None, ALU.is_equal), reads=[identf.b], writes=[identf.b])
    S.op("dve", lambda e: e.tensor_copy(ident[:], identf[:]), reads=[identf.b], writes=[ident.b])

    epsc = sb("epsc", [128, 1])
    S.op("dve", lambda e: e.memset(epsc[:], EPS), writes=[epsc.b])
    metat = sb("metat", [128, 8])
    S.dma("sp", metat[:], meta[:, :], writes=[metat.b])

    NT = NSLOT // 128
    posi = sb("posi", [128, NT], I32)
    posf = sb("posf", [128, NT + 1])
    S.op("pool", lambda e: e.iota(posi[:], pattern=[[128, NT]], base=0, channel_multiplier=1), writes=[posi.b])
    S.op("dve", lambda e: e.tensor_copy(posf[:, 0:NT], posi[:]), reads=[posi.b], writes=[posf.b])
    S.op("dve", lambda e: e.tensor_scalar(posf[:, 0:NT], posf[:, 0:NT], metat[:, 0:1], 0.0, ALU.subtract, ALU.max),
         reads=[posf.b, metat.b], writes=[posf.b])
    S.op("dve", lambda e: e.tensor_copy(posf[:, NT:NT + 1], metat[:, 1:2]), reads=[posf.b, metat.b], writes=[posf.b])
    fri = sb("fri", [128, 32], I32)
    frq = sb("frq", [128, 32])
    S.op("pool", lambda e: e.iota(fri[:], pattern=[[1, 32]], base=0, channel_multiplier=0), writes=[fri.b])
    S.op("dve", lambda e: e.tensor_copy(frq[:], fri[:]), reads=[fri.b], writes=[frq.b])
    S.op("act", lambda e: e.activation(frq[:], frq[:], AF.Exp, scale=-math.log(10000.0) / 32.0),
         reads=[frq.b], writes=[frq.b])
    TWO_PI = 2.0 * math.pi
    ang = sb("ang", [128, 32])
    angi = sb("angi", [128, 32], I32)
    angn = sb("angn", [128, 32])
    cs_t = sb("cs_t", [128, 32])
    sn_t = sb("sn_t", [128, 32])

    def sin_of(dst, shift):
        S.op("dve", lambda e: e.tensor_scalar(dst[:], ang[:], shift, 1.0 / TWO_PI, ALU.add, ALU.mult),
             reads=[ang.b], writes=[dst.b])
        S.op("dve", lambda e: e.tensor_copy(angi[:], dst[:]), reads=[dst.b], writes=[angi.b])
        S.op("dve", lambda e: e.tensor_copy(angn[:], angi[:]), reads=[angi.b], writes=[angn.b])
        S.op("dve", lambda e: e.tensor_tensor(dst[:], dst[:], angn[:], ALU.subtract), reads=[dst.b, angn.b], writes=[dst.b])
        S.op("dve", lambda e: e.tensor_scalar(dst[:], dst[:], TWO_PI, math.pi, ALU.mult, ALU.min), reads=[dst.b], writes=[dst.b])
        S.op("dve", lambda e: e.tensor_scalar(dst[:], dst[:], -math.pi, None, ALU.max), reads=[dst.b], writes=[dst.b])
        S.op("act", lambda e: e.activation(dst[:], dst[:], AF.Sin), reads=[dst.b], writes=[dst.b])

    def rope_tables(ti):
        S.op("dve", lambda e: e.tensor_scalar(ang[:], frq[:], posf[:, ti:ti + 1], None, ALU.mult),
             reads=[frq.b, posf.b], writes=[ang.b])
        sin_of(sn_t, 0.0)
        sin_of(cs_t, 0.5 * math.pi)

    gmix = sb("gmix", [128, 2, 8])
    S.dma("sp", gmix[:], norm_mix.rearrange("l (c p) -> p l c", p=128), writes=[gmix.b],
          allow_slow_non_contiguous=True)
    gqk = sb("gqk", [128, 4, 64])
    S.dma("sp", gqk[:], nsa_g_qk.rearrange("a d -> (a d)").partition_broadcast(128), writes=[gqk.b])

    pidx_i = sb("pidx_i", [128, 128], I32)
    jidx_i = sb("jidx_i", [128, 128], I32)
    pidx = sb("pidx", [128, 128])
    jidx = sb("jidx", [128, 128])
    S.op("pool", lambda e: e.iota(pidx_i[:], pattern=[[0, 128]], base=0, channel_multiplier=1), writes=[pidx_i.b])
    S.op("pool", lambda e: e.iota(jidx_i[:], pattern=[[1, 128]], base=0, channel_multiplier=0), writes=[jidx_i.b])
    S.op("dve", lambda e: e.tensor_copy(pidx[:], pidx_i[:]), reads=[pidx_i.b], writes=[pidx.b])
    S.op("dve", lambda e: e.tensor_copy(jidx[:], jidx_i[:]), reads=[jidx_i.b], writes=[jidx.b])
    fd_i = sb("fd_i", [128, 128], I32)

    def floordiv(dst, src, n):
        S.op("dve", lambda e: e.tensor_scalar(dst[:], src[:], 0.5, 1.0 / n, ALU.add, ALU.mult), reads=[src.b], writes=[dst.b])
        S.op("dve", lambda e: e.tensor_scalar(dst[:], dst[:], -0.5, None, ALU.add), reads=[dst.b], writes=[dst.b])
        S.op("dve", lambda e: e.tensor_copy(fd_i[:], dst[:]), reads=[dst.b], writes=[fd_i.b])
        S.op("dve", lambda e: e.tensor_copy(dst[:], fd_i[:]), reads=[fd_i.b], writes=[dst.b])

    pgt = sb("pgt", [128, 128])
    S.op("dve", lambda e: e.tensor_tensor(pgt[:], pidx[:], jidx[:], ALU.is_gt), reads=[pidx.b, jidx.b], writes=[pgt.b])
    pc_t = sb("pc_t", [128, 128])
    jc_t = sb("jc_t", [128, 128])
    Lm = {}
    Ci = {}
    for csz in (64, 4):
        floordiv(pc_t, pidx, csz)
        floordiv(jc_t, jidx, csz)
        lm = sb(f"Lm{csz}", [128, 128])
        ci = sb(f"Ci{csz}", [128, 32])
        S.op("dve", lambda e: e.tensor_tensor(lm[:], pc_t[:], jc_t[:], ALU.is_equal), reads=[pc_t.b, jc_t.b], writes=[lm.b])
        S.op("dve", lambda e: e.tensor_tensor(lm[:], lm[:], pgt[:], ALU.mult), reads=[lm.b, pgt.b], writes=[lm.b])
        S.op("dve", lambda e: e.tensor_tensor(ci[:], pc_t[:, 0:32], jidx[:, 0:32], ALU.is_equal),
             reads=[pc_t.b, jidx.b], writes=[ci.b])
        Lm[csz] = lm
        Ci[csz] = ci
    onec = sb("onec", [128, 1])
    S.op("dve", lambda e: e.memset(onec[:], 1.0), writes=[onec.b])
    ones1 = sb("ones1", [1, 128], BF16)
    S.op("dve", lambda e: e.memset(ones1[:], 1.0), writes=[ones1.b])
    wa2 = sb("wa2", [16, 256], BF16)
    S.dma("pool", wa2[:], gla_w_a2[:, :], writes=[wa2.b])
    bab = sb("bab", [1, 256], BF16)
    S.dma("pool", bab[:], gla_b_a.rearrange("(o n) -> o n", o=1), writes=[bab.b])

    Um = {}
    ple_t = sb("ple_t", [128, 128])
    S.op("dve", lambda e: e.tensor_tensor(ple_t[:], pidx[:], jidx[:], ALU.is_le), reads=[pidx.b, jidx.b], writes=[ple_t.b])
    Cj4 = sb("Cj4", [128, 128])
    for csz in (64, 4):
        floordiv(pc_t, pidx, csz)
        floordiv(jc_t, jidx, csz)
        um = sb(f"Um{csz}", [128, 128])
        S.op("dve", lambda e: e.tensor_tensor(um[:], pc_t[:], jc_t[:], ALU.is_equal), reads=[pc_t.b, jc_t.b], writes=[um.b])
        S.op("dve", lambda e: e.tensor_tensor(um[:], um[:], ple_t[:], ALU.mult), reads=[um.b, ple_t.b], writes=[um.b])
        Um[csz] = um
    S.op("dve", lambda e: e.tensor_copy(Cj4[:], jc_t[:]), reads=[jc_t.b], writes=[Cj4.b])
    causneg = sb("causneg", [128, 4, 128], BF16)
    winneg = sb("winneg", [128, 4, 128], BF16)
    S.op("dve", lambda e: e.tensor_scalar(causneg[:], pgt[:].unsqueeze(1).to_broadcast([128, 4, 128]), NEGB, None, ALU.mult),
         reads=[pgt.b], writes=[causneg.b])
    S.op("dve", lambda e: e.tensor_scalar(winneg[:], ple_t[:].unsqueeze(1).to_broadcast([128, 4, 128]), NEGB, None, ALU.mult),
         reads=[ple_t.b], writes=[winneg.b])
    V16 = sb("V16", [128, 128])
    S.op("dve", lambda e: e.scalar_tensor_tensor(V16[:], pidx[:], -16.0, jidx[:], ALU.mult, ALU.add),
         reads=[pidx.b, jidx.b], writes=[V16.b])
    BLK = sb("BLK", [128, 128])
    S.op("dve", lambda e: e.tensor_scalar(BLK[:], jidx[:], 64.0, None, ALU.is_ge), reads=[jidx.b], writes=[BLK.b])
    S.op("dve", lambda e: e.tensor_tensor(BLK[:], pidx[:], BLK[:], ALU.subtract), reads=[pidx.b, BLK.b], writes=[BLK.b])
    hi64 = sb("hi64", [128, 1])
    S.op("dve", lambda e: e.tensor_scalar(hi64[:], pidx[:, 0:1], 64.0, None, ALU.is_ge), reads=[pidx.b], writes=[hi64.b])
    C2S = sb("C2S", [128, 4, 128], BF16)
    c2a = sb("c2a", [128, 128])
    c2b = sb("c2b", [128, 128])
    for ct in range(4):
        S.op("dve", lambda e: e.tensor_scalar(c2a[:], pidx[:], 16.0, 2048.0 * ct + 32.0, ALU.mult, ALU.add), reads=[pidx.b], writes=[c2a.b])
        S.op("dve", lambda e: e.tensor_scalar(c2b[:], jidx[:], 64.0, 64.0, ALU.mult, ALU.add), reads=[jidx.b], writes=[c2b.b])
        S.op("dve", lambda e: e.tensor_tensor(c2a[:], c2a[:], c2b[:], ALU.min), reads=[c2a.b, c2b.b], writes=[c2a.b])
        S.op("dve", lambda e: e.tensor_scalar(c2b[:], c2b[:], -64.0 - 2048.0 * ct, None, ALU.add), reads=[c2b.b], writes=[c2b.b])
        S.op("dve", lambda e: e.scalar_tensor_tensor(c2b[:], pidx[:], 16.0, c2b[:], ALU.mult, ALU.max), reads=[pidx.b, c2b.b], writes=[c2b.b])
        S.op("dve", lambda e: e.tensor_scalar(c2b[:], c2b[:], 2048.0 * ct, None, ALU.add), reads=[c2b.b], writes=[c2b.b])
        S.op("dve", lambda e: e.tensor_tensor(c2a[:], c2a[:], c2b[:], ALU.subtract), reads=[c2a.b, c2b.b], writes=[c2a.b])
        S.op("dve", lambda e: e.tensor_scalar(C2S[:, ct, :], c2a[:], 0.0, 1.0 / 32.0, ALU.max, ALU.mult), reads=[c2a.b], writes=[C2S.b])
    padk = sb("padk", [128, 64])
    S.op("dve", lambda e: e.tensor_scalar(padk[:], jidx[:, 0:64], 128.0, metat[:, 0:1], ALU.mult, ALU.is_lt),
         reads=[jidx.b, metat.b], writes=[padk.b])
    S.op("dve", lambda e: e.tensor_scalar(padk[:], padk[:], NEGB, None, ALU.mult), reads=[padk.b], writes=[padk.b])
    padc = sb("padc", [128, 4])
    for ct in range(4):
        S.op("dve", lambda e: e.tensor_scalar(padc[:, ct:ct + 1], pidx[:, 0:1], 16.0, 2048.0 * ct, ALU.mult, ALU.add),
             reads=[pidx.b], writes=[padc.b])
    S.op("dve", lambda e: e.tensor_scalar(padc[:], padc[:], metat[:, 0:1], NEGB, ALU.is_lt, ALU.mult),
         reads=[padc.b, metat.b], writes=[padc.b])
    gno = sb("gno", [128, 128])
    S.dma("sp", gno[:], gla_g_norm.partition_broadcast(128), writes=[gno.b])

    esA = ExitStack()
    stk.append(esA)
    win = sb("win", [128, 8, A_IN], BF16)
    a_w_in_v = a_w_in.rearrange("(c p) n -> p c n", p=128)
    for c in range(8):
        S.dma("pool", win[:, c, :], a_w_in_v[:, c, :], writes=[win.b])

    Wbd = [sb(f"Wbd{a}", [128, 32, 128], BF16) for a in range(2)]
    peT = sb("peT", [128, 2, 32], BF16)
    with ExitStack() as est:
        stk.append(est)
        wst = sb("wst", [128, 32, 64])
        pst = sb("pst", [128, 2, 32])
        for a in range(2):
            S.op("pool", lambda e: e.memset(Wbd[a][:], 0.0), writes=[Wbd[a].b])
            for g in range(2):
                for q in range(4):
                    S.dma("sp", wst[64 * g:64 * g + 64, 8 * q:8 * q + 8, :],
                          nsa_w_phi[a, 8 * q:8 * q + 8].rearrange("p d e -> d p e"), writes=[wst.b])
                S.dma("sp", pst[64 * g:64 * g + 64, a, :], nsa_pe[a].rearrange("p d -> d p"), writes=[pst.b],
                      allow_slow_non_contiguous=True)
            for g in range(2):
                S.op("dve", lambda e: e.tensor_copy(Wbd[a][64 * g:64 * g + 64, :, 64 * g:64 * g + 64], wst[64 * g:64 * g + 64, :, :]),
                     reads=[wst.b, Wbd[a].b], writes=[Wbd[a].b])
        S.op("dve", lambda e: e.tensor_copy(peT[:], pst[:]), reads=[pst.b], writes=[peT.b])
        S.finish()
        stk.pop()
    pT = ps("pT", [128, 8, 128], BF16)
    pP = [ps(f"pP{i}", [128, 512]) for i in range(3)]
    pS = ps("pS", [128, 512])
    pO = ps("pO", [128, 512])
    pI = ps("pI", [128, 512])
    ppi = 0

    def nextpp():
        nonlocal ppi
        ppi += 1
        return pP[ppi % 3]

    cbias = sb("cbias", [128, 2])
    pb0 = nextpp()
    for a in range(2):
        for p in range(32):
            S.op("pe", lambda e: e.matmul(pb0[:, 2 * a:2 * a + 2], Wbd[a][:, p, :], peT[:, :, p], start=(p == 0), stop=(p == 31)),
                 reads=[Wbd[a].b, peT.b], writes=[pb0.b])
    S.op("act", lambda e: e.copy(cbias[:, 0:1], pb0[:, 0:1]), reads=[pb0.b], writes=[cbias.b])
    S.op("act", lambda e: e.copy(cbias[:, 1:2], pb0[:, 3:4]), reads=[pb0.b], writes=[cbias.b])

    xt = [sb(f"xt{i}", [128, D]) for i in range(2)]
    ssq = sb("ssq", [128, 1])
    xn = sb("xn", [128, D], BF16)
    hnT = sb("hnT", [128, 8, 128], BF16)
    proj = sb("proj", [128, A_IN])
    sq14 = sb("sq14", [128, 8, 64])
    ss14 = sb("ss14", [128, 8])
    rows = sb("rows", [128, 768])
    tmpa = sb("tmpa", [128, 8, 32])
    tmpb = sb("tmpb", [128, 8, 32])
    gaT = sb("gaT", [16, 128], BF16)
    gab = sb("gab", [128, 256])
    gmn = sb("gmn", [128, 256])
    lg = sb("lg", [128, 256])
    ecs = sb("ecs", [128, 256])
    khat = sb("khat", [128, 256], BF16)
    khm = sb("khm", [64, 256], BF16)
    vb = sb("vb", [128, 512], BF16)
    dec = sb("dec", [64, 4, 16])
    Sst = sb("Sst", [64, 4, 128])
    ebt = sb("ebt", [128, 128])
    qtT = [sb(f"qtT{h}", [128, 128], BF16) for h in range(2)]
    ktT = [sb(f"ktT{h}", [128, 128], BF16) for h in range(2)]
    qz = [[sb(f"qz{j}{h}", [128, 128], BF16) for h in range(2)] for j in range(2)]
    qtTz = [[sb(f"qtTz{hf}{hh}", [128, 128], BF16) for hh in range(2)] for hf in range(2)]
    Sbz = [[sb(f"Sbz{j}{h}", [128, 128], BF16) for h in range(4)] for j in range(2)]
    khz = [sb(f"khz{j}", [128, 256], BF16) for j in range(2)]
    for j in range(2):
        for h in range(2):
            S.op("pool", lambda e: e.memset(qz[j][h][:], 0.0), writes=[qz[j][h].b])
            S.op("pool", lambda e: e.memset(qtTz[j][h][:], 0.0), writes=[qtTz[j][h].b])
        for h in range(4):
            S.op("pool", lambda e: e.memset(Sbz[j][h][:], 0.0), writes=[Sbz[j][h].b])
    QcTz = [sb(f"QcTz{g}", [128, 4, 128], BF16) for g in range(2)]
    QrTz = [sb(f"QrTz{g}", [128, 4, 128], BF16) for g in range(2)]
    for g in range(2):
        S.op("pool", lambda e: e.memset(QcTz[g][:], 0.0), writes=[QcTz[g].b])
        S.op("pool", lambda e: e.memset(QrTz[g][:], 0.0), writes=[QrTz[g].b])
    lo64 = sb("lo64", [128, 1])
    S.op("dve", lambda e: e.tensor_scalar(lo64[:], hi64[:], -1.0, 1.0, ALU.mult, ALU.add), reads=[hi64.b], writes=[lo64.b])
    ATb = sb("ATb", [128, 4, 128], BF16)
    ogf = sb("ogf", [128, 4, 128])
    ogs = sb("ogs", [128, 4, 128])
    sgr = sb("sgr", [128, 512])
    attnb = sb("attnb", [128, 1024], BF16)
    dbgt = sb("dbgt", [128, 1024])
    xsq = dbgt
    qn = sb("qn", [128, 8, 64])
    qcp = sb("qcp", [128, 4, 128], BF16)
    qrp = sb("qrp", [128, 4, 128], BF16)
    kb4 = sb("kb4", [128, 4, 128], BF16)
    kT4 = sb("kT4", [128, 4, 128], BF16)
    gts = sb("gts", [128, 24])
    onsa = sb("onsa", [128, 8, 64])
    otmp = sb("otmp", [128, 4, 64])
    coef = sb("coef", [128, 4])
    rl4 = sb("rl4", [128, 4])
    Pt = [sb(f"Pt{i}", [128, 4, 128], BF16) for i in range(2)]
    mk = sb("mk", [128, 128], BF16)
    Ekt = sb("Ekt", [128, 128], BF16)
    Ekt2 = [Ekt, sb("Ektb", [128, 128], BF16)]
    imp = sb("imp", [128, 128])
    imw = sb("imw", [128, 128])
    vm = sb("vm", [128, 128])
    fm = sb("fm", [128, 128])
    curc = sb("curc", [128, 1])
    mx8 = sb("mx8", [128, 8])
    seln = sb("seln", [128, 128], BF16)
    selnT = [sb(f"selnT{g}", [128, 4, 128], BF16) for g in range(2)]

    xe_v = xe.rearrange("(t p) d -> t p d", p=128)
    groups = [(0, 512), (512, 1024), (1024, 1536), (1536, 2048), (2048, 2560), (2560, A_IN)]

    def qknorm(src_ap, nh, gidx, dst_ap, rd, wr):
        S.op("dve", lambda e: e.tensor_tensor(sq14[:, 0:nh, :], src_ap, src_ap, ALU.mult), reads=rd, writes=[sq14.b])
        S.op("dve", lambda e: e.tensor_reduce(ss14[:, 0:nh], sq14[:, 0:nh, :], AX.X, ALU.add),
             reads=[sq14.b], writes=[ss14.b])
        S.op("act", lambda e: e.activation(ss14[:, 0:nh], ss14[:, 0:nh], AF.Sqrt, bias=epsc[:, 0:1], scale=1.0 / 64.0),
             reads=[ss14.b, epsc.b], writes=[ss14.b])
        S.op("dve", lambda e: e.reciprocal(ss14[:, 0:nh], ss14[:, 0:nh]), reads=[ss14.b], writes=[ss14.b])
        S.op("dve", lambda e: e.tensor_tensor(dst_ap, src_ap, ss14[:, 0:nh].unsqueeze(2).to_broadcast([128, nh, 64]),
                                              ALU.mult), reads=rd + [ss14.b], writes=wr)
        S.op("dve", lambda e: e.tensor_tensor(dst_ap, dst_ap,
                                              gqk[:, gidx, :].unsqueeze(1).to_broadcast([128, nh, 64]), ALU.mult),
             reads=wr + [gqk.b], writes=wr)

    def rope(ap3, nh, ti, bufs):
        x1 = ap3[:, :, 0:32]
        x2 = ap3[:, :, 32:64]
        cs = cs_t[:].unsqueeze(1).to_broadcast([128, nh, 32])
        sn = sn_t[:].unsqueeze(1).to_broadcast([128, nh, 32])
        ta = tmpa[:, 0:nh, :]
        tb = tmpb[:, 0:nh, :]
        S.op("dve", lambda e: e.tensor_tensor(ta, x1, sn, ALU.mult), reads=bufs + [sn_t.b], writes=[tmpa.b])
        S.op("dve", lambda e: e.tensor_tensor(tb, x2, sn, ALU.mult), reads=bufs + [sn_t.b], writes=[tmpb.b])
        S.op("dve", lambda e: e.tensor_tensor(x1, x1, cs, ALU.mult), reads=bufs + [cs_t.b], writes=bufs)
        S.op("dve", lambda e: e.tensor_tensor(x2, x2, cs, ALU.mult), reads=bufs + [cs_t.b], writes=bufs)
        S.op("dve", lambda e: e.tensor_tensor(x1, x1, tb, ALU.subtract), reads=bufs + [tmpb.b], writes=bufs)
        S.op("dve", lambda e: e.tensor_tensor(x2, x2, ta, ALU.add), reads=bufs + [tmpa.b], writes=bufs)

    def project_tile(ti, src_ap, xbuf):
        S.dma("sp", xbuf[:], src_ap, writes=[xbuf.b])
        if int(os.environ.get('KROPE', 1)):
            rope_tables(ti)
        S.op("act", lambda e: e.activation(xsq[:], xbuf[:], AF.Square, accum_out=ssq[:]),
             reads=[xbuf.b], writes=[xsq.b, ssq.b])
        S.op("act", lambda e: e.activation(ssq[:], ssq[:], AF.Sqrt, bias=epsc[:, 0:1], scale=1.0 / D),
             reads=[ssq.b, epsc.b], writes=[ssq.b])
        S.op("dve", lambda e: e.reciprocal(ssq[:], ssq[:]), reads=[ssq.b], writes=[ssq.b])
        S.op("dve", lambda e: e.tensor_scalar(xn[:], xbuf[:], ssq[:, 0:1], None, ALU.mult),
             reads=[xbuf.b, ssq.b], writes=[xn.b])
        for c in range(8):
            S.op("pe", lambda e: e.transpose(pT[:, c, :], xn[:, c * 128:(c + 1) * 128], ident[:]),
                 reads=[xn.b, ident.b], writes=[pT.b])
        S.op("dve", lambda e: e.tensor_tensor(hnT[:], pT[:], gmix[:, 0, :].unsqueeze(2).to_broadcast([128, 8, 128]),
                                              ALU.mult), reads=[pT.b, gmix.b], writes=[hnT.b])
        for (c0, c1) in groups:
            pp = nextpp()
            n = c1 - c0
            for c in range(8):
                S.op("pe", lambda e: e.matmul(pp[:, 0:n], hnT[:, c, :], win[:, c, c0:c1],
                                              start=(c == 0), stop=(c == 7)),
                     reads=[hnT.b, win.b], writes=[pp.b])
            S.op("act", lambda e: e.copy(proj[:, c0:c1], pp[:, 0:n]), reads=[pp.b], writes=[proj.b])
        r = rows
        S.op("act", lambda e: e.copy(r[:], proj[:, 2064:2832]), reads=[proj.b], writes=[r.b])
        for (kidx, gidx) in ((0, 1), (2, 2), (4, 3)):
            src = proj[:, 2064 + kidx * 128: 2064 + (kidx + 1) * 128].rearrange("p (h d) -> p h d", d=64)
            dst = r[:, kidx * 128:(kidx + 1) * 128].rearrange("p (h d) -> p h d", d=64)
            qknorm(src, 2, gidx, dst, [proj.b], [r.b])
            if kidx > 0:
                rope(dst, 2, ti, [r.b])

    def gla_gates(csz, nch):
        KG = int(os.environ.get('KG', 99))
        if KG <= 1 and csz == 64:
            return
        pg = nextpp()
        if KG <= 2 and csz == 64:
            return
        for c in range(8):
            S.op("pe", lambda e: e.matmul(pg[0:16, 0:128], win[:, c, 1536:1552], hnT[:, c, :],
                                          start=(c == 0), stop=(c == 7)), reads=[hnT.b, win.b], writes=[pg.b])
        if KG <= 3 and csz == 64:
            return
        S.op("act", lambda e: e.copy(gaT[:], pg[0:16, 0:128]), reads=[pg.b], writes=[gaT.b])
        if KG <= 4 and csz == 64:
            return
        px = nextpp()
        if KG <= 5 and csz == 64:
            return
        S.op("pe", lambda e: e.matmul(px[:, 0:256], gaT[:], wa2[:], start=True, stop=False),
             reads=[gaT.b, wa2.b], writes=[px.b])
        if KG <= 6 and csz == 64:
            return
        S.op("pe", lambda e: e.matmul(px[:, 0:256], ones1[:], bab[:], start=False, stop=True),
             reads=[ones1.b, bab.b], writes=[px.b])
        if KG <= 7 and csz == 64:
            return
        S.op("act", lambda e: e.activation(gab[:], px[:, 0:256], AF.Abs), reads=[px.b], writes=[gab.b])
        if KG <= 8 and csz == 64:
            return
        S.op("act", lambda e: e.activation(gab[:], gab[:], AF.Exp, scale=-1.0), reads=[gab.b], writes=[gab.b])
        if KG <= 9 and csz == 64:
            return
        S.op("dve", lambda e: e.tensor_scalar(ecs[:], gab[:], 2.0, None, ALU.add), reads=[gab.b], writes=[ecs.b])
        S.op("dve", lambda e: e.reciprocal(ecs[:], ecs[:]), reads=[ecs.b], writes=[ecs.b])
        S.op("dve", lambda e: e.tensor_tensor(gab[:], gab[:], ecs[:], ALU.mult), reads=[gab.b, ecs.b], writes=[gab.b])
        S.op("dve", lambda e: e.tensor_tensor(ecs[:], gab[:], gab[:], ALU.mult), reads=[gab.b], writes=[ecs.b])
        S.op("dve", lambda e: e.tensor_scalar(gmn[:], ecs[:], 1.0 / 11.0, 1.0 / 9.0, ALU.mult, ALU.add), reads=[ecs.b], writes=[gmn.b])
        for cst in (1.0 / 7.0, 1.0 / 5.0, 1.0 / 3.0, 1.0):
            S.op("dve", lambda e: e.tensor_tensor(gmn[:], gmn[:], ecs[:], ALU.mult), reads=[gmn.b, ecs.b], writes=[gmn.b])
            S.op("dve", lambda e: e.tensor_scalar(gmn[:], gmn[:], cst, None, ALU.add), reads=[gmn.b], writes=[gmn.b])
        S.op("dve", lambda e: e.scalar_tensor_tensor(gab[:], gab[:], 2.0, gmn[:], ALU.mult, ALU.mult), reads=[gab.b, gmn.b], writes=[gab.b])
        if KG <= 10 and csz == 64:
            return
        S.op("dve", lambda e: e.tensor_single_scalar(gmn[:], px[:, 0:256], 0.0, ALU.min), reads=[px.b], writes=[gmn.b])
        if KG <= 11 and csz == 64:
            return
        S.op("dve", lambda e: e.tensor_tensor(lg[:], gmn[:], gab[:], ALU.subtract), reads=[gmn.b, gab.b], writes=[lg.b])
        if KG <= 12 and csz == 64:
            return
        S.op("dve", lambda e: e.tensor_scalar(lg[:], lg[:], 1.0 / 16.0, None, ALU.mult), reads=[lg.b], writes=[lg.b])
        if KG <= 13 and csz == 64:
            return
        pc = nextpp()
        if KG <= 14 and csz == 64:
            return
        S.op("pe", lambda e: e.matmul(pc[:, 0:256], Lm[csz][:], lg[:], start=True, stop=True),
             reads=[Lm[csz].b, lg.b], writes=[pc.b])
        if KG <= 15 and csz == 64:
            return
        S.op("act", lambda e: e.activation(ecs[:], pc[:, 0:256], AF.Exp), reads=[pc.b], writes=[ecs.b])
        if KG <= 16 and csz == 64:
            return
        S.op("dve", lambda e: e.tensor_tensor(khat[:], proj[:, 256:512], ecs[:], ALU.mult),
             reads=[proj.b, ecs.b], writes=[khat.b])
        if KG <= 17 and csz == 64:
            return
        S.op("act", lambda e: e.copy(vb[:], proj[:, 512:1024]), reads=[proj.b], writes=[vb.b])
        if KG <= 18 and csz == 64:
            return
        pd = nextpp()
        if KG <= 19 and csz == 64:
            return
        for h in range(4):
            S.op("pe", lambda e: e.matmul(pd[0:64, h * 16:h * 16 + 16], lg[:, 64 * h:64 * h + 64], Ci[csz][:, 0:16],
                                          start=True, stop=True), reads=[lg.b, Ci[csz].b], writes=[pd.b])
        if KG <= 20 and csz == 64:
            return
        for h in range(4):
            S.op("act", lambda e: e.activation(dec[:, h, 0:nch], pd[0:64, h * 16:h * 16 + nch], AF.Exp),
                 reads=[pd.b], writes=[dec.b])

    def gla_qk(csz):
        for hf in range(2):
            pq = nextpp()
            for c in range(8):
                S.op("pe", lambda e: e.matmul(pq[:, 0:128], win[:, c, 128 * hf:128 * hf + 128], hnT[:, c, :],
                                              start=(c == 0), stop=(c == 7)), reads=[hnT.b, win.b], writes=[pq.b])
            pk = nextpp()
            for c in range(8):
                S.op("pe", lambda e: e.matmul(pk[:, 0:128], win[:, c, 256 + 128 * hf:256 + 128 * hf + 128], hnT[:, c, :],
                                              start=(c == 0), stop=(c == 7)), reads=[hnT.b, win.b], writes=[pk.b])
            pb = nextpp()
            S.op("pe", lambda e: e.matmul(pb[:, 0:128], lg[:, 128 * hf:128 * hf + 128], Um[csz][:], start=True, stop=True),
                 reads=[lg.b, Um[csz].b], writes=[pb.b])
            S.op("act", lambda e: e.activation(ebt[:], pb[:, 0:128], AF.Exp), reads=[pb.b], writes=[ebt.b])
            S.op("dve", lambda e: e.scalar_tensor_tensor(qtT[hf][:], pq[:, 0:128], 0.125, ebt[:], ALU.mult, ALU.mult),
                 reads=[pq.b, ebt.b], writes=[qtT[hf].b])
            S.op("act", lambda e: e.activation(ebt[:], pb[:, 0:128], AF.Exp, scale=-1.0), reads=[pb.b], writes=[ebt.b])
            S.op("dve", lambda e: e.tensor_tensor(ktT[hf][:], pk[:, 0:128], ebt[:], ALU.mult),
                 reads=[pk.b, ebt.b], writes=[ktT[hf].b])
            for hh in range(2):
                S.op("act", lambda e: e.copy(qtTz[hf][hh][64 * hh:64 * hh + 64, :], qtT[hf][64 * hh:64 * hh + 64, :]),
                     reads=[qtT[hf].b], writes=[qtTz[hf][hh].b])

    def gla_intra(csz):
        pA = nextpp()
        for h in range(4):
            hf, hh = h // 2, h % 2
            S.op("pe", lambda e: e.matmul(pA[:, h * 128:(h + 1) * 128], ktT[hf][:, :],
                                          qtTz[hf][hh][:, :], start=True, stop=True),
                 reads=[ktT[hf].b, qtTz[hf][hh].b], writes=[pA.b])
        if int(os.environ.get('KI', 9)) <= 1:
            return
        S.op("dve", lambda e: e.tensor_tensor(ATb[:], pA[:, :].rearrange("p (h t) -> p h t", h=4),
                                              Um[csz][:].unsqueeze(1).to_broadcast([128, 4, 128]), ALU.mult),
             reads=[pA.b, Um[csz].b], writes=[ATb.b])

    def gla_post(dst_ap, wr):
        S.op("act", lambda e: e.copy(ogf[:], pO[:, :].rearrange("p (h v) -> p h v", h=4)), reads=[pO.b], writes=[ogf.b])
        S.op("dve", lambda e: e.tensor_tensor(ogs[:], ogf[:], ogf[:], ALU.mult), reads=[ogf.b], writes=[ogs.b])
        S.op("dve", lambda e: e.tensor_reduce(ss14[:, 0:4], ogs[:], AX.X, ALU.add), reads=[ogs.b], writes=[ss14.b])
        S.op("act", lambda e: e.activation(ss14[:, 0:4], ss14[:, 0:4], AF.Sqrt, bias=epsc[:, 0:1], scale=1.0 / 128.0),
             reads=[ss14.b, epsc.b], writes=[ss14.b])
        S.op("dve", lambda e: e.reciprocal(ss14[:, 0:4], ss14[:, 0:4]), reads=[ss14.b], writes=[ss14.b])
        S.op("dve", lambda e: e.tensor_tensor(ogf[:], ogf[:], ss14[:, 0:4].unsqueeze(2).to_broadcast([128, 4, 128]), ALU.mult),
             reads=[ogf.b, ss14.b], writes=[ogf.b])
        S.op("dve", lambda e: e.tensor_tensor(ogf[:], ogf[:], gno[:].unsqueeze(1).to_broadcast([128, 4, 128]), ALU.mult),
             reads=[ogf.b, gno.b], writes=[ogf.b])
        S.op("act", lambda e: e.activation(sgr[:], proj[:, 1024:1536], AF.Exp, scale=-1.0), reads=[proj.b], writes=[sgr.b])
        S.op("dve", lambda e: e.tensor_scalar(sgr[:], sgr[:], 1.0, None, ALU.add), reads=[sgr.b], writes=[sgr.b])
        S.op("dve", lambda e: e.reciprocal(sgr[:], sgr[:]), reads=[sgr.b], writes=[sgr.b])
        S.op("dve", lambda e: e.tensor_tensor(sgr[:], sgr[:], proj[:, 1024:1536], ALU.mult), reads=[sgr.b, proj.b], writes=[sgr.b])
        S.op("dve", lambda e: e.tensor_tensor(dst_ap, ogf[:].rearrange("p h v -> p (h v)"), sgr[:], ALU.mult),
             reads=[ogf.b, sgr.b], writes=wr)

    def nsa_q(ti):
        src = proj[:, 1552:2064].rearrange("p (h d) -> p h d", d=64)
        qknorm(src, 8, 0, qn[:], [proj.b], [qn.b])
        S.op("act", lambda e: e.copy(qcp[:].rearrange("p j (g d) -> p j g d", g=2),
                                     qn[:].rearrange("p (g j) d -> p j g d", g=2)), reads=[qn.b], writes=[qcp.b])
        rope(qn[:], 8, ti, [qn.b])
        S.op("act", lambda e: e.copy(qrp[:].rearrange("p j (g d) -> p j g d", g=2),
                                     qn[:].rearrange("p (g j) d -> p j g d", g=2)), reads=[qn.b], writes=[qrp.b])
        for j in range(4):
            S.op("pe", lambda e: e.transpose(pT[:, j, :], qcp[:, j, :], ident[:]), reads=[qcp.b, ident.b], writes=[pT.b])
            S.op("pe", lambda e: e.transpose(pT[:, 4 + j, :], qrp[:, j, :], ident[:]), reads=[qrp.b, ident.b], writes=[pT.b])
        for g in range(2):
            S.op("act", lambda e: e.copy(QcTz[g][64 * g:64 * g + 64, :, :], pT[64 * g:64 * g + 64, 0:4, :]),
                 reads=[pT.b], writes=[QcTz[g].b])
            S.op("act", lambda e: e.copy(QrTz[g][64 * g:64 * g + 64, :, :], pT[64 * g:64 * g + 64, 4:8, :]),
                 reads=[pT.b], writes=[QrTz[g].b])
        S.op("act", lambda e: e.activation(gts[:], proj[:, 2832:2856], AF.Exp, scale=-1.0), reads=[proj.b], writes=[gts.b])
        S.op("dve", lambda e: e.tensor_scalar(gts[:], gts[:], 1.0, None, ALU.add), reads=[gts.b], writes=[gts.b])
        S.op("dve", lambda e: e.reciprocal(gts[:], gts[:]), reads=[gts.b], writes=[gts.b])

    def branch_out(g, br, first):
        pov = pO[:, 0:260].rearrange("p (j c) -> p j c", j=4)
        S.op("dve", lambda e: e.tensor_scalar(rl4[:], pov[:, :, 64], 1e-20, None, ALU.max), reads=[pO.b], writes=[rl4.b])
        S.op("dve", lambda e: e.reciprocal(rl4[:], rl4[:]), reads=[rl4.b], writes=[rl4.b])
        gv = gts[:].rearrange("p (h b) -> p h b", b=3)[:, 4 * g:4 * g + 4, br]
        S.op("dve", lambda e: e.tensor_tensor(coef[:], rl4[:], gv, ALU.mult), reads=[rl4.b, gts.b], writes=[coef.b])
        dst = onsa[:, 4 * g:4 * g + 4, :]
        if first:
            S.op("dve", lambda e: e.tensor_tensor(dst, pov[:, :, 0:64], coef[:].unsqueeze(2).to_broadcast([128, 4, 64]), ALU.mult),
                 reads=[pO.b, coef.b], writes=[onsa.b])
        else:
            S.op("dve", lambda e: e.tensor_tensor(otmp[:], pov[:, :, 0:64], coef[:].unsqueeze(2).to_broadcast([128, 4, 64]), ALU.mult),
                 reads=[pO.b, coef.b], writes=[otmp.b])
            S.op("dve", lambda e: e.tensor_tensor(dst, dst, otmp[:], ALU.add), reads=[onsa.b, otmp.b], writes=[onsa.b])

    pti = 0

    def nextPt():
        nonlocal pti
        pti += 1
        return Pt[pti % 2]

    def kside(A, ti, r):
        rv = r[:].rearrange("p (a b c) -> p a b c", a=3, b=2)
        S.op("act", lambda e: e.copy(kb4[:, 0:3, :], rv[:, :, 0, :]), reads=[r.b], writes=[kb4.b])
        S.op("act", lambda e: e.copy(kb4[:, 3, :], r[:, 128:256]), reads=[r.b], writes=[kb4.b])
        for a in range(4):
            S.op("pe", lambda e: e.transpose(pT[:, a, :], kb4[:, a, :], ident[:]), reads=[kb4.b, ident.b], writes=[pT.b])
        S.op("act", lambda e: e.copy(kT4[:], pT[:, 0:4, :]), reads=[pT.b], writes=[kT4.b])
        S.op("dve", lambda e: e.tensor_copy(A.ksT[:, ti, :], kT4[:, 1, :]), reads=[kT4.b], writes=[A.ksT.b])
        S.op("dve", lambda e: e.tensor_copy(A.kwT[:, ti % 8, :], kT4[:, 2, :]), reads=[kT4.b], writes=[A.kwT.b])
        S.op("dve", lambda e: e.tensor_copy(A.vs1[:, ti, :, 0:64], r[:, 384:512].rearrange("p (g d) -> p g d", g=2)),
             reads=[r.b], writes=[A.vs1.b])
        S.op("dve", lambda e: e.tensor_copy(A.vw1[:, ti % 8, :, 0:64], r[:, 640:768].rearrange("p (g d) -> p g d", g=2)),
             reads=[r.b], writes=[A.vw1.b])
        for a, src_i in ((0, 0), (1, 3)):
            for half, arr in ((0, A.clo[a]), (1, A.chi[a])):
                pp = nextpp()
                for p in range(16):
                    S.op("pe", lambda e: e.matmul(pp[:, 0:8], Wbd[a][:, 16 * half + p, :],
                                                  kT4[:, src_i, :].rearrange("q (c p) -> q p c", p=16)[:, p, :],
                                                  start=(p == 0), stop=(p == 15)),
                         reads=[Wbd[a].b, kT4.b], writes=[pp.b])
                S.op("act", lambda e: e.copy(arr[:, 8 * ti + 1:8 * ti + 9], pp[:, 0:8]), reads=[pp.b], writes=[arr.b])
        n0 = 8 * ti - 1
        lo_n = max(n0, 0)
        cnt = 8 * ti + 7 - lo_n
        for a, dstT in ((0, A.kcmpT), (1, A.vcmpT)):
            S.op("dve", lambda e: e.tensor_tensor(sq14[:, 0, 0:cnt], A.clo[a][:, lo_n + 1:lo_n + 1 + cnt],
                                                  A.chi[a][:, lo_n + 2:lo_n + 2 + cnt], ALU.add),
                 reads=[A.clo[a].b, A.chi[a].b], writes=[sq14.b])
            S.op("dve", lambda e: e.tensor_scalar(dstT[:, lo_n:lo_n + cnt], sq14[:, 0, 0:cnt], cbias[:, a:a + 1], None, ALU.add),
                 reads=[sq14.b, cbias.b], writes=[dstT.b])
        for ct in sorted({lo_n // 128, (8 * ti + 6) // 128}):
            S.op("pe", lambda e: e.transpose(pT[:, 4, :], A.vcmpT[:, 128 * ct:128 * ct + 128], ident[:]),
                 reads=[A.vcmpT.b, ident.b], writes=[pT.b])
            S.op("act", lambda e: e.copy(A.vcmp1[:, ct, :, 0:64], pT[:, 4, :].rearrange("p (g d) -> p g d", g=2)),
                 reads=[pT.b], writes=[A.vcmp1.b])

    def nsa_attend(A, ti):
        nsa_q(ti)
        nct = (8 * ti + 6) // 128 + 1
        for g in range(2):
            gs = slice(64 * g, 64 * g + 64)
            for ct in range(nct):
                pp = nextpp()
                S.op("pe", lambda e: e.matmul(pp[:, :], A.kcmpT[:, 128 * ct:128 * ct + 128],
                                              QcTz[g][:, :, :].rearrange("p j t -> p (j t)"), start=True, stop=True),
                     reads=[A.kcmpT.b, QcTz[g].b], writes=[pp.b])
                P = nextPt()
                S.op("act", lambda e: e.activation(P[:].rearrange("p j t -> p (j t)"), pp[:, :], AF.Exp,
                                                   bias=A.padc(ct), scale=0.125),
                     reads=[pp.b, A.padb], writes=[P.b])
                if ct >= nct - 2:
                    off = 2048.0 * ct - 128.0 * ti + 31.0
                    S.op("dve", lambda e: e.tensor_scalar(mk[:], V16[:], off, None, ALU.is_ge), reads=[V16.b], writes=[mk.b])
                    S.op("dve", lambda e: e.tensor_tensor(P[:], P[:], mk[:].unsqueeze(1).to_broadcast([128, 4, 128]), ALU.mult),
                         reads=[P.b, mk.b], writes=[P.b])
                for j in range(4):
                    S.op("pe", lambda e: e.matmul(pO[:, 65 * j:65 * j + 65], P[:, j, :], A.vcmp1[:, ct, g, :],
                                                  start=(ct == 0 and j == 0), stop=(ct == nct - 1), skip_group_check=True),
                         reads=[P.b, A.vcmp1.b], writes=[pO.b])
                    S.op("pe", lambda e: e.matmul(pI[:, 128 * j:128 * j + 128], P[:, j, :], C2S[:, ct, :],
                                                  start=(ct == 0 and j == 0), stop=(ct == nct - 1), skip_group_check=True),
                         reads=[P.b, C2S.b], writes=[pI.b])
            branch_out(g, 0, True)
            S.op("dve", lambda e: e.tensor_scalar(imp[:], pI[:, 0:128], rl4[:, 0:1], None, ALU.mult),
                 reads=[pI.b, rl4.b], writes=[imp.b])
            for j in range(1, 4):
                S.op("dve", lambda e: e.scalar_tensor_tensor(imp[:], pI[:, 128 * j:128 * j + 128], rl4[:, j:j + 1], imp[:],
                                                             ALU.mult, ALU.add), reads=[pI.b, rl4.b, imp.b], writes=[imp.b])
            S.op("dve", lambda e: e.tensor_scalar(curc[:], hi64[:], float(2 * ti), None, ALU.add), reads=[hi64.b], writes=[curc.b])
            S.op("dve", lambda e: e.tensor_scalar(vm[:], jidx[:], curc[:, 0:1], None, ALU.is_le), reads=[jidx.b, curc.b], writes=[vm.b])
            S.op("dve", lambda e: e.tensor_scalar(fm[:], jidx[:], A.off64, None, ALU.is_ge), reads=[jidx.b, metat.b], writes=[fm.b])
            S.op("dve", lambda e: e.tensor_tensor(vm[:], vm[:], fm[:], ALU.mult), reads=[vm.b, fm.b], writes=[vm.b])
            S.op("dve", lambda e: e.tensor_scalar(fm[:], jidx[:], curc[:, 0:1], None, ALU.is_equal), reads=[jidx.b, curc.b], writes=[fm.b])
            S.op("dve", lambda e: e.tensor_scalar(imw[:], jidx[:], A.off64, None, ALU.is_equal), reads=[jidx.b, metat.b], writes=[imw.b])
            S.op("dve", lambda e: e.tensor_tensor(fm[:], fm[:], imw[:], ALU.max), reads=[fm.b, imw.b], writes=[fm.b])
            S.op("dve", lambda e: e.tensor_tensor(imp[:], imp[:], vm[:], ALU.mult), reads=[imp.b, vm.b], writes=[imp.b])
            S.op("dve", lambda e: e.tensor_scalar(vm[:], vm[:], 1.0e9, -1.0e9, ALU.mult, ALU.add), reads=[vm.b], writes=[vm.b])
            S.op("dve", lambda e: e.tensor_tensor(imp[:], imp[:], vm[:], ALU.add), reads=[imp.b, vm.b], writes=[imp.b])
            S.op("dve", lambda e: e.scalar_tensor_tensor(imp[:], fm[:], 1.0e30, imp[:], ALU.mult, ALU.max), reads=[imp.b, fm.b], writes=[imp.b])
            S.op("dve", lambda e: e.max(mx8[:], imp[:]), reads=[imp.b], writes=[mx8.b])
            S.op("dve", lambda e: e.match_replace(imw[:], mx8[:], imp[:], -3.0e38), reads=[imp.b, mx8.b], writes=[imw.b])
            S.op("dve", lambda e: e.max(mx8[:], imw[:]), reads=[imw.b], writes=[mx8.b])
            S.op("dve", lambda e: e.tensor_scalar(seln[:], imp[:], mx8[:, 7:8], NEGB, ALU.is_lt, ALU.mult),
                 reads=[imp.b, mx8.b], writes=[seln.b])
            S.op("pe", lambda e: e.transpose(pT[:, 5, :], seln[:], ident[:]), reads=[seln.b, ident.b], writes=[pT.b])
            S.op("dve", lambda e: e.tensor_copy(selnT[g][:], pT[:, 5, :].unsqueeze(1).to_broadcast([128, 4, 128])),
                 reads=[pT.b], writes=[selnT[g].b])
        for g in range(2):
            def slc_qk(kt):
                ek = Ekt2[kt % 2]
                S.op("dve", lambda e: e.tensor_scalar(ek[:], BLK[:], float(2 * kt), None, ALU.is_equal),
                     reads=[BLK.b], writes=[ek.b])
                pp = nextpp()
                S.op("pe", lambda e: e.matmul(pp[:, :], A.ksT[:, kt, :], QrTz[g][:, :, :].rearrange("p j t -> p (j t)"),
                                              start=True, stop=False), reads=[A.ksT.b, QrTz[g].b], writes=[pp.b])
                S.op("pe", lambda e: e.matmul(pp[:, :], ek[:], selnT[g][:].rearrange("p j t -> p (j t)"),
                                              start=False, stop=(kt != ti)), reads=[ek.b, selnT[g].b], writes=[pp.b])
                if kt == ti:
                    S.op("pe", lambda e: e.matmul(pp[:, :], ident[:], causneg[:].rearrange("p j t -> p (j t)"),
                                                  start=False, stop=True), reads=[ident.b, causneg.b], writes=[pp.b])
                return pp

            def slc_act(kt, pp):
                P = nextPt()
                S.op("act", lambda e: e.activation(P[:].rearrange("p j t -> p (j t)"), pp[:, :], AF.Exp,
                                                   bias=A.padk(kt), scale=0.125),
                     reads=[pp.b, A.padb], writes=[P.b])
                return P

            def slc_pv(kt, P):
                for j in range(4):
                    S.op("pe", lambda e: e.matmul(pO[:, 65 * j:65 * j + 65], P[:, j, :], A.vs1[:, kt, g, :],
                                                  start=(kt == 0 and j == 0), stop=(kt == ti), skip_group_check=True),
                         reads=[P.b, A.vs1.b], writes=[pO.b])

            pp_cur = slc_qk(0)
            P_prev = None
            for kt in range(ti + 1):
                P_cur = slc_act(kt, pp_cur)
                if kt + 1 <= ti:
                    pp_cur = slc_qk(kt + 1)
                slc_pv(kt, P_cur)
            branch_out(g, 1, False)
        for g in range(2):
            kts = [kt for kt in range(ti - 4, ti + 1) if kt >= 0]

            def win_qk(kt):
                pp = nextpp()
                extra = (kt == ti) or (kt == ti - 4)
                S.op("pe", lambda e: e.matmul(pp[:, :], A.kwT[:, kt % 8, :], QrTz[g][:, :, :].rearrange("p j t -> p (j t)"),
                                              start=True, stop=not extra), reads=[A.kwT.b, QrTz[g].b], writes=[pp.b])
                if extra:
                    mneg_ = causneg if kt == ti else winneg
                    S.op("pe", lambda e: e.matmul(pp[:, :], ident[:], mneg_[:].rearrange("p j t -> p (j t)"),
                                                  start=False, stop=True), reads=[ident.b, mneg_.b], writes=[pp.b])
                return pp

            def win_act(kt, pp):
                P = nextPt()
                S.op("act", lambda e: e.activation(P[:].rearrange("p j t -> p (j t)"), pp[:, :], AF.Exp,
                                                   bias=A.padk(kt), scale=0.125),
                     reads=[pp.b, A.padb], writes=[P.b])
                return P

            def win_pv(kt, P):
                for j in range(4):
                    S.op("pe", lambda e: e.matmul(pO[:, 65 * j:65 * j + 65], P[:, j, :], A.vw1[:, kt % 8, g, :],
                                                  start=(kt == kts[0] and j == 0), stop=(kt == ti), skip_group_check=True),
                         reads=[P.b, A.vw1.b], writes=[pO.b])

            pp_cur = win_qk(kts[0])
            for i_, kt in enumerate(kts):
                P_cur = win_act(kt, pp_cur)
                if i_ + 1 < len(kts):
                    pp_cur = win_qk(kts[i_ + 1])
                win_pv(kt, P_cur)
            branch_out(g, 2, False)
        S.op("act", lambda e: e.copy(attnb[:, 512:1024], onsa[:].rearrange("p h d -> p (h d)")), reads=[onsa.b], writes=[attnb.b])

    with ExitStack() as es2:
        def sb2(name, shape, dt=F32):
            return T(es2.enter_context(nc.sbuf_tensor(name, list(shape), dt)))
        ksT = sb2("ksT", [128, 64, 128], BF16)
        vs1 = sb2("vs1", [128, 64, 2, 65], BF16)
        kwT = sb2("kwT", [128, 8, 128], BF16)
        vw1 = sb2("vw1", [128, 8, 2, 65], BF16)
        clo = [sb2(f"clo{a}", [128, 520]) for a in range(2)]
        chi = [sb2(f"chi{a}", [128, 520]) for a in range(2)]
        kcmpT = sb2("kcmpT", [128, 512], BF16)
        vcmpT = sb2("vcmpT", [128, 512], BF16)
        vcmp1 = sb2("vcmp1", [128, 4, 2, 65], BF16)
        S.op("pool", lambda e: e.memset(vs1[:], 1.0), writes=[vs1.b])
        S.op("pool", lambda e: e.memset(vw1[:], 1.0), writes=[vw1.b])
        S.op("pool", lambda e: e.memset(vcmp1[:], 1.0), writes=[vcmp1.b])
        S.op("pool", lambda e: e.memset(kcmpT[:], 0.0), writes=[kcmpT.b])
        S.op("pool", lambda e: e.memset(vcmpT[:], 0.0), writes=[vcmpT.b])
        for a in range(2):
            S.op("pool", lambda e: e.memset(clo[a][:], 0.0), writes=[clo[a].b])
            S.op("pool", lambda e: e.memset(chi[a][:], 0.0), writes=[chi[a].b])
        S.op("dve", lambda e: e.memset(Sst[:], 0.0), writes=[Sst.b])

        class NS:
            pass
        A = NS()
        A.ksT, A.vs1, A.kwT, A.vw1, A.clo, A.chi, A.kcmpT, A.vcmpT, A.vcmp1 = ksT, vs1, kwT, vw1, clo, chi, kcmpT, vcmpT, vcmp1
        A.padk = lambda kt: padk[:, kt:kt + 1]
        A.padc = lambda ct: padc[:, ct:ct + 1]
        A.padb = padk.b
        A.off64 = metat[:, 2:3]

        for ti in range(int(os.environ.get('KT_END', NT))):
            full = ti >= int(os.environ.get('KFULL', 32))
            xbuf = xt[ti % 2]
            project_tile(ti, xe_v[ti], xbuf)
            r = rows
            if ti >= 48:
                t0 = (ti - 48) * 128
                S.dma("sp", o_cmp_p[t0:t0 + 128, :], r[:, 0:256], reads=[r.b])
                S.dma("sp", o_slc_p[t0:t0 + 128, :], r[:, 256:512], reads=[r.b])
            if ti >= 60:
                t0 = (ti - 60) * 128
                S.dma("sp", o_win_p[t0:t0 + 128, :], r[:, 512:768], reads=[r.b])
            KSTOP = int(os.environ.get('KSTOP', 99))
            if KSTOP <= 1:
                continue
            kside(A, ti, r)
            if KSTOP <= 4:
                continue
            gla_gates(64, 2)
            if KSTOP <= 5:
                continue
            if full:
                gla_qk(64)
                if KSTOP <= 6:
                    continue
                gla_intra(64)
                if KSTOP <= 7:
                    continue
                for hf in range(2):
                    S.op("act", lambda e: e.copy(qz[0][hf][:, 0:64], qtT[hf][:, 0:64]), reads=[qtT[hf].b], writes=[qz[0][hf].b])
                    S.op("act", lambda e: e.copy(qz[1][hf][:, 64:128], qtT[hf][:, 64:128]), reads=[qtT[hf].b], writes=[qz[1][hf].b])
            for j in range(2):
                if full:
                    for h in range(4):
                        hf, hh = h // 2, h % 2
                        S.op("dve", lambda e: e.tensor_copy(Sbz[j][h][64 * hh:64 * hh + 64, :], Sst[:, h, :]),
                             reads=[Sst.b], writes=[Sbz[j][h].b])
                cm = lo64 if j == 0 else hi64
                S.op("dve", lambda e: e.tensor_scalar(khz[j][:], khat[:], cm[:, 0:1], None, ALU.mult),
                     reads=[khat.b, cm.b], writes=[khz[j].b])
                for h in range(4):
                    S.op("pe", lambda e: e.matmul(pS[0:64, h * 128:(h + 1) * 128],
                                                  khz[j][:, 64 * h:64 * h + 64],
                                                  vb[:, 128 * h:128 * h + 128], start=True, stop=True),
                         reads=[khz[j].b, vb.b], writes=[pS.b])
                S.op("dve", lambda e: e.tensor_tensor(Sst[:], Sst[:], dec[:, :, j:j + 1].to_broadcast([64, 4, 128]), ALU.mult),
                     reads=[Sst.b, dec.b], writes=[Sst.b])
                S.op("dve", lambda e: e.tensor_tensor(Sst[:], Sst[:], pS[0:64, :].rearrange("p (h v) -> p h v", h=4), ALU.add),
                     reads=[Sst.b, pS.b], writes=[Sst.b])
            if not full:
                continue
            if KSTOP <= 9:
                continue
            for h in range(4):
                hf, hh = h // 2, h % 2
                S.op("pe", lambda e: e.matmul(pO[:, h * 128:(h + 1) * 128], ATb[:, h, :], vb[:, 128 * h:128 * h + 128],
                                              start=True, stop=False), reads=[ATb.b, vb.b], writes=[pO.b])
                for j in range(2):
                    S.op("pe", lambda e: e.matmul(pO[:, h * 128:(h + 1) * 128], qz[j][hf][:, :],
                                                  Sbz[j][h][:, :], start=False, stop=(j == 1)),
                         reads=[qz[j][hf].b, Sbz[j][h].b], writes=[pO.b])
            if KSTOP <= 10:
                continue
            gla_post(attnb[:, 0:512], [attnb.b])

            if KSTOP <= 11:
                continue
            nsa_attend(A, ti)
            S.dma("sp", attn0[ti - 32], attnb[:], reads=[attnb.b])
            if ti >= 48:
                S.op("dve", lambda e: e.tensor_copy(dbgt[:], attnb[:]), reads=[attnb.b], writes=[dbgt.b])
                S.dma("sp", dbg_attn[(ti - 48) * 128:(ti - 47) * 128, :], dbgt[:], reads=[dbgt.b])
        S.dma("sp", o_gla_p.rearrange("h k v -> k h v"), Sst[:], reads=[Sst.b])
        S.finish()

    def make_rows(ti):
        r = rows
        S.op("act", lambda e: e.copy(r[:], proj[:, 2064:2832]), reads=[proj.b], writes=[r.b])
        for (kidx, gidx) in ((0, 1), (2, 2), (4, 3)):
            src = proj[:, 2064 + kidx * 128: 2064 + (kidx + 1) * 128].rearrange("p (h d) -> p h d", d=64)
            dst = r[:, kidx * 128:(kidx + 1) * 128].rearrange("p (h d) -> p h d", d=64)
            qknorm(src, 2, gidx, dst, [proj.b], [r.b])
            if kidx > 0:
                rope(dst, 2, ti, [r.b])

    with ExitStack() as es3:
        stk.append(es3)
        Ssm = sb("Ssm", [64, 64, 128])
        S.dma("sp", Ssm[:], state_gla.rearrange("s h k v -> k (s h) v"), writes=[Ssm.b])
        project_tile(NT, xs[:, :], xt[0])
        r = rows
        S.dma("sp", sproj_d[:, :], proj[:], reads=[proj.b])
        S.dma("sp", o_cmp_s[:, :], r[0:64, 0:256], reads=[r.b])
        S.dma("sp", o_slc_s[:, :], r[0:64, 256:512], reads=[r.b])
        for sq in range(16):
            S.dma("sp", o_win_s[sq, 508:512, :], r[4 * sq:4 * sq + 4, 512:768], reads=[r.b])
        gla_gates(4, 16)
        gla_qk(4)
        gla_intra(4)
        mcol = sb("mcol", [128, 128], BF16)
        qms = [sb(f"qms{h}", [128, 128], BF16) for h in range(2)]
        for h in range(4):
            S.op("pe", lambda e: e.matmul(pO[:, h * 128:(h + 1) * 128], ATb[:, h, :], vb[:, 128 * h:128 * h + 128],
                                          start=(h == 0), stop=False, skip_group_check=True), reads=[ATb.b, vb.b], writes=[pO.b])
        for sq in range(16):
            S.op("dve", lambda e: e.tensor_scalar(mcol[:], Cj4[:], float(sq), None, ALU.is_equal), reads=[Cj4.b], writes=[mcol.b])
            for hf in range(2):
                S.op("dve", lambda e: e.tensor_tensor(qms[hf][:], qtT[hf][:], mcol[:], ALU.mult), reads=[qtT[hf].b, mcol.b], writes=[qms[hf].b])
            for h in range(4):
                hf, hh = h // 2, h % 2
                S.op("dve", lambda e: e.tensor_copy(Sbz[0][h][64 * hh:64 * hh + 64, :], Ssm[:, 4 * sq + h, :]),
                     reads=[Ssm.b], writes=[Sbz[0][h].b])
                S.op("pe", lambda e: e.matmul(pO[:, h * 128:(h + 1) * 128], qms[hf][:, :], Sbz[0][h][:, :],
                                              start=False, stop=(sq == 15), skip_group_check=True),
                     reads=[qms[hf].b, Sbz[0][h].b], writes=[pO.b])
        gla_post(attnb[:, 0:512], [attnb.b])
        S.dma("sp", attn0[32][0:64, 0:512], attnb[0:64, 0:512], reads=[attnb.b])
        S.op("dve", lambda e: e.tensor_copy(dbgt[:, 0:512], attnb[:, 0:512]), reads=[attnb.b], writes=[dbgt.b])
        S.dma("sp", dbg_as[:, 0:512], dbgt[0:64, 0:512], reads=[dbgt.b])
        for sq in range(16):
            S.op("dve", lambda e: e.tensor_scalar(khm[:], khat[0:64, :], Ci[4][0:64, sq:sq + 1], None, ALU.mult),
                 reads=[khat.b, Ci[4].b], writes=[khm.b])
            for h in range(4):
                S.op("pe", lambda e: e.matmul(pS[0:64, h * 128:(h + 1) * 128], khm[:, 64 * h:64 * h + 64],
                                              vb[0:64, 128 * h:128 * h + 128], start=True, stop=True),
                     reads=[khm.b, vb.b], writes=[pS.b])
            sv = Ssm[:, 4 * sq:4 * sq + 4, :]
            S.op("dve", lambda e: e.tensor_tensor(sv, sv, dec[:, :, sq:sq + 1].to_broadcast([64, 4, 128]), ALU.mult),
                 reads=[Ssm.b, dec.b], writes=[Ssm.b])
            S.op("dve", lambda e: e.tensor_tensor(sv, sv, pS[0:64, :].rearrange("p (h v) -> p h v", h=4), ALU.add),
                 reads=[Ssm.b, pS.b], writes=[Ssm.b])
        S.dma("sp", o_gla_s.rearrange("s h k v -> k (s h) v"), Ssm[:], reads=[Ssm.b])
        S.dma("sp", o_win_s[:, 0:508, :], cache_win[:, 4:512, :])
        S.finish()
        stk.pop()

    with ExitStack() as es4:
        stk.append(es4)
        As = type("NS", (), {})()
        As.ksT = sb("ksTs", [128, 17, 128], BF16)
        As.vs1 = sb("vs1s", [128, 17, 2, 65], BF16)
        As.kwT = sb("kwTs", [128, 8, 128], BF16)
        As.vw1 = sb("vw1s", [128, 8, 2, 65], BF16)
        As.clo = [sb(f"clos{a}", [128, 520]) for a in range(2)]
        As.chi = [sb(f"chis{a}", [128, 520]) for a in range(2)]
        As.kcmpT = sb("kcmpTs", [128, 512], BF16)
        As.vcmpT = sb("vcmpTs", [128, 512], BF16)
        As.vcmp1 = sb("vcmp1s", [128, 4, 2, 65], BF16)
        zc = sb("zc", [128, 1])
        S.op("dve", lambda e: e.memset(zc[:], 0.0), writes=[zc.b])
        As.padk = lambda kt: zc[:, 0:1]
        As.padc = lambda ct: zc[:, 0:1]
        As.padb = zc.b
        As.off64 = zc[:, 0:1]
        S.op("pool", lambda e: e.memset(As.vs1[:], 1.0), writes=[As.vs1.b])
        S.op("pool", lambda e: e.memset(As.vw1[:], 1.0), writes=[As.vw1.b])
        S.op("pool", lambda e: e.memset(As.vcmp1[:], 1.0), writes=[As.vcmp1.b])
        S.op("pool", lambda e: e.memset(As.kcmpT[:], 0.0), writes=[As.kcmpT.b])
        S.op("pool", lambda e: e.memset(As.vcmpT[:], 0.0), writes=[As.vcmpT.b])
        S.op("pool", lambda e: e.memset(As.kwT[:], 0.0), writes=[As.kwT.b])
        for a in range(2):
            S.op("pool", lambda e: e.memset(As.clo[a][:], 0.0), writes=[As.clo[a].b])
            S.op("pool", lambda e: e.memset(As.chi[a][:], 0.0), writes=[As.chi[a].b])
        ptab_i = sb("ptab_i", [128, 256], I32)
        ptf = sb("ptf", [128, 256])
        idxu = sb("idxu", [128, 256], mybir.dt.uint32)
        S.dma("sp", ptab_i[:], page_tab.rearrange("s i -> (s i)").partition_broadcast(128), writes=[ptab_i.b])
        S.op("dve", lambda e: e.tensor_copy(ptf[:], ptab_i[:]), reads=[ptab_i.b], writes=[ptf.b])
        S.op("dve", lambda e: e.tensor_scalar(ptf[:], ptf[:], 128.0, pidx[:, 0:1], ALU.mult, ALU.add), reads=[ptf.b, pidx.b], writes=[ptf.b])
        S.op("dve", lambda e: e.tensor_copy(idxu[:], ptf[:]), reads=[ptf.b], writes=[idxu.b])
        poolc = pool_c.rearrange("n p c -> (n p) c")
        pools = pool_s.rearrange("n p c -> (n p) c")
        rows2 = [rows, sb("rowsB", [128, 768])]
        S.op("pool", lambda e: e.memset(rows2[1][:], 0.0), writes=[rows2[1].b])
        for sq in range(int(os.environ.get('KSEQ', 16))):
            for lt in range(16):
                j = sq * 16 + lt
                rr = rows2[lt % 2]
                S.dma("pool", None, None, reads=[idxu.b], writes=[rr.b],
                      fn=lambda e: e.indirect_dma_start(rr[:, 0:256], None, poolc,
                                                        bass.IndirectOffsetOnAxis(ap=idxu[:, j:j + 1], axis=0)))
                S.dma("pool", None, None, reads=[idxu.b], writes=[rr.b],
                      fn=lambda e: e.indirect_dma_start(rr[:, 256:512], None, pools,
                                                        bass.IndirectOffsetOnAxis(ap=idxu[:, j:j + 1], axis=0)))
                if lt >= 12:
                    S.dma("sp", rr[:, 512:768], cache_win[sq, (lt - 12) * 128:(lt - 11) * 128, :], writes=[rr.b])
                kside(As, lt, rr)
            S.op("pool", lambda e: e.memset(proj[:], 0.0), writes=[proj.b])
            S.dma("sp", proj[0:4, :], sproj_d[4 * sq:4 * sq + 4, :], reads=[proj.b], writes=[proj.b])
            make_rows(NT)
            kside(As, 16, rows)
            nsa_attend(As, 16)
            S.dma("sp", attn0[32][4 * sq:4 * sq + 4, 512:1024], attnb[0:4, 512:1024], reads=[attnb.b])
            S.op("dve", lambda e: e.tensor_copy(dbgt[0:4, 512:1024], attnb[0:4, 512:1024]), reads=[attnb.b], writes=[dbgt.b])
            S.dma("sp", dbg_as[4 * sq:4 * sq + 4, 512:1024], dbgt[0:4, 512:1024], reads=[dbgt.b])
        S.finish()
        stk.pop()

    S.finish()
    stk.pop()
    esA.close()

    def channel_phase(layer, KA, w_out_d, tiles):
        with ExitStack() as esB:
            stk.append(esB)
            KC = KA // 128
            wout = sb("wout", [128, KC, D], BF16)
            wgate = sb("wgate", [128, 8, D], BF16)
            wproj = sb("wproj", [128, 2, D], BF16)
            for c in range(KC):
                S.dma("pool", wout[:, c, :], w_out_d[c * 128:(c + 1) * 128, :], writes=[wout.b])
            for c in range(8):
                S.dma("pool", wgate[:, c, :], ple_w_gate[layer, c * 128:(c + 1) * 128, :], writes=[wgate.b])
            for c in range(2):
                S.dma("pool", wproj[:, c, :], ple_w_proj[layer, c * 128:(c + 1) * 128, :], writes=[wproj.b])
            gml = sb("gml", [128, 8])
            gpl = sb("gpl", [128, 8])
            S.dma("sp", gml[:], norm_mlp[layer].rearrange("(c p) -> p c", p=128), writes=[gml.b], allow_slow_non_contiguous=True)
            S.dma("sp", gpl[:], norm_ple[layer].rearrange("(c p) -> p c", p=128), writes=[gpl.b], allow_slow_non_contiguous=True)
            w1s = [sb(f"w1s{i}", [128, 8, 512], BF16) for i in range(2)]
            w2s = [sb(f"w2s{i}", [128, 4, D], BF16) for i in range(2)]
            hmid = sb("hmid", [128, 4, D])
            hng = sb("hng", [128, 8, 512], BF16)
            aTs = [sb(f"aTs{i}", [128, 4, 512], BF16) for i in range(2)]
            xin = [sb(f"xin{i}", [128, D]) for i in range(2)]
            ain = [sb(f"ain{i}", [128, KA], BF16) for i in range(2)]
            aT = sb("aT", [128, KC, 128], BF16)
            junk = sb("junk", [128, D])
            ssb = sb("ssb", [128, 1])
            xnb = sb("xnb", [128, D], BF16)
            hnl = sb("hnl", [128, 8, 128], BF16)
            rlu = sb("rlu", [128, 512])
            gate = sb("gate", [128, D])
            pin = sb("pin", [128, 256])
            pinb = sb("pinb", [128, 256], BF16)
            pTl = sb("pTl", [128, 2, 128], BF16)
            hout = sb("hout", [128, D])
            w1v = mlp_w1[layer].rearrange("(c p) f -> p c f", p=128)
            w2v = mlp_w2[layer].rearrange("(c p) n -> p c n", p=128)

            def norm_T(src_ap, src_b, gvec, dstT_ap, dst_b):
                S.op("act", lambda e: e.activation(junk[:], src_ap, AF.Square, accum_out=ssb[:]),
                     reads=[src_b], writes=[junk.b, ssb.b])
                S.op("act", lambda e: e.activation(ssb[:], ssb[:], AF.Sqrt, bias=epsc[:, 0:1], scale=1.0 / D),
                     reads=[ssb.b, epsc.b], writes=[ssb.b])
                S.op("dve", lambda e: e.reciprocal(ssb[:], ssb[:]), reads=[ssb.b], writes=[ssb.b])
                S.op("dve", lambda e: e.tensor_scalar(xnb[:], src_ap, ssb[:, 0:1], None, ALU.mult),
                     reads=[src_b, ssb.b], writes=[xnb.b])
                for c in range(8):
                    S.op("pe", lambda e: e.transpose(pT[:, c, :], xnb[:, c * 128:(c + 1) * 128], ident[:]),
                         reads=[xnb.b, ident.b], writes=[pT.b])
                S.op("dve", lambda e: e.tensor_tensor(dstT_ap, pT[:], gvec[:].unsqueeze(2).to_broadcast([128, 8, 128]), ALU.mult),
                     reads=[pT.b, gvec.b], writes=[dst_b])

            wi = 0
            for g0 in range(0, len(tiles), 4):
                grp = tiles[g0:g0 + 4]
                ng = len(grp)
                NTK = ng * 128
                for tt, (x_src, a_src, p_src, dsts) in enumerate(grp):
                    xb = xin[tt % 2]
                    ab = ain[tt % 2]
                    S.dma("sp", xb[:], x_src, writes=[xb.b])
                    S.dma("sp", ab[:], a_src, writes=[ab.b])
                    for c in range(KC):
                        S.op("pe", lambda e: e.transpose(pT[:, c, :], ab[:, c * 128:(c + 1) * 128], ident[:]),
                             reads=[ab.b, ident.b], writes=[pT.b])
                    S.op("act", lambda e: e.copy(aT[:], pT[:, 0:KC, :]), reads=[pT.b], writes=[aT.b])
                    for half in range(2):
                        pp = nextpp()
                        for c in range(KC):
                            S.op("pe", lambda e: e.matmul(pp[:, :], aT[:, c, :], wout[:, c, half * 512:(half + 1) * 512],
                                                          start=(c == 0), stop=(c == KC - 1)),
                                 reads=[aT.b, wout.b], writes=[pp.b])
                        S.op("dve", lambda e: e.tensor_tensor(hmid[:, tt, half * 512:(half + 1) * 512], pp[:, :],
                                                              xb[:, half * 512:(half + 1) * 512], ALU.add),
                             reads=[pp.b, xb.b], writes=[hmid.b])
                    norm_T(hmid[:, tt, :], hmid.b, gml, hng[:, :, tt * 128:(tt + 1) * 128], hng.b)
                for f in range(8):
                    wa, wb = w1s[wi % 2], w2s[wi % 2]
                    at = aTs[wi % 2]
                    wi += 1
                    for c in range(8):
                        S.dma("pool", wa[:, c, :], w1v[:, c, f * 512:(f + 1) * 512], writes=[wa.b])
                    for c in range(4):
                        S.dma("pool", wb[:, c, :], w2v[:, 4 * f + c, :], writes=[wb.b])
                    for fc in range(4):
                        pp = nextpp()
                        for c in range(8):
                            S.op("pe", lambda e: e.matmul(pp[:, 0:NTK], wa[:, c, fc * 128:(fc + 1) * 128], hng[:, c, 0:NTK],
                                                          start=(c == 0), stop=(c == 7)),
                                 reads=[wa.b, hng.b], writes=[pp.b])
                        S.op("act", lambda e: e.activation(rlu[:, 0:NTK], pp[:, 0:NTK], AF.Relu), reads=[pp.b], writes=[rlu.b])
                        S.op("dve", lambda e: e.tensor_tensor(at[:, fc, 0:NTK], rlu[:, 0:NTK], rlu[:, 0:NTK], ALU.mult),
                             reads=[rlu.b], writes=[at.b])
                    for tt in range(ng):
                        for half in range(2):
                            pp = nextpp()
                            for fc in range(4):
                                S.op("pe", lambda e: e.matmul(pp[:, :], at[:, fc, tt * 128:(tt + 1) * 128],
                                                              wb[:, fc, half * 512:(half + 1) * 512],
                                                              start=(fc == 0), stop=(fc == 3)),
                                     reads=[at.b, wb.b], writes=[pp.b])
                            hv = hmid[:, tt, half * 512:(half + 1) * 512]
                            S.op("dve", lambda e: e.tensor_tensor(hv, hv, pp[:, :], ALU.add), reads=[hmid.b, pp.b], writes=[hmid.b])
                for tt, (x_src, a_src, p_src, dsts) in enumerate(grp):
                    norm_T(hmid[:, tt, :], hmid.b, gpl, hnl[:], hnl.b)
                    S.dma("sp", pin[:], p_src, writes=[pin.b])
                    S.op("act", lambda e: e.copy(pinb[:], pin[:]), reads=[pin.b], writes=[pinb.b])
                    for c in range(2):
                        S.op("pe", lambda e: e.transpose(pT[:, c, :], pinb[:, c * 128:(c + 1) * 128], ident[:]),
                             reads=[pinb.b, ident.b], writes=[pT.b])
                    S.op("act", lambda e: e.copy(pTl[:], pT[:, 0:2, :]), reads=[pT.b], writes=[pTl.b])
                    for half in range(2):
                        hs = slice(half * 512, (half + 1) * 512)
                        pg = nextpp()
                        for c in range(8):
                            S.op("pe", lambda e: e.matmul(pg[:, :], hnl[:, c, :], wgate[:, c, hs], start=(c == 0), stop=(c == 7)),
                                 reads=[hnl.b, wgate.b], writes=[pg.b])
                        S.op("act", lambda e: e.activation(gate[:, hs], pg[:, :], AF.Exp, scale=-1.0), reads=[pg.b], writes=[gate.b])
                        S.op("dve", lambda e: e.tensor_scalar(gate[:, hs], gate[:, hs], 1.0, None, ALU.add), reads=[gate.b], writes=[gate.b])
                        S.op("dve", lambda e: e.reciprocal(gate[:, hs], gate[:, hs]), reads=[gate.b], writes=[gate.b])
                        pq = nextpp()
                        for c in range(2):
                            S.op("pe", lambda e: e.matmul(pq[:, :], pTl[:, c, :], wproj[:, c, hs], start=(c == 0), stop=(c == 1)),
                                 reads=[pTl.b, wproj.b], writes=[pq.b])
                        S.op("dve", lambda e: e.tensor_tensor(gate[:, hs], gate[:, hs], pq[:, :], ALU.mult), reads=[gate.b, pq.b], writes=[gate.b])
                        S.op("dve", lambda e: e.tensor_tensor(hout[:, hs], hmid[:, tt, hs], gate[:, hs], ALU.add),
                             reads=[hmid.b, gate.b], writes=[hout.b])
                    for (dst, r0, r1) in dsts:
                        S.dma("sp", dst, hout[r0:r1, :], reads=[hout.b])
            S.finish()
            stk.pop()

    tiles0 = []
    for ti in range(32, 64):
        dsts = [(h1s[ti - 32], 0, 128)]
        if ti >= 48:
            dsts.append((dbg_h1[(ti - 48) * 128:(ti - 47) * 128, :], 0, 128))
        tiles0.append((xe_v[ti], attn0[ti - 32], pe0[ti - 32], dsts))
    tiles0.append((xs[:, :], attn0[32], pe0[32], [(h1s[32], 0, 128), (dbg_h1s[:, :], 0, 128)]))
    channel_phase(0, 1024, a_w_out, tiles0)

    WG = (128, 512, 2048)
    DG = (1, 4, 16)
    with ExitStack() as esC:
        stk.append(esC)
        cwin = sb("cwin", [128, 8, 4608], BF16)
        cwv = c_w_in.rearrange("(c p) n -> p c n", p=128)
        for c in range(8):
            for hh_ in range(2):
                S.dma("pool", cwin[:, c, hh_ * 2304:(hh_ + 1) * 2304], cwv[:, c, hh_ * 2304:(hh_ + 1) * 2304], writes=[cwin.b])
        cgq = sb("cgq", [128, 6, 64])
        S.dma("sp", cgq[:], c_g_qk.rearrange("a d -> (a d)").partition_broadcast(128), writes=[cgq.b])
        hin = [sb(f"hin{i}", [128, D]) for i in range(2)]
        junkc = sb("junkc", [128, D])
        ssc = sb("ssc", [128, 1])
        xnc = sb("xnc", [128, D], BF16)
        hnc = sb("hnc", [128, 8, 128], BF16)
        zt = sb("zt", [128, 8, 64])
        zb = sb("zb", [128, 512], BF16)
        kvrow = [sb(f"kvrow{i}", [128, 1024]) for i in range(2)]
        sqc = sb("sqc", [128, 8, 64])
        s8 = sb("s8", [128, 8])
        ta8 = sb("ta8", [128, 8, 32])
        tb8 = sb("tb8", [128, 8, 32])

        def qkn_rope(src_psum, pb_, gidx, dst_ap, dst_b):
            S.op("act", lambda e: e.copy(dst_ap, src_psum), reads=[pb_], writes=[dst_b])
            S.op("dve", lambda e: e.tensor_tensor(sqc[:], dst_ap, dst_ap, ALU.mult), reads=[dst_b], writes=[sqc.b])
            S.op("dve", lambda e: e.tensor_reduce(s8[:], sqc[:], AX.X, ALU.add), reads=[sqc.b], writes=[s8.b])
            S.op("act", lambda e: e.activation(s8[:], s8[:], AF.Sqrt, bias=epsc[:, 0:1], scale=1.0 / 64.0), reads=[s8.b, epsc.b], writes=[s8.b])
            S.op("dve", lambda e: e.reciprocal(s8[:], s8[:]), reads=[s8.b], writes=[s8.b])
            S.op("dve", lambda e: e.tensor_tensor(dst_ap, dst_ap, s8[:].unsqueeze(2).to_broadcast([128, 8, 64]), ALU.mult),
                 reads=[dst_b, s8.b], writes=[dst_b])
            S.op("dve", lambda e: e.tensor_tensor(dst_ap, dst_ap, cgq[:, gidx, :].unsqueeze(1).to_broadcast([128, 8, 64]), ALU.mult),
                 reads=[dst_b, cgq.b], writes=[dst_b])
            x1 = dst_ap[:, :, 0:32]
            x2 = dst_ap[:, :, 32:64]
            cs = cs_t[:].unsqueeze(1).to_broadcast([128, 8, 32])
            sn = sn_t[:].unsqueeze(1).to_broadcast([128, 8, 32])
            S.op("dve", lambda e: e.tensor_tensor(ta8[:], x1, sn, ALU.mult), reads=[dst_b, sn_t.b], writes=[ta8.b])
            S.op("dve", lambda e: e.tensor_tensor(tb8[:], x2, sn, ALU.mult), reads=[dst_b, sn_t.b], writes=[tb8.b])
            S.op("dve", lambda e: e.tensor_tensor(x1, x1, cs, ALU.mult), reads=[dst_b, cs_t.b], writes=[dst_b])
            S.op("dve", lambda e: e.tensor_tensor(x2, x2, cs, ALU.mult), reads=[dst_b, cs_t.b], writes=[dst_b])
            S.op("dve", lambda e: e.tensor_tensor(x1, x1, tb8[:], ALU.subtract), reads=[dst_b, tb8.b], writes=[dst_b])
            S.op("dve", lambda e: e.tensor_tensor(x2, x2, ta8[:], ALU.add), reads=[dst_b, ta8.b], writes=[dst_b])

        for idx in range(33):
            sample = idx == 32
            ti = NT if sample else 32 + idx
            own = idx >= 16
            hb_ = hin[idx % 2]
            S.dma("sp", hb_[:], h1s[idx], writes=[hb_.b])
            rope_tables(ti)
            S.op("act", lambda e: e.activation(junkc[:], hb_[:], AF.Square, accum_out=ssc[:]), reads=[hb_.b], writes=[junkc.b, ssc.b])
            S.op("act", lambda e: e.activation(ssc[:], ssc[:], AF.Sqrt, bias=epsc[:, 0:1], scale=1.0 / D), reads=[ssc.b, epsc.b], writes=[ssc.b])
            S.op("dve", lambda e: e.reciprocal(ssc[:], ssc[:]), reads=[ssc.b], writes=[ssc.b])
            S.op("dve", lambda e: e.tensor_scalar(xnc[:], hb_[:], ssc[:, 0:1], None, ALU.mult), reads=[hb_.b, ssc.b], writes=[xnc.b])
            for c in range(8):
                S.op("pe", lambda e: e.transpose(pT[:, c, :], xnc[:, c * 128:(c + 1) * 128], ident[:]), reads=[xnc.b, ident.b], writes=[pT.b])
            S.op("dve", lambda e: e.tensor_tensor(hnc[:], pT[:], gmix[:, 1, :].unsqueeze(2).to_broadcast([128, 8, 128]), ALU.mult),
                 reads=[pT.b, gmix.b], writes=[hnc.b])
            for g in range(3):
                kv = kvrow[g % 2]
                for r_ in range(3):
                    if r_ == 0 and not own:
                        continue
                    c0 = g * 1536 + r_ * 512
                    pp = nextpp()
                    for c in range(8):
                        S.op("pe", lambda e: e.matmul(pp[:, :], hnc[:, c, :], cwin[:, c, c0:c0 + 512], start=(c == 0), stop=(c == 7)),
                             reads=[hnc.b, cwin.b], writes=[pp.b])
                    ppv = pp[:, :].rearrange("p (h d) -> p h d", d=64)
                    if r_ == 0:
                        qkn_rope(ppv, pp.b, 2 * g, zt[:], zt.b)
                        if sample:
                            S.dma("sp", qsd[g], zt[:].rearrange("p h d -> p (h d)"), reads=[zt.b])
                        else:
                            S.op("act", lambda e: e.copy(zb[:], zt[:].rearrange("p h d -> p (h d)")), reads=[zt.b], writes=[zb.b])
                            S.dma("sp", qd[g][(idx - 16) * 128:(idx - 15) * 128, :], zb[:], reads=[zb.b])
                    elif r_ == 1:
                        qkn_rope(ppv, pp.b, 2 * g + 1, kv[:, 0:512].rearrange("p (h d) -> p h d", d=64), kv.b)
                        if not sample:
                            S.op("act", lambda e: e.copy(zb[:], kv[:, 0:512]), reads=[kv.b], writes=[zb.b])
                            S.dma("sp", kd[g][idx * 128:(idx + 1) * 128, :], zb[:], reads=[zb.b])
                    else:
                        S.op("act", lambda e: e.copy(kv[:, 512:1024], pp[:, :]), reads=[pp.b], writes=[kv.b])
                        if not sample:
                            S.op("act", lambda e: e.copy(zb[:], pp[:, :]), reads=[pp.b], writes=[zb.b])
                            S.dma("sp", vd[g][idx * 128:(idx + 1) * 128, :], zb[:], reads=[zb.b])
                if sample:
                    S.dma("sp", ksd[g], kv[:], reads=[kv.b])
                    w_ = WG[g]
                    for sq in range(16):
                        S.dma("sp", o_dil_s[g][sq, w_ - 4:w_, :], kv[4 * sq:4 * sq + 4, :], reads=[kv.b])
                    for sq in range(0, 16, 4):
                        S.dma("sp", o_dil_s[g][sq:sq + 4, 0:w_ - 4, :], cache_dil[g][sq:sq + 4, 4:w_, :])
                else:
                    first = 64 - WG[g] // 128
                    if ti >= first:
                        S.dma("sp", o_dil_p[g][(ti - first) * 128:(ti - first + 1) * 128, :], kv[:], reads=[kv.b])
        S.finish()
        stk.pop()

    with ExitStack() as esD:
        stk.append(esD)
        pidx_i2 = sb("pidx_i2", [128, 128], I32)
        jidx_i2 = sb("jidx_i2", [128, 128], I32)
        pidx2 = sb("pidx2", [128, 128])
        jidx2 = sb("jidx2", [128, 128])
        S.op("pool", lambda e: e.iota(pidx_i2[:], pattern=[[0, 128]], base=0, channel_multiplier=1), writes=[pidx_i2.b])
        S.op("pool", lambda e: e.iota(jidx_i2[:], pattern=[[1, 128]], base=0, channel_multiplier=0), writes=[jidx_i2.b])
        S.op("dve", lambda e: e.tensor_copy(pidx2[:], pidx_i2[:]), reads=[pidx_i2.b], writes=[pidx2.b])
        S.op("dve", lambda e: e.tensor_copy(jidx2[:], jidx_i2[:]), reads=[jidx_i2.b], writes=[jidx2.b])
        mtmp = sb("mtmp", [128, 128])
        mneg = [sb(f"mneg{i}", [128, 4, 128], BF16) for i in range(2)]
        S.op("dve", lambda e: e.tensor_tensor(mtmp[:], pidx2[:], jidx2[:], ALU.is_lt), reads=[pidx2.b, jidx2.b], writes=[mtmp.b])
        S.op("dve", lambda e: e.tensor_scalar(mneg[0][:], mtmp[:].unsqueeze(1).to_broadcast([128, 4, 128]), NEGB, None, ALU.mult),
             reads=[mtmp.b], writes=[mneg[0].b])
        S.op("dve", lambda e: e.tensor_tensor(mtmp[:], pidx2[:], jidx2[:], ALU.is_gt), reads=[pidx2.b, jidx2.b], writes=[mtmp.b])
        S.op("dve", lambda e: e.tensor_scalar(mneg[1][:], mtmp[:].unsqueeze(1).to_broadcast([128, 4, 128]), NEGB, None, ALU.mult),
             reads=[mtmp.b], writes=[mneg[1].b])
        zeroc = sb("zeroc", [128, 1])
        S.op("dve", lambda e: e.memset(zeroc[:], 0.0), writes=[zeroc.b])
        qb_t = sb("qb_t", [128, 512], BF16)
        kb_t = [sb(f"kb_t{i}", [128, 512], BF16) for i in range(2)]
        vb_t = [sb(f"vb_t{i}", [128, 512], BF16) for i in range(2)]
        v1_t = [sb(f"v1_t{i}", [128, 8, 65], BF16) for i in range(2)]
        for i in range(2):
            S.op("pool", lambda e: e.memset(v1_t[i][:], 1.0), writes=[v1_t[i].b])
        qTz = [sb(f"qTz{h}", [128, 128], BF16) for h in range(8)]
        for h in range(8):
            S.op("pool", lambda e: e.memset(qTz[h][:], 0.0), writes=[qTz[h].b])
        kTt = [sb(f"kTt{i}", [128, 4, 128], BF16) for i in range(2)]
        Pd = [sb(f"Pd{i}", [128, 4, 128], BF16) for i in range(2)]
        ores = sb("ores", [128, 520])
        pdi = 0
        for g in range(3):
            dg = DG[g]
            kview = kd[g].rearrange("(i s) c -> s i c", s=dg)
            vview = vd[g].rearrange("(i s) c -> s i c", s=dg)
            qview = qd[g].rearrange("(i s) c -> s i c", s=dg)
            oview = od[g].rearrange("(i s) c -> s i c", s=dg)
            nqb = 2048 // (128 * dg)
            for r_ in range(dg):
                for qb in range(nqb):
                    i0 = 2048 // dg + 128 * qb
                    S.dma("sp", qb_t[:], qview[r_, 128 * qb:128 * qb + 128, :], writes=[qb_t.b])
                    for kb in range(2):
                        ks = i0 - 128 + 128 * kb
                        S.dma("sp", kb_t[kb][:], kview[r_, ks:ks + 128, :], writes=[kb_t[kb].b])
                        S.dma("sp", vb_t[kb][:], vview[r_, ks:ks + 128, :], writes=[vb_t[kb].b])
                        S.op("dve", lambda e: e.tensor_copy(v1_t[kb][:, :, 0:64], vb_t[kb][:].rearrange("p (h d) -> p h d", d=64)),
                             reads=[vb_t[kb].b], writes=[v1_t[kb].b])
                        for hp in range(4):
                            S.op("pe", lambda e: e.transpose(pT[:, hp, :], kb_t[kb][:, hp * 128:(hp + 1) * 128], ident[:]),
                                 reads=[kb_t[kb].b, ident.b], writes=[pT.b])
                        S.op("act", lambda e: e.copy(kTt[kb][:], pT[:, 0:4, :]), reads=[pT.b], writes=[kTt[kb].b])
                    for hp in range(4):
                        S.op("pe", lambda e: e.transpose(pT[:, 4 + hp, :], qb_t[:, hp * 128:(hp + 1) * 128], ident[:]),
                             reads=[qb_t.b, ident.b], writes=[pT.b])
                    for h in range(8):
                        hp, hh = h // 2, h % 2
                        S.op("act", lambda e: e.copy(qTz[h][64 * hh:64 * hh + 64, :], pT[64 * hh:64 * hh + 64, 4 + hp, :]),
                             reads=[pT.b], writes=[qTz[h].b])
                    for hb4 in range(2):
                        pacc = pO if hb4 == 0 else pI
                        for kb in range(2):
                            pp = nextpp()
                            S.op("pe", lambda e: e.matmul(pp[:, :], ident[:], mneg[kb][:].rearrange("p j t -> p (j t)"),
                                                          start=True, stop=False, skip_group_check=True),
                                 reads=[ident.b, mneg[kb].b], writes=[pp.b])
                            for j in range(4):
                                h = 4 * hb4 + j
                                S.op("pe", lambda e: e.matmul(pp[:, 128 * j:128 * j + 128], kTt[kb][:, h // 2, :], qTz[h][:, :],
                                                              start=False, stop=(j == 3), skip_group_check=True),
                                     reads=[kTt[kb].b, qTz[h].b], writes=[pp.b])
                            P = Pd[pdi % 2]
                            pdi += 1
                            bias_ap = metat[:, 3:4] if (kb == 0 and qb == 0) else zeroc[:, 0:1]
                            S.op("act", lambda e: e.activation(P[:].rearrange("p j t -> p (j t)"), pp[:, :], AF.Exp, bias=bias_ap, scale=0.125),
                                 reads=[pp.b, metat.b, zeroc.b], writes=[P.b])
                            for j in range(4):
                                h = 4 * hb4 + j
                                S.op("pe", lambda e: e.matmul(pacc[:, 65 * j:65 * j + 65], P[:, j, :], v1_t[kb][:, h, :],
                                                              start=(kb == 0 and j == 0), stop=(kb == 1), skip_group_check=True),
                                     reads=[P.b, v1_t[kb].b], writes=[pacc.b])
                        S.op("act", lambda e: e.copy(ores[:, 260 * hb4:260 * hb4 + 260], pacc[:, 0:260]), reads=[pacc.b], writes=[ores.b])
                    S.dma("sp", oview[r_, 128 * qb:128 * qb + 128, :], ores[:], reads=[ores.b])
        S.finish()
        o3 = [sb(f"o3_{g}", [128, 8, 65]) for g in range(3)]
        rden = sb("rden", [128, 8])
        ab1 = sb("ab1", [128, 8, 64], BF16)
        for t in range(16):
            for g in range(3):
                S.dma("sp", o3[g][:].rearrange("p h c -> p (h c)"), od[g][t * 128:(t + 1) * 128, :], writes=[o3[g].b])
            S.op("dve", lambda e: e.tensor_tensor(o3[0][:], o3[0][:], o3[1][:], ALU.add), reads=[o3[0].b, o3[1].b], writes=[o3[0].b])
            S.op("dve", lambda e: e.tensor_tensor(o3[0][:], o3[0][:], o3[2][:], ALU.add), reads=[o3[0].b, o3[2].b], writes=[o3[0].b])
            S.op("dve", lambda e: e.reciprocal(rden[:], o3[0][:, :, 64]), reads=[o3[0].b], writes=[rden.b])
            S.op("dve", lambda e: e.tensor_tensor(ab1[:], o3[0][:, :, 0:64], rden[:].unsqueeze(2).to_broadcast([128, 8, 64]), ALU.mult),
                 reads=[o3[0].b, rden.b], writes=[ab1.b])
            S.dma("sp", attn1[t], ab1[:].rearrange("p h d -> p (h d)"), reads=[ab1.b])
        S.finish()
        stk.pop()

    with ExitStack() as esE:
        stk.append(esE)
        e_pi = sb("e_pi", [8, 8], I32)
        e_ji = sb("e_ji", [8, 8], I32)
        e_pf = sb("e_pf", [8, 8])
        e_jf = sb("e_jf", [8, 8])
        S.op("pool", lambda e: e.iota(e_pi[:], pattern=[[0, 8]], base=0, channel_multiplier=1), writes=[e_pi.b])
        S.op("pool", lambda e: e.iota(e_ji[:], pattern=[[1, 8]], base=0, channel_multiplier=0), writes=[e_ji.b])
        S.op("dve", lambda e: e.tensor_copy(e_pf[:], e_pi[:]), reads=[e_pi.b], writes=[e_pf.b])
        S.op("dve", lambda e: e.tensor_copy(e_jf[:], e_ji[:]), reads=[e_ji.b], writes=[e_jf.b])
        eq8 = sb("eq8", [8, 8])
        S.op("dve", lambda e: e.tensor_tensor(eq8[:], e_pf[:], e_jf[:], ALU.is_equal), reads=[e_pf.b, e_jf.b], writes=[eq8.b])
        onesb = sb("onesb", [128, 1], BF16)
        S.op("dve", lambda e: e.memset(onesb[:], 1.0), writes=[onesb.b])
        Kt = [sb(f"sdK{i}", [128, 1024]) for i in range(2)]
        K1 = [sb(f"sdK1{i}", [1, 1024]) for i in range(2)]
        qbc = [sb(f"sdq{i}", [128, 512]) for i in range(2)]
        Vb = sb("sdVb", [128, 512], BF16)
        V1b = sb("sdV1b", [1, 512], BF16)
        prod = sb("sdprod", [128, 8, 64])
        prod1 = sb("sdprod1", [1, 8, 64])
        sc = sb("sdsc", [128, 8])
        sc1 = sb("sdsc1", [1, 8])
        pex = sb("sdpex", [128, 8], BF16)
        pex1 = sb("sdpex1", [1, 8], BF16)
        numacc = sb("sdnum", [8, 8, 64])
        denacc = sb("sdden", [8, 1])
        dsel = sb("sddsel", [8, 8, 64])
        o8 = sb("sdo8", [8, 64])
        o8b = sb("sdo8b", [8, 64], BF16)
        ci = 0
        for sq in range(16):
            for t in range(4):
                row = 4 * sq + t
                for g in range(3):
                    dg = DG[g]
                    kt_, k1_, qb_ = Kt[ci % 2], K1[ci % 2], qbc[ci % 2]
                    ci += 1
                    if g == 0:
                        n_c = 128 - t
                        S.dma("sp", kt_[0:n_c, :], cache_dil[0][sq, t:128, :], writes=[kt_.b])
                        if t > 0:
                            S.dma("sp", kt_[n_c:128, :], ksd[0][4 * sq:4 * sq + t, :], writes=[kt_.b])
                    else:
                        S.dma("sp", kt_[:, :], cache_dil[g][sq].rearrange("(i s) c -> s i c", s=dg)[t, 0:128, :], writes=[kt_.b])
                    S.dma("sp", k1_[:, :], ksd[g][row:row + 1, :], writes=[k1_.b])
                    S.dma("sp", qb_[:, :], qsd[g][row].partition_broadcast(128), writes=[qb_.b])
                    S.op("dve", lambda e: e.tensor_tensor(prod[:], kt_[:, 0:512].rearrange("p (h d) -> p h d", d=64),
                                                          qb_[:].rearrange("p (h d) -> p h d", d=64), ALU.mult),
                         reads=[kt_.b, qb_.b], writes=[prod.b])
                    S.op("dve", lambda e: e.tensor_reduce(sc[:], prod[:], AX.X, ALU.add), reads=[prod.b], writes=[sc.b])
                    S.op("act", lambda e: e.activation(pex[:], sc[:], AF.Exp, scale=0.125), reads=[sc.b], writes=[pex.b])
                    S.op("dve", lambda e: e.tensor_tensor(prod1[:], k1_[0:1, 0:512].rearrange("p (h d) -> p h d", d=64),
                                                          qb_[0:1, :].rearrange("p (h d) -> p h d", d=64), ALU.mult),
                         reads=[k1_.b, qb_.b], writes=[prod1.b])
                    S.op("dve", lambda e: e.tensor_reduce(sc1[:], prod1[:], AX.X, ALU.add), reads=[prod1.b], writes=[sc1.b])
                    S.op("act", lambda e: e.activation(pex1[:], sc1[:], AF.Exp, scale=0.125), reads=[sc1.b], writes=[pex1.b])
                    S.op("act", lambda e: e.copy(Vb[:], kt_[:, 512:1024]), reads=[kt_.b], writes=[Vb.b])
                    S.op("act", lambda e: e.copy(V1b[:], k1_[0:1, 512:1024]), reads=[k1_.b], writes=[V1b.b])
                    pn = nextpp()
                    S.op("pe", lambda e: e.matmul(pn[0:8, :], pex[:, :], Vb[:, :], start=True, stop=False), reads=[pex.b, Vb.b], writes=[pn.b])
                    S.op("pe", lambda e: e.matmul(pn[0:8, :], pex1[:, :], V1b[:, :], start=False, stop=True), reads=[pex1.b, V1b.b], writes=[pn.b])
                    pdn = nextpp()
                    S.op("pe", lambda e: e.matmul(pdn[0:8, 0:1], pex[:, :], onesb[:, :], start=True, stop=False), reads=[pex.b, onesb.b], writes=[pdn.b])
                    S.op("pe", lambda e: e.matmul(pdn[0:8, 0:1], pex1[:, :], onesb[0:1, :], start=False, stop=True), reads=[pex1.b, onesb.b], writes=[pdn.b])
                    if g == 0:
                        S.op("act", lambda e: e.copy(numacc[:].rearrange("p h d -> p (h d)"), pn[0:8, :]), reads=[pn.b], writes=[numacc.b])
                        S.op("act", lambda e: e.copy(denacc[:], pdn[0:8, 0:1]), reads=[pdn.b], writes=[denacc.b])
                    else:
                        S.op("dve", lambda e: e.tensor_tensor(numacc[:].rearrange("p h d -> p (h d)"), numacc[:].rearrange("p h d -> p (h d)"),
                                                              pn[0:8, :], ALU.add), reads=[numacc.b, pn.b], writes=[numacc.b])
                        S.op("dve", lambda e: e.tensor_tensor(denacc[:], denacc[:], pdn[0:8, 0:1], ALU.add), reads=[denacc.b, pdn.b], writes=[denacc.b])
                S.op("dve", lambda e: e.tensor_tensor(dsel[:], numacc[:], eq8[:].unsqueeze(2).to_broadcast([8, 8, 64]), ALU.mult),
                     reads=[numacc.b, eq8.b], writes=[dsel.b])
                S.op("dve", lambda e: e.tensor_reduce(o8[:], dsel[:].rearrange("p h d -> p d h"), AX.X, ALU.add), reads=[dsel.b], writes=[o8.b])
                S.op("dve", lambda e: e.reciprocal(denacc[:], denacc[:]), reads=[denacc.b], writes=[denacc.b])
                S.op("dve", lambda e: e.tensor_scalar(o8b[:], o8[:], denacc[:, 0:1], None, ALU.mult), reads=[o8.b, denacc.b], writes=[o8b.b])
                S.dma("sp", attn1[16][row].rearrange("(h d) -> h d", d=64), o8b[:], reads=[o8b.b])
        S.finish()
        stk.pop()

    tiles1 = []
    for t in range(16):
        tiles1.append((h1s[16 + t], attn1[t], pe1[t], [(y_p[t * 128:(t + 1) * 128, :], 0, 128)]))
    tiles1.append((h1s[32], attn1[16], pe1[16], [(y_s[:, :], 0, 64)]))
    channel_phase(1, 512, c_w_out, tiles1)

    S.finish()
    es.close()
    return nc, S


_PROG = None


def kernel(x_prompt, x_sample, state_gla, cache_nsa_cmp, cache_nsa_slc, cache_nsa_win,
           cache_dil_0, cache_dil_1, cache_dil_2, page_table, p_prompt, p_sample,
           norm_mix, norm_mlp, norm_ple, a_w_in, a_w_out, gla_w_a2, gla_b_a, gla_g_norm,
           nsa_g_qk, nsa_w_phi, nsa_pe, c_w_in, c_g_qk, c_w_out, mlp_w1, mlp_w2,
           ple_w_proj, ple_w_gate):
    global _PROG
    if _PROG is None:
        _PROG = build_program()
    nc, S = _PROG
    f32 = np.float32
    x_prompt = np.asarray(x_prompt, f32)
    x_sample = np.asarray(x_sample, f32)
    in_maps = []
    pool_c_h = np.asarray(cache_nsa_cmp, f32)[0].reshape(2560, 128, 256)
    pool_s_h = np.asarray(cache_nsa_slc, f32)[0].reshape(2560, 128, 256)
    for c in range(NCORES):
        b, seg = c // 4, c % 4
        P0 = seg * SEG
        OFF = NSLOT - (P0 + SEG)
        xe = np.zeros((NSLOT, D), f32)
        xe[OFF:] = x_prompt[b, :P0 + SEG]
        meta = np.zeros((128, 8), f32)
        meta[:, 0] = OFF
        meta[:, 1] = 2048 + (np.arange(128) % 4)
        meta[:, 2] = OFF // 64
        meta[:, 3] = NEGB if OFF >= 6144 else 0.0
        xs = np.zeros((128, D), f32)
        xs[:64] = x_sample[16 * c:16 * c + 16].reshape(64, D)
        pp_ = np.asarray(p_prompt, f32)
        ps_ = np.asarray(p_sample, f32)
        pe0 = np.zeros((33, 128, 256), f32)
        pe1 = np.zeros((17, 128, 256), f32)
        for l, (pe_, nt_) in enumerate(((pe0, 32), (pe1, 16))):
            ext = np.zeros((nt_ * 128, 256), f32)
            lo = P0 + SEG - nt_ * 128
            src = pp_[l, b, max(lo, 0):P0 + SEG]
            ext[nt_ * 128 - len(src):] = src
            pe_[:nt_] = ext.reshape(nt_, 128, 256)
            pe_[nt_, :64] = ps_[l, 16 * c:16 * c + 16].reshape(64, 256)
        in_maps.append({
            "xe": xe, "meta": meta, "xs": xs, "pe0": pe0, "pe1": pe1,
            "cache_dil0": np.ascontiguousarray(np.asarray(cache_dil_0, f32)[0, 16 * c:16 * c + 16]).reshape(16, 128, 1024),
            "cache_dil1": np.ascontiguousarray(np.asarray(cache_dil_1, f32)[0, 16 * c:16 * c + 16]).reshape(16, 512, 1024),
            "cache_dil2": np.ascontiguousarray(np.asarray(cache_dil_2, f32)[0, 16 * c:16 * c + 16]).reshape(16, 2048, 1024),
            "page_tab": np.ascontiguousarray(np.asarray(page_table)[16 * c:16 * c + 16]).astype(np.int32),
            "pool_c": pool_c_h, "pool_s": pool_s_h,
            "c_w_in": np.asarray(c_w_in[0], f32), "c_g_qk": np.asarray(c_g_qk[0], f32).reshape(6, 64),
            "norm_mlp": np.asarray(norm_mlp, f32), "norm_ple": np.asarray(norm_ple, f32),
            "a_w_out": np.asarray(a_w_out[0], f32), "c_w_out": np.asarray(c_w_out[0], f32),
            "mlp_w1": np.asarray(mlp_w1, f32), "mlp_w2": np.asarray(mlp_w2, f32),
            "ple_w_proj": np.asarray(ple_w_proj, f32), "ple_w_gate": np.asarray(ple_w_gate, f32),
            "norm_mix": np.asarray(norm_mix, f32),
            "a_w_in": np.asarray(a_w_in[0], f32),
            "nsa_g_qk": np.asarray(nsa_g_qk[0], f32),
            "gla_w_a2": np.asarray(gla_w_a2[0], f32),
            "gla_b_a": np.asarray(gla_b_a[0], f32),
            "gla_g_norm": np.asarray(gla_g_norm[0], f32),
            "nsa_w_phi": np.asarray(nsa_w_phi[0], f32),
            "nsa_pe": np.asarray(nsa_pe[0], f32),
            "state_gla": np.ascontiguousarray(np.asarray(state_gla, f32)[0, 16 * c:16 * c + 16]),
            "cache_win": np.ascontiguousarray(np.asarray(cache_nsa_win, f32)[0, 16 * c:16 * c + 16]).reshape(16, 512, 256),
        })
    res = run_bass_kernel_spmd(nc, in_maps, core_ids=list(range(NCORES)))
    R = res.results
    global _DBG
    _DBG = R

    def prompt_rows(name):
        out = np.zeros((1, 2, SEQ, 2, 2, 64), f32)
        for c in range(NCORES):
            b, seg = c // 4, c % 4
            out[0, b, seg * SEG:(seg + 1) * SEG] = R[c][name].reshape(SEG, 2, 2, 64)
        return out

    def cat(name, shp):
        return np.concatenate([R[c][name].reshape(shp) for c in range(NCORES)], axis=0)[None]

    cmp_p = prompt_rows("o_cmp_p")
    slc_p = prompt_rows("o_slc_p")
    win_p = np.stack([R[3]["o_win_p"].reshape(512, 2, 2, 64), R[7]["o_win_p"].reshape(512, 2, 2, 64)])[None]
    gla_p = np.stack([R[3]["o_gla_p"], R[7]["o_gla_p"]])[None]
    gla_s = cat("o_gla_s", (16, 4, 64, 128))
    cmp_s = cat("o_cmp_s", (16, 4, 2, 2, 64))
    slc_s = cat("o_slc_s", (16, 4, 2, 2, 64))
    win_s = cat("o_win_s", (16, 512, 2, 2, 64))
    z = lambda *s: np.zeros(s, f32)
    yp = np.zeros((2, SEQ, D), f32)
    for c in range(NCORES):
        b, seg = c // 4, c % 4
        yp[b, seg * SEG:(seg + 1) * SEG] = R[c]["y_p"]
    ys = np.concatenate([R[c]["y_s"].reshape(16, 4, D) for c in range(NCORES)], axis=0)
    dil_p = [np.stack([R[3][f"o_dil_p{g}"].reshape(w, 2, 8, 64), R[7][f"o_dil_p{g}"].reshape(w, 2, 8, 64)])[None]
             for g, w in enumerate((128, 512, 2048))]
    return (yp, ys, gla_p, gla_s,
            cmp_p, cmp_s, slc_p, slc_s,
            win_p, win_s,
            dil_p[0], cat("o_dil_s0", (16, 128, 2, 8, 64)),
            dil_p[1], cat("o_dil_s1", (16, 512, 2, 8, 64)),
            dil_p[2], cat("o_dil_s2", (16, 2048, 2, 8, 64)))
```
